# Optimizing a Trainium2 kernel written in Bass

```python
import math
import jax, jax.numpy as jnp
from jax import lax
import numpy as np

D_MODEL = 4096
BATCH = 4
SEQ = 4096
DEPTH = 1
DEC_BATCH = 8
DEC_SEQ = 2048
PAST_LEN = 128

DA_HEADS = 8
DA_HEAD_DIM = 128
DA_QK = DA_HEADS * 2 * DA_HEAD_DIM
DA_V = DA_HEADS * 2 * DA_HEAD_DIM
GLA_HEADS = 4
GLA_DK = 256
GLA_DV = 512
GLA_K = GLA_HEADS * GLA_DK
GLA_V = GLA_HEADS * GLA_DV
GLA_GATE_RANK = 16
GLA_GATE_TAU = 16.0
GLA_CHUNK = 64
D_FF = 11008
CONV_WIDTH = 3
REL_BUCKETS = 32
REL_MAX_DIST = 128
Q_BLOCK = 128
EPS = 1e-6
D_IN = DA_QK + DA_QK + DA_V + GLA_K + GLA_K + GLA_V + GLA_V + 2 * GLA_GATE_RANK + 2 * D_MODEL

kernel_name = 'hybrid_diffattn_gla_convffn_encoder'


def _rmsnorm(x, g):
    xf = x.astype(jnp.float32)
    y = xf * lax.rsqrt(jnp.mean(xf * xf, axis=-1, keepdims=True) + EPS)
    return (y * g.astype(jnp.float32)).astype(x.dtype)


def _t5_bucket(rel):
    half = REL_BUCKETS // 2
    max_exact = half // 2
    ret = jnp.where(rel > 0, half, 0)
    n = jnp.abs(rel)
    nf = jnp.maximum(n, 1).astype(jnp.float32)
    large = max_exact + (jnp.log(nf / max_exact) / math.log(REL_MAX_DIST / max_exact)
                         * (half - max_exact)).astype(jnp.int32)
    large = jnp.minimum(large, half - 1)
    return ret + jnp.where(n < max_exact, n, large)


def _diff_attention(q, k, v, rel_bias, lam):
    B, H, _, T, d = q.shape
    nq = T // Q_BLOCK
    qb = q.reshape(B, H, 2, nq, Q_BLOCK, d).transpose(3, 0, 1, 2, 4, 5)
    kpos = jnp.arange(T, dtype=jnp.int32)

    def block(args):
        i, qi = args
        qpos = i * Q_BLOCK + jnp.arange(Q_BLOCK, dtype=jnp.int32)
        bias = rel_bias[_t5_bucket(kpos[None, :] - qpos[:, None])]
        bias = jnp.transpose(bias, (2, 0, 1)).astype(jnp.float32)
        s = jnp.einsum('bhmqd,bhmkd->bhmqk', qi, k).astype(jnp.float32) + bias[None, :, None]
        p = jax.nn.softmax(s, axis=-1)
        w = p[:, :, 0] - lam * p[:, :, 1]
        return jnp.einsum('bhqk,bhkv->bhqv', w.astype(v.dtype), v)

    o = lax.map(block, (jnp.arange(nq, dtype=jnp.int32), qb))
    return o.transpose(1, 0, 3, 2, 4).reshape(B, T, H, 2 * d)


def _gla_chunked(q, k, v, log_a):
    B, H, T, dk = q.shape
    dv = v.shape[-1]
    C = GLA_CHUNK
    N = T // C
    q, k, log_a = (t.reshape(B, H, N, C, dk) for t in (q, k, log_a))
    v = v.reshape(B, H, N, C, dv)
    b = jnp.cumsum(log_a, axis=3)
    b_last = b[:, :, :, -1:, :]
    q_dec = q * jnp.exp(b)
    k_inv = k * jnp.exp(-b)
    mask = jnp.tril(jnp.ones((C, C), dtype=bool))
    att = jnp.where(mask, jnp.einsum('bhncd,bhnsd->bhncs', q_dec, k_inv), 0.0)
    o_intra = jnp.einsum('bhncs,bhnsv->bhncv', att, v)
    k_dec = k * jnp.exp(b_last - b)
    decay = jnp.exp(b_last[:, :, :, 0, :])

    def step(S, xs):
        qd, kd, vn, dn = xs
        o = jnp.einsum('bhcd,bhdv->bhcv', qd, S)
        S = dn[..., None] * S + jnp.einsum('bhcd,bhcv->bhdv', kd, vn)
        return S, o

    xs = tuple(jnp.moveaxis(t, 2, 0) for t in (q_dec, k_dec, v, decay))
    S0 = jnp.zeros((B, H, dk, dv), jnp.float32)
    _, o_inter = lax.scan(step, S0, xs)
    o = o_intra + jnp.moveaxis(o_inter, 0, 2)
    return o.reshape(B, H, T, dv)


def _depthwise_conv3(a, w, bias):
    ap = jnp.pad(a, ((0, 0), (1, 1), (0, 0)))
    return ap[:, :-2] * w[0] + ap[:, 1:-1] * w[1] + ap[:, 2:] * w[2] + bias


def _encoder_layer(x, lambda_init, rel_bias, g_mix, w_in, q_norm_g, k_norm_g, lambda_q1, lambda_k1,
                   lambda_q2, lambda_k2, da_subln_g, w_gate_fwd, b_gate_fwd, w_gate_bwd, b_gate_bwd,
                   gla_norm_g, w_branch_a, w_branch_b, w_out, g_ffn, w_up, conv_w, conv_b, w_down):
    B, T, _ = x.shape
    f32 = jnp.float32
    h = _rmsnorm(x, g_mix)
    z = h @ w_in
    widths = [DA_QK, DA_QK, DA_V, GLA_K, GLA_K, GLA_V, GLA_V, GLA_GATE_RANK, GLA_GATE_RANK, D_MODEL, D_MODEL]
    points = [int(c) for c in np.cumsum(widths)[:-1]]
    (da_q, da_k, da_v, gl_q, gl_k, gl_v, gl_r, gl_lf, gl_lb, gate_a, gate_b) = jnp.split(z, points, axis=-1)

    q = _rmsnorm(da_q.reshape(B, T, DA_HEADS, 2, DA_HEAD_DIM), q_norm_g) * (DA_HEAD_DIM ** -0.5)
    k = _rmsnorm(da_k.reshape(B, T, DA_HEADS, 2, DA_HEAD_DIM), k_norm_g)
    q = q.transpose(0, 2, 3, 1, 4)
    k = k.transpose(0, 2, 3, 1, 4)
    v = da_v.reshape(B, T, DA_HEADS, 2 * DA_HEAD_DIM).transpose(0, 2, 1, 3)
    lam = (jnp.exp(jnp.sum(lambda_q1.astype(f32) * lambda_k1.astype(f32)))
           - jnp.exp(jnp.sum(lambda_q2.astype(f32) * lambda_k2.astype(f32))) + lambda_init)
    o_a = _diff_attention(q, k, v, rel_bias, lam)
    o_a = (_rmsnorm(o_a, da_subln_g) * (1.0 - lambda_init)).reshape(B, T, DA_V)

    def heads(t, dh):
        return t.reshape(B, T, GLA_HEADS, dh).transpose(0, 2, 1, 3).astype(f32)

    gq = heads(gl_q, GLA_DK) * (GLA_DK ** -0.5)
    gk = heads(gl_k, GLA_DK)
    gv = heads(gl_v, GLA_DV)
    la_f = heads(jax.nn.log_sigmoid((gl_lf @ w_gate_fwd + b_gate_fwd).astype(f32)) / GLA_GATE_TAU, GLA_DK)
    la_b = heads(jax.nn.log_sigmoid((gl_lb @ w_gate_bwd + b_gate_bwd).astype(f32)) / GLA_GATE_TAU, GLA_DK)
    flip = lambda t: jnp.flip(t, axis=2)
    o_b = _gla_chunked(gq, gk, gv, la_f) + flip(_gla_chunked(flip(gq), flip(gk), flip(gv), flip(la_b)))
    o_b = _rmsnorm(o_b.transpose(0, 2, 1, 3), gla_norm_g).astype(x.dtype)
    o_b = (o_b * jax.nn.silu(gl_r.reshape(B, T, GLA_HEADS, GLA_DV))).reshape(B, T, GLA_V)

    merged = jax.nn.sigmoid(gate_a) * (o_a @ w_branch_a) + jax.nn.sigmoid(gate_b) * (o_b @ w_branch_b)
    x = x + merged @ w_out

    h2 = _rmsnorm(x, g_ffn)
    a, g = jnp.split(h2 @ w_up, 2, axis=-1)
    a = _depthwise_conv3(a, conv_w, conv_b)
    return x + (jax.nn.gelu(a) * g) @ w_down


def setup_inputs(seed: int = 0) -> dict:
    key = jax.random.key(seed)
    ks = jax.random.split(key, 25)
    f32 = jnp.float32
    L = DEPTH

    def nrm(k, shape, scale):
        return jax.random.normal(k, shape, f32) * scale

    def gain(k, shape):
        return 1.0 + 0.02 * jax.random.normal(k, shape, f32)

    return {
        'x_prompt': nrm(ks[0], (BATCH, SEQ, D_MODEL), 1.0),
        'x_sample': nrm(ks[1], (DEC_BATCH, DEC_SEQ, D_MODEL), 1.0),
        'rel_bias': nrm(ks[2], (REL_BUCKETS, DA_HEADS), 0.5),
        'g_mix': gain(ks[3], (L, D_MODEL)),
        'w_in': nrm(ks[4], (L, D_MODEL, D_IN), D_MODEL ** -0.5),
        'q_norm_g': gain(ks[5], (L, DA_HEAD_DIM)),
        'k_norm_g': gain(ks[6], (L, DA_HEAD_DIM)),
        'lambda_q1': nrm(ks[7], (L, DA_HEAD_DIM), 0.1),
        'lambda_k1': nrm(ks[8], (L, DA_HEAD_DIM), 0.1),
        'lambda_q2': nrm(ks[9], (L, DA_HEAD_DIM), 0.1),
        'lambda_k2': nrm(ks[10], (L, DA_HEAD_DIM), 0.1),
        'da_subln_g': gain(ks[11], (L, 2 * DA_HEAD_DIM)),
        'w_gate_fwd': nrm(ks[12], (L, GLA_GATE_RANK, GLA_K), GLA_GATE_RANK ** -0.5),
        'b_gate_fwd': nrm(ks[13], (L, GLA_K), 0.1),
        'w_gate_bwd': nrm(ks[14], (L, GLA_GATE_RANK, GLA_K), GLA_GATE_RANK ** -0.5),
        'b_gate_bwd': nrm(ks[15], (L, GLA_K), 0.1),
        'gla_norm_g': gain(ks[16], (L, GLA_DV)),
        'w_branch_a': nrm(ks[17], (L, DA_V, D_MODEL), DA_V ** -0.5),
        'w_branch_b': nrm(ks[18], (L, GLA_V, D_MODEL), GLA_V ** -0.5),
        'w_out': nrm(ks[19], (L, D_MODEL, D_MODEL), D_MODEL ** -0.5),
        'g_ffn': gain(ks[20], (L, D_MODEL)),
        'w_up': nrm(ks[21], (L, D_MODEL, 2 * D_FF), D_MODEL ** -0.5),
        'conv_w': nrm(ks[22], (L, CONV_WIDTH, D_FF), CONV_WIDTH ** -0.5),
        'conv_b': nrm(ks[23], (L, D_FF), 0.02),
        'w_down': nrm(ks[24], (L, D_FF, D_MODEL), D_FF ** -0.5),
    }


def reference(x_prompt, x_sample, rel_bias, g_mix, w_in, q_norm_g, k_norm_g, lambda_q1, lambda_k1,
              lambda_q2, lambda_k2, da_subln_g, w_gate_fwd, b_gate_fwd, w_gate_bwd, b_gate_bwd,
              gla_norm_g, w_branch_a, w_branch_b, w_out, g_ffn, w_up, conv_w, conv_b, w_down):
    y_prompt = x_prompt
    y_sample = x_sample
    for l in range(DEPTH):
        lambda_init = 0.8 - 0.6 * math.exp(-0.3 * l)
        lp = (g_mix[l], w_in[l], q_norm_g[l], k_norm_g[l], lambda_q1[l], lambda_k1[l], lambda_q2[l],
              lambda_k2[l], da_subln_g[l], w_gate_fwd[l], b_gate_fwd[l], w_gate_bwd[l], b_gate_bwd[l],
              gla_norm_g[l], w_branch_a[l], w_branch_b[l], w_out[l], g_ffn[l], w_up[l], conv_w[l],
              conv_b[l], w_down[l])
        y_prompt = _encoder_layer(y_prompt, lambda_init, rel_bias, *lp)
        y_sample = _encoder_layer(y_sample, lambda_init, rel_bias, *lp)
    return (y_prompt, y_sample)
```

```python
import math
from contextlib import ExitStack
import numpy as np
import concourse.bass as bass
import concourse.mybir as mybir
from concourse.bass_utils import run_bass_kernel_spmd

F32 = mybir.dt.float32
BF16 = mybir.dt.bfloat16
AF = mybir.ActivationFunctionType
ALU = mybir.AluOpType
EPS = 1e-6
NEG = -30000.0
SHIFT = -8.0


class Dep:
    __slots__ = ("w", "r")

    def __init__(self):
        self.w = []
        self.r = []


class Op:
    __slots__ = ("eng", "fn", "deps", "signal", "isdma", "sigidx", "slot", "target")


class Prog:
    ENG = ("pe", "act", "dve", "pool", "sp")

    def __init__(self, nc, nslots=14):
        self.nc = nc
        self.ops = {e: [] for e in self.ENG}
        self.all = []
        self.pending = {e: set() for e in self.ENG}
        self.K = nslots
        self.last_compute = {}
        self.dma_since = []

    def _dep(self, op, x, raw):
        if x is op:
            return
        if x.eng == op.eng and not x.isdma and not op.isdma:
            if op.eng == "pe" or not raw:
                return
        op.deps.add(x)
        x.signal = True

    def _add(self, eng, fn, r, w, isdma):
        op = Op()
        op.eng, op.fn, op.isdma, op.signal, op.deps = eng, fn, isdma, isdma, set()
        op.sigidx = 0
        for d in r:
            for x in d.w:
                self._dep(op, x, True)
        for d in w:
            for x in d.w:
                self._dep(op, x, False)
            for x in d.r:
                self._dep(op, x, False)
        for x in self.pending[eng]:
            if x is not op:
                op.deps.add(x)
                x.signal = True
        self.pending[eng] = set()
        for d in r:
            if (not isdma) and d.r and d.r[-1].eng == eng and not d.r[-1].isdma:
                d.r[-1] = op
            else:
                d.r.append(op)
        for d in w:
            if d.r:
                d.w = [op]
                d.r = []
            elif (not isdma) and d.w and d.w[-1].eng == eng and not d.w[-1].isdma:
                d.w[-1] = op
            else:
                d.w.append(op)
        self.ops[eng].append(op)
        self.all.append(op)
        if isdma:
            self.dma_since.append(op)
        else:
            self.last_compute[eng] = op
        return op

    def op(self, eng, fn, r=(), w=()):
        return self._add(eng, fn, r, w, False)

    def dma(self, q, out, in_, r=(), w=(), **kw):
        return self._add(q, lambda e: e.dma_start(out=out, in_=in_, **kw), r, w, True)

    def barrier(self):
        last = set(self.last_compute.values()) | set(self.dma_since)
        self.dma_since = []
        for e in self.ENG:
            self.pending[e] |= last

    def finish(self):
        self.barrier()
        self._add("sp", None, (), (), False)

    def emit(self):
        nc = self.nc
        cnt = {e: 0 for e in self.ENG}
        dcount = {e: 0 for e in self.ENG}
        slot_last = {}
        for op in self.all:
            if op.isdma:
                s = dcount[op.eng] % self.K
                dcount[op.eng] += 1
                op.slot = (op.eng, s)
                prev = slot_last.get(op.slot)
                op.target = (prev.target if prev else 0) + 16
                if prev is not None:
                    op.deps.add(prev)
                slot_last[op.slot] = op
            elif op.signal:
                cnt[op.eng] += 1
                op.sigidx = cnt[op.eng]
        with ExitStack() as st:
            esem = {e: st.enter_context(nc.semaphore("s_" + e)) for e in self.ENG}
            dsem = {}
            for e in self.ENG:
                for s in range(min(self.K, dcount[e])):
                    dsem[(e, s)] = st.enter_context(nc.semaphore("d_%s%d" % (e, s)))
            block = st.enter_context(nc.Block())

            def replay(e, engine):
                waited = {}
                for op in self.ops[e]:
                    need = {}
                    for x in op.deps:
                        if x.isdma:
                            key, val, sem = ("d",) + x.slot, x.target, dsem[x.slot]
                        else:
                            key, val, sem = ("e", x.eng), x.sigidx, esem[x.eng]
                        if val > need.get(key, (0, None))[0]:
                            need[key] = (val, sem)
                    for key, (val, sem) in need.items():
                        if waited.get(key, 0) < val:
                            engine.wait_ge(sem, val)
                            waited[key] = val
                    if op.fn is None:
                        continue
                    ins = op.fn(engine)
                    if op.isdma:
                        ins.then_inc(dsem[op.slot], 16)
                    elif op.signal:
                        ins.then_inc(esem[e], 1)

            block.sync(lambda eng: replay("sp", eng))
            block.tensor(lambda eng: replay("pe", eng))
            block.scalar(lambda eng: replay("act", eng))
            block.vector(lambda eng: replay("dve", eng))
            block.gpsimd(lambda eng: replay("pool", eng))


class Ring:
    def __init__(self, aps):
        self.aps = aps
        self.deps = [Dep() for _ in aps]
        self.i = 0

    def next(self):
        k = self.i % len(self.aps)
        self.i += 1
        return self.aps[k], self.deps[k]


def full_cfg():
    return dict(D=4096, NT=4096, HA=8, HG=4, DFF=11008)


def t5_bucket_np(rel):
    half, max_exact = 16, 8
    ret = np.where(rel > 0, half, 0)
    n = np.abs(rel)
    nf = np.maximum(n, 1).astype(np.float32)
    large = max_exact + (np.log(nf / np.float32(max_exact)) / np.float32(math.log(128 / max_exact))
                         * np.float32(half - max_exact)).astype(np.int32)
    large = np.minimum(large, half - 1)
    return ret + np.where(n < max_exact, n, large)


def host_consts():
    c = {}
    c["c_ident"] = np.eye(128, dtype=np.float32)
    c["c_flip"] = np.eye(128, dtype=np.float32)[::-1].copy()
    s = np.arange(128)[:, None]
    t = np.arange(128)[None, :]
    same = (s // 64) == (t // 64)
    g = np.zeros((128, 6, 128), np.float32)
    g[:, 0] = (same & (s <= t)) * (-1.0 / 16)
    g[:, 1] = (same & (s > t)) * (-1.0 / 16)
    g[:, 2] = (same & (s >= t)) * (-1.0 / 16)
    g[:, 3] = (same & (s < t)) * (-1.0 / 16)
    g[:, 4] = (same & (s <= t)) * 1.0
    g[:, 5] = (same & (s >= t)) * 1.0
    c["c_gla"] = g
    oh = np.zeros((32, 6, 640), np.float32)
    for di, dl in enumerate(range(-1, 5)):
        i = np.arange(640)
        rel = 128 * dl + 127 - i
        b = t5_bucket_np(rel.astype(np.int32))
        oh[b, di, i] = 1.0
    c["c_onehot"] = oh.reshape(32, 6 * 640)
    return c


def in_groups(cfg):
    HA, HG, D = cfg["HA"], cfg["HG"], cfg["D"]
    QK, GK_, GV_ = HA * 256, HG * 256, HG * 512
    o = 0
    g = []
    for name, n, kind in (("q", QK, "fm_qk"), ("k", QK, "fm_qk"), ("v", QK, "tm"), ("gq", GK_, "fm_plain"),
                          ("gk", GK_, "fm_plain"), ("gv", GV_, "tm"), ("gr", GV_, "tm_silu"),
                          ("glr", 32, "fm_glr"), ("ga", D, "fm_sig"), ("gb", D, "fm_sig")):
        g.append((name, o, n, min(512, n), kind))
        o += n
    return g, o


def build(cfg, dbg=()):
    D, NT, HA, HG, DFF = cfg["D"], cfg["NT"], cfg["HA"], cfg["HG"], cfg["DFF"]
    KC = D // 128
    NB = NT // 128
    TT = 512
    NTT = NT // TT
    NQC = NT // 512
    NKT = NT // 128
    QK, GKW, GVW = HA * 256, HG * 256, HG * 512
    KCB = QK // 128
    KCG = GVW // 128
    FT = DFF // 128
    groups, DIN = in_groups(cfg)
    nc = bass.Bass("TRN2", target_bir_lowering=False)

    def din(name, shape):
        return nc.dram_tensor(name, list(shape), F32, kind="ExternalInput").ap()

    def dscr(name, shape, dt=BF16):
        kind = "ExternalOutput" if name in dbg else "Internal"
        return nc.dram_tensor(name, list(shape), dt, kind=kind).ap()

    x_in = din("x", (NT, D))
    y_out = nc.dram_tensor("y", [NT, D], F32, kind="ExternalOutput").ap()
    rel_bias = din("rel_bias", (32, 8))
    g_mix = din("g_mix", (1, D))
    w_in = din("w_in", (D, DIN))
    q_norm_g = din("q_norm_g", (1, 128))
    k_norm_g = din("k_norm_g", (1, 128))
    lam_in = din("lam4", (1, 512))
    da_subln_g = din("da_subln_g", (1, 256))
    w_gate_f = din("w_gate_fwd", (16, GKW))
    b_gate_f = din("b_gate_fwd", (1, GKW))
    w_gate_b = din("w_gate_bwd", (16, GKW))
    b_gate_b = din("b_gate_bwd", (1, GKW))
    gla_norm_g = din("gla_norm_g", (1, 512))
    w_br_a = din("w_branch_a", (QK, D))
    w_br_b = din("w_branch_b", (GVW, D))
    w_out = din("w_out", (D, D))
    g_ffn = din("g_ffn", (1, D))
    w_up = din("w_up", (D, 2 * DFF))
    conv_w = din("conv_w", (3, DFF))
    conv_b = din("conv_b", (1, DFF))
    w_down = din("w_down", (DFF, D))
    c_ident = din("c_ident", (128, 128))
    c_flip = din("c_flip", (128, 128))
    c_gla = din("c_gla", (128, 6, 128))
    c_onehot = din("c_onehot", (32, 6 * 640))
    core_flags = din("core_flags", (128, 32))

    WB = {}
    for (name, c0, n, CW, kind) in groups:
        WB[name] = dscr("wb_" + name, (n // CW, 128, KC, CW))
    WB["bra"] = dscr("wb_bra", (D // 512 if D >= 512 else 1, 128, KCB, min(512, D)))
    WB["brb"] = dscr("wb_brb", (D // 512 if D >= 512 else 1, 128, KCG, min(512, D)))
    WB["out"] = dscr("wb_out", (D // 512 if D >= 512 else 1, 128, KC, min(512, D)))
    WB["upa"] = dscr("wb_upa", (FT, 128, KC, 128))
    WB["upg"] = dscr("wb_upg", (FT, 128, KC, 128))
    WB["down"] = dscr("wb_down", (D // 128, 128, FT, 128))
    QT = dscr("s_qt", (2 * HA, 128, NT))
    KT = dscr("s_kt", (2 * HA, 128, NT))
    VV = dscr("s_v", (NT, QK))
    GQT = dscr("s_gqt", (2 * HG, 128, NT))
    GKT = dscr("s_gkt", (2 * HG, 128, NT))
    GKM = dscr("s_gk", (NT, GKW))
    GVM = dscr("s_gv", (NT, GVW))
    GRM = dscr("s_gr", (NT, GVW))
    GLT = dscr("s_glt", (32, NT), F32)
    SGA = dscr("s_sga", (KC, 128, NT))
    SGB = dscr("s_sgb", (KC, 128, NT))
    OAT = dscr("s_oat", (KCB, 128, NT))
    OBT = dscr("s_obt", (KCG, 128, NT))
    OFW = dscr("s_of", (NT, GVW), F32)
    X1 = dscr("s_x1", (NT, D), F32)
    UBI = dscr("s_ubias", (8, 6 * 640), F32)

    P = Prog(nc)
    es = ExitStack()

    SB_BYTES = 200 * 1024
    BIG = es.enter_context(nc.sbuf_tensor("big", [128, SB_BYTES // 2], BF16))
    bump = [0]

    def sb(name, shape, dt=F32):
        esz = 4 if dt == F32 else 2
        n = 1
        for v in shape[1:]:
            n *= v
        nb = (n * esz + 63) // 64 * 64
        off = bump[0]
        bump[0] += nb
        assert bump[0] <= SB_BYTES, ("SBUF overflow", name, bump[0])
        v = BIG[0:shape[0], off // 2: off // 2 + n * esz // 2]
        if dt == F32:
            v = v.bitcast(F32)
        if len(shape) == 3:
            v = v.rearrange("p (a b) -> p a b", b=shape[2])
        return v

    banks = [es.enter_context(nc.psum_tensor("ps%d" % i, [128, 512], F32)) for i in range(8)]
    PS = Ring([b[:] for b in banks])

    ident_f = sb("ident_f", (128, 128))
    ident_b = sb("ident_b", (128, 128), BF16)
    flip_f = sb("flip_f", (128, 128))
    ones_b = sb("ones_b", (128, 128), BF16)
    ones_f = sb("ones_f", (128, 128))
    gla_f = sb("gla_f", (128, 6, 128))
    gla_b = sb("gla_b", (128, 4, 128), BF16)
    flags = sb("flags", (128, 32))
    eps_col = sb("eps_col", (128, 1))
    gq_col = sb("gq_col", (128, 2))
    cdep = Dep()
    P.dma("sp", ident_f, c_ident, w=[cdep])
    P.dma("sp", flip_f, c_flip, w=[cdep])
    P.dma("sp", gla_f, c_gla, w=[cdep])
    P.dma("sp", flags, core_flags, w=[cdep])
    P.dma("sp", gq_col[:, 0:1], q_norm_g.rearrange("o d -> d o"), w=[cdep])
    P.dma("sp", gq_col[:, 1:2], k_norm_g.rearrange("o d -> d o"), w=[cdep])
    P.op("dve", lambda e: e.tensor_copy(out=ident_b, in_=ident_f), r=[cdep], w=[cdep])
    P.op("dve", lambda e: e.memset(ones_b, 1.0), w=[cdep])
    P.op("dve", lambda e: e.memset(ones_f, 1.0), w=[cdep])
    P.op("dve", lambda e: e.memset(eps_col, EPS), w=[cdep])
    P.op("dve", lambda e: e.tensor_copy(out=gla_b, in_=gla_f[:, 0:4, :]), r=[cdep], w=[cdep])
    P.op("dve", lambda e: e.tensor_scalar(out=gq_col[:, 0:1], in0=gq_col[:, 0:1], scalar1=128.0 ** -0.5,
                                          scalar2=None, op0=ALU.mult), r=[cdep], w=[cdep])
    P.barrier()

    def prep_weight(src2d, K, c0, ncols, CW, dst, st32, st16):
        kcn = K // 128
        nk = max(1, min(kcn, 4096 // CW))
        for cg in range(ncols // CW):
            for k0 in range(0, kcn, nk):
                n = min(nk, kcn - k0)
                a32, d32 = st32.next()
                a16, d16 = st16.next()
                src = src2d[k0 * 128:(k0 + n) * 128, c0 + cg * CW: c0 + (cg + 1) * CW].rearrange(
                    "(k p) c -> p k c", p=128)
                v32 = a32[:, 0:n * CW].rearrange("p (k c) -> p k c", c=CW)
                P.dma("sp", v32, src, w=[d32])
                eng = ("dve", "pool", "act")[prep_weight.i % 3]
                prep_weight.i += 1
                if eng == "act":
                    P.op("act", lambda e, o=a16[:, 0:n * CW], i=a32[:, 0:n * CW]: e.copy(out=o, in_=i),
                         r=[d32], w=[d16])
                else:
                    P.op(eng, lambda e, o=a16[:, 0:n * CW], i=a32[:, 0:n * CW]: e.tensor_copy(out=o, in_=i),
                         r=[d32], w=[d16])
                P.dma("pool", dst[cg, :, k0:k0 + n, :],
                      a16[:, 0:n * CW].rearrange("p (k c) -> p k c", c=CW), r=[d16], w=[wdep])
    prep_weight.i = 0
    wdep = Dep()
    base_mark = bump[0]
    if True:
        st32 = Ring([sb("p0a%d" % i, (128, 4096)) for i in range(3)])
        st16 = Ring([sb("p0b%d" % i, (128, 4096), BF16) for i in range(3)])
        for (name, c0, n, CW, kind) in groups:
            prep_weight(w_in, D, c0, n, CW, WB[name], st32, st16)
        cwd = min(512, D)
        prep_weight(w_br_a, QK, 0, D, cwd, WB["bra"], st32, st16)
        prep_weight(w_br_b, GVW, 0, D, cwd, WB["brb"], st32, st16)
        prep_weight(w_out, D, 0, D, cwd, WB["out"], st32, st16)
        prep_weight(w_up, D, 0, DFF, 128, WB["upa"], st32, st16)
        prep_weight(w_up, D, DFF, DFF, 128, WB["upg"], st32, st16)
        prep_weight(w_down, DFF, 0, D, 128, WB["down"], st32, st16)
        P.barrier()

    def rstd_from_ss(ss_ap, n, dep):
        P.op("act", lambda e: e.activation(out=ss_ap, in_=ss_ap, func=AF.Ln, bias=eps_col[0:ss_ap.shape[0], :],
                                           scale=1.0 / n), r=[dep], w=[dep])
        P.op("act", lambda e: e.activation(out=ss_ap, in_=ss_ap, func=AF.Exp, scale=-0.5), r=[dep], w=[dep])

    def norm_transpose_block(loader, g_rep, gT, xt_ring, xn_ring, ss_ring, hT, hT_dep, col0, width=128):
        xa, xd = xt_ring.next()
        na, nd = xn_ring.next()
        sa, sd = ss_ring.next()
        loader(xa, xd)
        P.op("act", lambda e: e.activation(out=na[0:width, :], in_=xa[0:width, :], func=AF.Square,
                                           accum_out=sa[0:width, :]), r=[xd], w=[sd, nd])
        rstd_from_ss(sa[0:width, :], D, sd)
        if g_rep is not None:
            P.op("dve", lambda e: e.scalar_tensor_tensor(out=na[0:width, :], in0=xa[0:width, :], scalar=sa[0:width, 0:1],
                                                         in1=g_rep[0:width, :], op0=ALU.mult, op1=ALU.mult),
                 r=[xd, sd], w=[nd])
        else:
            P.op("dve", lambda e: e.tensor_scalar(out=na[0:width, :], in0=xa[0:width, :], scalar1=sa[0:width, 0:1],
                                                  scalar2=None, op0=ALU.mult), r=[xd, sd], w=[nd])
        G = min(4, KC)
        for kg in range(KC // G):
            pa, pd = PS.next()
            pb = pa.bitcast(BF16)
            for i in range(G):
                kc = kg * G + i
                P.op("pe", lambda e, o=pb[:, i * 128:i * 128 + width], i_=na[0:width, kc * 128:(kc + 1) * 128]:
                     e.transpose(out=o, in_=i_, identity=ident_b[0:width, 0:width]), r=[nd], w=[pd])
            if g_rep is not None:
                src = pb[:, 0:G * 128].rearrange("p (g t) -> p g t", t=128)[:, :, 0:width]
                dst = hT[:, kg * G:(kg + 1) * G, col0:col0 + width]
                if kg % 2 == 0:
                    P.op("act", lambda e, o=dst, i_=src: e.copy(out=o, in_=i_), r=[pd], w=[hT_dep])
                else:
                    P.op("dve", lambda e, o=dst, i_=src: e.tensor_copy(out=o, in_=i_), r=[pd], w=[hT_dep])
            else:
                for i in range(G):
                    kc = kg * G + i
                    src = pb[:, i * 128:i * 128 + width]
                    dst = hT[:, kc, col0:col0 + width]
                    if i % 2 == 0:
                        P.op("act", lambda e, o=dst, i_=src, kc=kc: e.activation(out=o, in_=i_, func=AF.Copy,
                                                                                 scale=gT[:, kc:kc + 1]),
                             r=[pd], w=[hT_dep])
                    else:
                        P.op("dve", lambda e, o=dst, i_=src, kc=kc: e.tensor_scalar(out=o, in0=i_, scalar1=gT[:, kc:kc + 1],
                                                                                    scalar2=None, op0=ALU.mult),
                             r=[pd], w=[hT_dep])

    def mm_group(out_ps, pd, lhs_list, rhs_list, r):
        n = len(lhs_list)
        for i in range(n):
            P.op("pe", lambda e, l=lhs_list[i], rr=rhs_list[i], s=(i == 0), t=(i == n - 1):
                 e.matmul(out_ps, lhsT=l, rhs=rr, start=s, stop=t), r=r, w=[pd])

    bump[0] = base_mark
    if True:
        psb = sb
        g_rep = psb("a_grep", (128, D))
        xt_ring = Ring([psb("a_xt%d" % i, (128, D)) for i in range(2)])
        xn_ring = Ring([psb("a_xn%d" % i, (128, D), BF16) for i in range(2)])
        ss_ring = Ring([psb("a_ss%d" % i, (128, 1)) for i in range(2)])
        hT = psb("a_hT", (128, KC, TT), BF16)
        hT_dep = Dep()
        wt_ring = Ring([psb("a_wt%d" % i, (128, KC, 512), BF16) for i in range(2)])
        sq_ring = Ring([psb("a_sq%d" % i, (128, 512), BF16) for i in range(2)])
        rb_ring = Ring([psb("a_rb%d" % i, (128, 512)) for i in range(2)])
        st_ring = Ring([psb("a_st%d" % i, (128, 512), BF16) for i in range(4)])
        st32_ring = Ring([psb("a_st32%d" % i, (32, 512)) for i in range(2)])
        gdep = Dep()
        P.dma("sp", g_rep, g_mix.partition_broadcast(128)[:, 0, :], w=[gdep])
        P.barrier()
        sdep = Dep()
        for tt in range(NTT):
            t0 = tt * TT
            for b in range(TT // 128):
                norm_transpose_block(lambda xa, xd, r0=t0 + b * 128: P.dma("sp", xa, x_in[r0:r0 + 128, :], w=[xd]),
                                     g_rep, None, xt_ring, xn_ring, ss_ring, hT, hT_dep, b * 128)
            for (name, c0, n, CW, kind) in groups:
                for cg in range(n // CW):
                    wa, wd = wt_ring.next()
                    wv = wa[:, :, 0:CW]
                    P.dma("sp", wv, WB[name][cg], w=[wd])
                    if kind.startswith("tm"):
                        for b in range(TT // 128):
                            pa, pd = PS.next()
                            mm_group(pa[:, 0:CW], pd, [hT[:, kc, b * 128:(b + 1) * 128] for kc in range(KC)],
                                     [wv[:, kc, :] for kc in range(KC)], [hT_dep, wd])
                            sa, sd = st_ring.next()
                            fn = AF.Silu if kind == "tm_silu" else AF.Copy
                            if kind == "tm_silu":
                                P.op("act", lambda e, o=sa[:, 0:CW], i_=pa[:, 0:CW]:
                                     e.activation(out=o, in_=i_, func=AF.Silu), r=[pd], w=[sd])
                            else:
                                P.op("dve", lambda e, o=sa[:, 0:CW], i_=pa[:, 0:CW]: e.tensor_copy(out=o, in_=i_),
                                     r=[pd], w=[sd])
                            dst = {"v": VV, "gv": GVM, "gr": GRM}[name]
                            P.dma("pool", dst[t0 + b * 128: t0 + (b + 1) * 128, cg * CW:(cg + 1) * CW], sa[:, 0:CW],
                                  r=[sd], w=[sdep])
                        continue
                    for j in range(max(1, CW // 128)):
                        M = min(128, CW)
                        ct = cg * max(1, CW // 128) + j
                        pa, pd = PS.next()
                        mm_group(pa[0:M, :], pd, [wv[:, kc, j * 128:j * 128 + M] for kc in range(KC)],
                                 [hT[:, kc, :] for kc in range(KC)], [hT_dep, wd])
                        if kind == "fm_qk":
                            qa, qd = sq_ring.next()
                            P.op("act", lambda e, o=qa, i_=pa: e.activation(out=o, in_=i_, func=AF.Square),
                                 r=[pd], w=[qd])
                            p2, p2d = PS.next()
                            P.op("pe", lambda e, o=p2, i_=qa: e.matmul(o, lhsT=ones_b, rhs=i_, start=True, stop=True),
                                 r=[qd], w=[p2d])
                            ra, rd = rb_ring.next()
                            P.op("act", lambda e, o=ra, i_=p2: e.activation(out=o, in_=i_, func=AF.Ln, bias=eps_col,
                                                                            scale=1.0 / 128), r=[p2d], w=[rd])
                            P.op("act", lambda e, o=ra: e.activation(out=o, in_=o, func=AF.Exp, scale=-0.5),
                                 r=[rd], w=[rd])
                            sa, sd = st_ring.next()
                            gcol = gq_col[:, 0:1] if name == "q" else gq_col[:, 1:2]
                            P.op("dve", lambda e, o=sa, i_=pa, g=gcol, r_=ra:
                                 e.scalar_tensor_tensor(out=o, in0=i_, scalar=g, in1=r_, op0=ALU.mult, op1=ALU.mult),
                                 r=[pd, rd], w=[sd])
                            dst = QT if name == "q" else KT
                            P.dma("pool", dst[ct, :, t0:t0 + TT], sa, r=[sd], w=[sdep])
                        elif kind == "fm_plain":
                            sa, sd = st_ring.next()
                            sc = 256.0 ** -0.5 if name == "gq" else 1.0
                            P.op("act", lambda e, o=sa, i_=pa, s=sc: e.activation(out=o, in_=i_, func=AF.Copy, scale=s),
                                 r=[pd], w=[sd])
                            dst = GQT if name == "gq" else GKT
                            P.dma("pool", dst[ct, :, t0:t0 + TT], sa, r=[sd], w=[sdep])
                        elif kind == "fm_sig":
                            sa, sd = st_ring.next()
                            P.op("act", lambda e, o=sa, i_=pa: e.activation(out=o, in_=i_, func=AF.Sigmoid),
                                 r=[pd], w=[sd])
                            dst = SGA if name == "ga" else SGB
                            P.dma("pool", dst[ct, :, t0:t0 + TT], sa, r=[sd], w=[sdep])
                        elif kind == "fm_glr":
                            sa, sd = st32_ring.next()
                            P.op("dve", lambda e, o=sa, i_=pa[0:32, :]: e.tensor_copy(out=o, in_=i_), r=[pd], w=[sd])
                            P.dma("pool", GLT[:, t0:t0 + TT], sa, r=[sd], w=[sdep])
            name, c0, n, CW, kind = [g for g in groups if g[0] == "gk"][0]
            for cg in range(n // CW):
                wa, wd = wt_ring.next()
                wv = wa[:, :, 0:CW]
                P.dma("sp", wv, WB[name][cg], w=[wd])
                for b in range(TT // 128):
                    pa, pd = PS.next()
                    mm_group(pa[:, 0:CW], pd, [hT[:, kc, b * 128:(b + 1) * 128] for kc in range(KC)],
                             [wv[:, kc, :] for kc in range(KC)], [hT_dep, wd])
                    sa, sd = st_ring.next()
                    P.op("dve", lambda e, o=sa[:, 0:CW], i_=pa[:, 0:CW]: e.tensor_copy(out=o, in_=i_), r=[pd], w=[sd])
                    P.dma("pool", GKM[t0 + b * 128: t0 + (b + 1) * 128, cg * CW:(cg + 1) * CW], sa[:, 0:CW],
                          r=[sd], w=[sdep])
        P.barrier()
    if "stopA" in dbg:
        P.finish()
        P.emit()
        es.close()
        return nc

    bump[0] = base_mark
    if True:
        PS_O = Ring(PS.aps[0:4])
        PS_S = Ring(PS.aps[4:7])
        PS_T = Ring(PS.aps[7:8])
        lamv = sb("b_lamv", (1, 512))
        lamp = sb("b_lamp", (1, 512))
        lams = sb("b_lams", (1, 4))
        neglam = sb("b_neglam", (128, 1))
        gsub = sb("b_gsub", (128, 256))
        relb = sb("b_relb", (32, 8))
        onehot = sb("b_onehot", (32, 6 * 640))
        ub_sb = sb("b_ubsb", (8, 6 * 640))
        fbl = sb("b_fbl", (128, 2, 8))
        fb4 = sb("b_fb4", (128, 8, 4))
        BT = sb("b_bt", (128, 8, 512))
        th_ring = Ring([sb("b_th%d" % i, (128, 512)) for i in range(2)])
        q_ring = Ring([sb("b_q%d" % i, (128, 2, NT), BF16) for i in range(2)])
        k_ring = Ring([sb("b_k%d" % i, (128, 2, NT), BF16) for i in range(2)])
        v_ring = Ring([sb("b_v%d" % i, (128, NKT, 257), BF16) for i in range(2)])
        PT = Ring([sb("b_pt%d" % i, (128, 512), BF16) for i in range(3)])
        TMP = Ring([sb("b_tmp%d" % i, (128, 512)) for i in range(2)])
        OA = sb("b_oa", (128, 4, 256))
        oa_dep = [Dep() for _ in range(4)]
        rz_ring = Ring([sb("b_rz%d" % i, (128, 1)) for i in range(4)])
        ss2_ring = Ring([sb("b_ss%d" % i, (128, 1)) for i in range(4)])
        on_ring = Ring([sb("b_on%d" % i, (128, 256), BF16) for i in range(2)])
        ost_ring = Ring([sb("b_ost%d" % i, (128, 2, 128), BF16) for i in range(2)])
        junkb = (sb("b_junk", (128, 256), BF16), Dep())
        sd0 = Dep()
        P.dma("sp", lamv, lam_in, w=[sd0])
        P.dma("sp", gsub, da_subln_g.partition_broadcast(128)[:, 0, :], w=[sd0])
        P.dma("sp", relb, rel_bias, w=[sd0])
        P.dma("sp", onehot, c_onehot, w=[sd0])
        P.dma("sp", fbl[:, 0, :], rel_bias[15:16, :].partition_broadcast(128)[:, 0, :], w=[sd0])
        P.dma("sp", fbl[:, 1, :], rel_bias[31:32, :].partition_broadcast(128)[:, 0, :], w=[sd0])
        for i in range(2):
            P.op("dve", lambda e, i=i: e.memset(v_ring.aps[i][:, :, 256:257], 1.0), w=[sd0])
        P.op("dve", lambda e: e.tensor_tensor(out=lamp[:, 0:128], in0=lamv[:, 0:128], in1=lamv[:, 128:256], op=ALU.mult),
             r=[sd0], w=[sd0])
        P.op("dve", lambda e: e.tensor_tensor(out=lamp[:, 128:256], in0=lamv[:, 256:384], in1=lamv[:, 384:512],
                                              op=ALU.mult), r=[sd0], w=[sd0])
        P.op("dve", lambda e: e.tensor_reduce(out=lams[:, 0:2], in_=lamp[:, 0:256].rearrange("p (a b) -> p a b", b=128),
                                              axis=mybir.AxisListType.X, op=ALU.add), r=[sd0], w=[sd0])
        P.op("act", lambda e: e.activation(out=lams[:, 0:2], in_=lams[:, 0:2], func=AF.Exp), r=[sd0], w=[sd0])
        P.op("dve", lambda e: e.tensor_tensor(out=lams[:, 2:3], in0=lams[:, 1:2], in1=lams[:, 0:1], op=ALU.subtract),
             r=[sd0], w=[sd0])
        P.op("dve", lambda e: e.tensor_scalar(out=lams[:, 2:3], in0=lams[:, 2:3], scalar1=-0.2, scalar2=None,
                                              op0=ALU.add), r=[sd0], w=[sd0])
        pa, pd = PS_T.next()
        P.op("pe", lambda e: e.matmul(pa[:, 0:1], lhsT=ones_f[0:1, :], rhs=lams[:, 2:3], start=True, stop=True),
             r=[sd0], w=[pd])
        P.op("dve", lambda e: e.tensor_copy(out=neglam, in_=pa[:, 0:1]), r=[pd], w=[sd0])
        P.op("dve", lambda e: e.tensor_scalar(out=gsub, in0=gsub, scalar1=0.8, scalar2=None, op0=ALU.mult),
             r=[sd0], w=[sd0])
        for v in range(4):
            P.op("dve", lambda e, v=v: e.tensor_scalar(out=fb4[:, :, v], in0=fbl[:, v % 2, :], scalar1=SHIFT,
                                                       scalar2=(flags[:, 0:1] if v >= 2 else 0.0),
                                                       op0=ALU.add, op1=ALU.add), r=[sd0], w=[sd0])
        for ch in range(8):
            pa, pd = PS_S.next()
            P.op("pe", lambda e, pa=pa, ch=ch: e.matmul(pa[0:8, 0:480], lhsT=relb, rhs=onehot[:, ch * 480:(ch + 1) * 480],
                                                        start=True, stop=True), r=[sd0], w=[pd])
            P.op("dve", lambda e, pa=pa, ch=ch: e.tensor_copy(out=ub_sb[:, ch * 480:(ch + 1) * 480], in_=pa[0:8, 0:480]),
                 r=[pd], w=[sd0])
        P.dma("pool", UBI, ub_sb, r=[sd0], w=[sd0])
        P.barrier()
        bt_dep = Dep()
        odep = Dep()
        for h in range(HA):
            qa, qd = q_ring.next()
            ka, kd = k_ring.next()
            va, vd = v_ring.next()
            P.dma("sp", qa, QT[2 * h:2 * h + 2].rearrange("m p t -> p m t"), w=[qd])
            P.dma("sp", ka, KT[2 * h:2 * h + 2].rearrange("m p t -> p m t"), w=[kd])
            P.dma("sp", va[:, :, 0:256], VV[:, h * 256:(h + 1) * 256].rearrange("(j p) c -> p j c", p=128), w=[vd])
            for di in range(6):
                ta, td = th_ring.next()
                src = bass.AP(tensor=UBI.tensor, offset=h * 6 * 640 + di * 640, ap=[[1, 128], [1, 512]])
                P.dma("sp", ta, src, w=[td])
                pa, pd = PS_T.next()
                P.op("pe", lambda e, pa=pa, ta=ta: e.matmul(pa, lhsT=flip_f, rhs=ta, start=True, stop=True),
                     r=[td], w=[pd])
                P.op("dve", lambda e, pa=pa, di=di: e.tensor_scalar(out=BT[:, di, :], in0=pa, scalar1=SHIFT, scalar2=None,
                                                                    op0=ALU.add), r=[pd], w=[bt_dep])
            P.op("dve", lambda e: e.tensor_scalar(out=BT[:, 6, :], in0=BT[:, 5, :], scalar1=flags[:, 0:1], scalar2=None,
                                                  op0=ALU.add), r=[bt_dep], w=[bt_dep])
            P.op("dve", lambda e: e.tensor_scalar(out=BT[:, 7, :], in0=BT[:, 0, :], scalar1=flags[:, 0:1], scalar2=None,
                                                  op0=ALU.add), r=[bt_dep], w=[bt_dep])
            for c in range(NQC):
                for m in range(2):
                    qm = qa[:, m, c * 512:(c + 1) * 512]
                    O = [PS_O.next() for _ in range(4)]

                    def st_mm(j):
                        sa, sd = PS_S.next()
                        P.op("pe", lambda e, sa=sa, j=j, ka=ka, m=m, qm=qm: e.matmul(sa, lhsT=ka[:, m, j * 128:(j + 1) * 128], rhs=qm,
                                                                  start=True, stop=True), r=[kd, qd], w=[sd])
                        return sa, sd
                    nxt = st_mm(0)
                    for j in range(NKT):
                        sa, sd = nxt
                        if j + 1 < NKT:
                            nxt = st_mm(j + 1)
                        pt, ptd = PT.next()
                        dl = j - 4 * c
                        if -1 <= dl <= 4:
                            var = dl + 1
                            if j == NKT // 2 and c == NQC // 2 - 1:
                                var = 6
                            if j == NKT // 2 - 1 and c == NQC // 2:
                                var = 7
                            ta, td = TMP.next()
                            P.op("dve", lambda e, ta=ta, sa=sa, var=var: e.tensor_tensor(out=ta, in0=sa, in1=BT[:, var, :],
                                                                                        op=ALU.add),
                                 r=[sd, bt_dep], w=[td])
                            P.op("act", lambda e, pt=pt, ta=ta: e.activation(out=pt, in_=ta, func=AF.Exp),
                                 r=[td], w=[ptd])
                        else:
                            idx = (0 if dl < 0 else 1) + (2 if ((j < NKT // 2) != (c < NQC // 2)) else 0)
                            P.op("act", lambda e, pt=pt, sa=sa, idx=idx, h=h: e.activation(out=pt, in_=sa, func=AF.Exp,
                                                                                     bias=fb4[:, h, idx:idx + 1]),
                                 r=[sd], w=[ptd])
                        for qb in range(4):
                            P.op("pe", lambda e, qb=qb, pt=pt, j=j, O=O, va=va: e.matmul(O[qb][0][:, 0:257],
                                                                             lhsT=pt[:, qb * 128:(qb + 1) * 128],
                                                                             rhs=va[:, j, :], start=(j == 0),
                                                                             stop=(j == NKT - 1)),
                                 r=[ptd, vd], w=[O[qb][1]])
                    for qb in range(4):
                        oa_, od_ = O[qb]
                        rz, rzd = rz_ring.next()
                        P.op("dve", lambda e, rz=rz, oa_=oa_: e.reciprocal(out=rz, in_=oa_[:, 256:257]), r=[od_], w=[rzd])
                        if m == 0:
                            P.op("dve", lambda e, rz=rz, oa_=oa_, qb=qb: e.tensor_scalar(out=OA[:, qb, :], in0=oa_[:, 0:256],
                                                                                        scalar1=rz, scalar2=None,
                                                                                        op0=ALU.mult),
                                 r=[od_, rzd], w=[oa_dep[qb]])
                            continue
                        P.op("dve", lambda e, rz=rz: e.tensor_scalar(out=rz, in0=rz, scalar1=neglam, scalar2=None,
                                                                     op0=ALU.mult), r=[rzd], w=[rzd])
                        P.op("dve", lambda e, rz=rz, oa_=oa_, qb=qb: e.scalar_tensor_tensor(
                            out=OA[:, qb, :], in0=oa_[:, 0:256], scalar=rz, in1=OA[:, qb, :], op0=ALU.mult, op1=ALU.add),
                            r=[od_, rzd, oa_dep[qb]], w=[oa_dep[qb]])
                        ss, ssd = ss2_ring.next()
                        P.op("act", lambda e, ss=ss, qb=qb: e.activation(out=junkb[0], in_=OA[:, qb, :], func=AF.Square,
                                                                         accum_out=ss), r=[oa_dep[qb]], w=[ssd, junkb[1]])
                        rstd_from_ss(ss, 256, ssd)
                        on, ond = on_ring.next()
                        P.op("dve", lambda e, on=on, ss=ss, qb=qb: e.scalar_tensor_tensor(
                            out=on, in0=OA[:, qb, :], scalar=ss, in1=gsub, op0=ALU.mult, op1=ALU.mult),
                            r=[oa_dep[qb], ssd], w=[ond])
                        pa, pd = PS_T.next()
                        pb = pa.bitcast(BF16)
                        for i in range(2):
                            P.op("pe", lambda e, i=i, pb=pb, on=on: e.transpose(out=pb[:, i * 128:(i + 1) * 128],
                                                                                in_=on[:, i * 128:(i + 1) * 128],
                                                                                identity=ident_b), r=[ond], w=[pd])
                        osb, osd = ost_ring.next()
                        P.op("act", lambda e, osb=osb, pb=pb: e.copy(out=osb, in_=pb[:, 0:256].rearrange(
                            "p (a b) -> p a b", b=128)), r=[pd], w=[osd])
                        t0 = c * 512 + qb * 128
                        P.dma("pool", OAT[2 * h:2 * h + 2, :, t0:t0 + 128].rearrange("k p t -> p k t"), osb,
                              r=[osd], w=[odep])
        P.barrier()
    if "stopB" in dbg:
        P.finish()
        P.emit()
        es.close()
        return nc

    bump[0] = base_mark
    if True:
        G2 = 2 * HG
        NHB = max(1, GKW // 512)
        HW_ = min(512, GKW)
        PS_G = Ring(PS.aps[0:2])
        PS_PO = Ring(PS.aps[2:6])
        PS_KV = Ring(PS.aps[6:8])
        wgp = [sb("c_wgf", (32, GKW)), sb("c_wgb", (32, GKW))]
        bgp = [sb("c_bgf", (1, GKW)), sb("c_bgb", (1, GKW))]
        ggr = sb("c_ggr", (128, 512))
        S = sb("c_S", (128, G2, 512))
        Sb = sb("c_Sb", (128, G2, 512), BF16)
        s_dep = [Dep() for _ in range(G2)]
        sb_dep = [Dep() for _ in range(G2)]

        def ring2(name, shape, dt=F32, n=2):
            return Ring([sb("%s%d" % (name, i), shape, dt) for i in range(n)])
        r_glt = ring2("c_glt", (32, 128))
        r_gqt = ring2("c_gqt", (128, G2, 128), BF16)
        r_gkt = ring2("c_gkt", (128, G2, 128), BF16)
        r_gk = ring2("c_gk", (128, GKW), BF16)
        r_gv = ring2("c_gv", (128, GVW), BF16)
        r_gr = ring2("c_gr", (128, GVW), BF16)
        r_of = ring2("c_of", (128, GVW))
        r_ls = ring2("c_ls", (128, GKW))
        r_lb = ring2("c_lb", (128, GKW), BF16)
        r_ep = ring2("c_ep", (128, G2, 128))
        r_em = ring2("c_em", (128, G2, 128))
        r_qd = ring2("c_qd", (128, G2, 128), BF16)
        r_ki = ring2("c_ki", (128, G2, 128), BF16)
        r_ed = ring2("c_ed", (128, GKW))
        r_kd = ring2("c_kd", (128, GKW), BF16)
        r_at = ring2("c_at", (128, HG, 128), BF16)
        r_ofs = ring2("c_ofs", (128, GVW))
        r_os = ring2("c_os", (128, 512))
        r_gg = ring2("c_gg", (128, 512))
        r_ob = ring2("c_ob", (128, 512), BF16)
        r_obst = ring2("c_obst", (128, 4, 128), BF16)
        r_ss = ring2("c_ss", (128, 1), n=4)
        junkc = (sb("c_junk", (128, 512), BF16), Dep())
        sd0 = Dep()
        for t in wgp:
            P.op("dve", lambda e, t=t: e.memset(t, 0.0), w=[sd0])
        P.dma("sp", wgp[0][0:16, :], w_gate_f, r=[sd0], w=[sd0])
        P.dma("sp", wgp[1][16:32, :], w_gate_b, r=[sd0], w=[sd0])
        P.dma("sp", bgp[0], b_gate_f, w=[sd0])
        P.dma("sp", bgp[1], b_gate_b, w=[sd0])
        P.dma("sp", ggr, gla_norm_g.partition_broadcast(128)[:, 0, :], w=[sd0])
        P.barrier()
        ofdep = Dep()
        obdep = Dep()
        NCH = NT // 64
        for di in range(2):
            fwd = di == 0
            for g in range(G2):
                P.op("dve", lambda e, g=g: e.memset(S[:, g, :], 0.0), w=[s_dep[g]])
                P.op("dve", lambda e, g=g: e.memset(Sb[:, g, :], 0.0), w=[sb_dep[g]])
            blocks = range(NB) if fwd else range(NB - 1, -1, -1)
            for blk in blocks:
                t0 = blk * 128
                glt, gltd = r_glt.next()
                gqt, gqtd = r_gqt.next()
                gkt, gktd = r_gkt.next()
                gk, gkd = r_gk.next()
                gv, gvd = r_gv.next()
                P.dma("sp", glt, GLT[:, t0:t0 + 128], w=[gltd])
                P.dma("sp", gqt, GQT[:, :, t0:t0 + 128].rearrange("k p t -> p k t"), w=[gqtd])
                P.dma("sp", gkt, GKT[:, :, t0:t0 + 128].rearrange("k p t -> p k t"), w=[gktd])
                P.dma("sp", gk, GKM[t0:t0 + 128, :], w=[gkd])
                P.dma("sp", gv, GVM[t0:t0 + 128, :], w=[gvd])
                if not fwd:
                    of, ofd = r_of.next()
                    gr, grd = r_gr.next()
                    P.dma("sp", of, OFW[t0:t0 + 128, :], r=[ofdep], w=[ofd])
                    P.dma("sp", gr, GRM[t0:t0 + 128, :], w=[grd])
                ls, lsd = r_ls.next()
                lb, lbd = r_lb.next()
                for hb in range(NHB):
                    pa, pd = PS_G.next()
                    cs = slice(hb * HW_, (hb + 1) * HW_)
                    P.op("pe", lambda e, pa=pa, cs=cs, glt=glt, di=di: e.matmul(pa[:, 0:HW_], lhsT=glt, rhs=wgp[di][:, cs],
                                                                         start=True, stop=False), r=[gltd], w=[pd])
                    P.op("pe", lambda e, pa=pa, cs=cs, di=di: e.matmul(pa[:, 0:HW_], lhsT=ones_f[0:1, :], rhs=bgp[di][:, cs],
                                                                start=False, stop=True), w=[pd])
                    P.op("act", lambda e, pa=pa, cs=cs, ls=ls: e.activation(out=ls[:, cs], in_=pa[:, 0:HW_], func=AF.Exp,
                                                                            scale=-1.0), r=[pd], w=[lsd])
                P.op("act", lambda e, ls=ls, lb=lb: e.activation(out=lb, in_=ls, func=AF.Ln, bias=ones_f[:, 0:1]),
                     r=[lsd], w=[lbd])
                ep, epd = r_ep.next()
                em, emd = r_em.next()
                qd_, qdd = r_qd.next()
                ki, kid = r_ki.next()
                tri = 0 if fwd else 2
                for g0 in range(0, G2, 4):
                    ng = min(4, G2 - g0)
                    pa, pd = PS_G.next()
                    for i in range(ng):
                        P.op("pe", lambda e, pa=pa, i=i, g=g0 + i, lb=lb, tri=tri: e.matmul(
                            pa[:, i * 128:(i + 1) * 128], lhsT=lb[:, g * 128:(g + 1) * 128], rhs=gla_b[:, tri, :],
                            start=True, stop=True), r=[lbd], w=[pd])
                    pv = pa[:, 0:ng * 128].rearrange("p (a b) -> p a b", b=128)
                    P.op("act", lambda e, pv=pv, ep=ep, g0=g0, ng=ng: e.activation(out=ep[:, g0:g0 + ng, :], in_=pv,
                                                                                   func=AF.Exp), r=[pd], w=[epd])
                    P.op("act", lambda e, pv=pv, em=em, g0=g0, ng=ng: e.activation(out=em[:, g0:g0 + ng, :], in_=pv,
                                                                                   func=AF.Exp, scale=-1.0),
                         r=[pd], w=[emd])
                P.op("dve", lambda e, qd_=qd_, gqt=gqt, ep=ep: e.tensor_tensor(out=qd_, in0=gqt, in1=ep, op=ALU.mult),
                     r=[gqtd, epd], w=[qdd])
                P.op("dve", lambda e, ki=ki, gkt=gkt, em=em: e.tensor_tensor(out=ki, in0=gkt, in1=em, op=ALU.mult),
                     r=[gktd, emd], w=[kid])
                ed, edd = r_ed.next()
                kd_, kdd = r_kd.next()
                ut = 1 if fwd else 3
                for hb in range(NHB):
                    pa, pd = PS_G.next()
                    cs = slice(hb * HW_, (hb + 1) * HW_)
                    P.op("pe", lambda e, pa=pa, cs=cs, lb=lb, ut=ut: e.matmul(pa[:, 0:HW_], lhsT=gla_b[:, ut, :], rhs=lb[:, cs],
                                                                       start=True, stop=True), r=[lbd], w=[pd])
                    P.op("act", lambda e, pa=pa, cs=cs, ed=ed: e.activation(out=ed[:, cs], in_=pa[:, 0:HW_], func=AF.Exp),
                         r=[pd], w=[edd])
                P.op("dve", lambda e, kd_=kd_, gk=gk, ed=ed: e.tensor_tensor(out=kd_, in0=gk, in1=ed, op=ALU.mult),
                     r=[gkd, edd], w=[kdd])
                at, atd = r_at.next()
                pa, pd = PS_G.next()
                for hd in range(HG):
                    for dh in range(2):
                        P.op("pe", lambda e, pa=pa, hd=hd, dh=dh, ki=ki, qd_=qd_: e.matmul(
                            pa[:, hd * 128:(hd + 1) * 128], lhsT=ki[:, hd * 2 + dh, :], rhs=qd_[:, hd * 2 + dh, :],
                            start=(dh == 0), stop=(dh == 1)), r=[kid, qdd], w=[pd])
                mk = 4 if fwd else 5
                for hd in range(HG):
                    P.op("dve", lambda e, pa=pa, hd=hd, at=at, mk=mk: e.tensor_tensor(
                        out=at[:, hd, :], in0=pa[:, hd * 128:(hd + 1) * 128], in1=gla_f[:, mk, :], op=ALU.mult),
                        r=[pd], w=[atd])
                if fwd and blk == 0 and "d_lb" in dbg:
                    P.dma("pool", dscr("d_lb", (128, GKW)), lb, r=[lbd], w=[Dep()])
                    P.dma("pool", dscr("d_ep", (128, G2, 128), F32), ep, r=[epd], w=[Dep()])
                    P.dma("pool", dscr("d_at", (128, HG, 128)), at, r=[atd], w=[Dep()])
                    P.dma("pool", dscr("d_qd", (128, G2, 128)), qd_, r=[qdd], w=[Dep()])
                    P.dma("pool", dscr("d_kd", (128, GKW)), kd_, r=[kdd], w=[Dep()])
                po = [PS_PO.next() for _ in range(HG)]
                for ch in ((0, 1) if fwd else (1, 0)):
                    n = blk * 2 + ch
                    if (fwd and n == NCH // 2) or ((not fwd) and n == NCH // 2 - 1):
                        for g in range(G2):
                            P.op("dve", lambda e, g=g: e.tensor_scalar(out=S[:, g, :], in0=S[:, g, :], scalar1=flags[:, 1:2],
                                                                       scalar2=None, op0=ALU.mult),
                                 r=[s_dep[g]], w=[s_dep[g]])
                            P.op("act", lambda e, g=g: e.copy(out=Sb[:, g, :], in_=S[:, g, :]), r=[s_dep[g]], w=[sb_dep[g]])
                    rows = slice(ch * 64, ch * 64 + 64)
                    dcol = (ch * 64 + 63) if fwd else (ch * 64)
                    for hd in range(HG):
                        pa, pd = po[hd]
                        vs = slice(hd * 512, (hd + 1) * 512)
                        P.op("pe", lambda e, pa=pa, at=at, gv=gv, hd=hd, rows=rows, vs=vs: e.matmul(
                            pa[rows, :], lhsT=at[rows, hd, rows], rhs=gv[rows, vs], start=True, stop=False),
                            r=[atd, gvd], w=[pd])
                        for dh in range(2):
                            g = hd * 2 + dh
                            P.op("pe", lambda e, pa=pa, qd_=qd_, g=g, rows=rows, dh=dh: e.matmul(
                                pa[rows, :], lhsT=qd_[:, g, rows], rhs=Sb[:, g, :], start=False, stop=(dh == 1)),
                                r=[qdd, sb_dep[g]], w=[pd])
                        for dh in range(2):
                            g = hd * 2 + dh
                            ka_, kvd = PS_KV.next()
                            P.op("pe", lambda e, ka_=ka_, kd_=kd_, gv=gv, g=g, rows=rows, vs=vs: e.matmul(
                                ka_, lhsT=kd_[rows, g * 128:(g + 1) * 128], rhs=gv[rows, vs], start=True, stop=True),
                                r=[kdd, gvd], w=[kvd])
                            P.op("dve", lambda e, ka_=ka_, g=g, ep=ep, dcol=dcol: e.scalar_tensor_tensor(
                                out=S[:, g, :], in0=S[:, g, :], scalar=ep[:, g, dcol:dcol + 1], in1=ka_,
                                op0=ALU.mult, op1=ALU.add), r=[kvd, epd, s_dep[g], sb_dep[g]], w=[s_dep[g]])
                            P.op("act", lambda e, g=g: e.copy(out=Sb[:, g, :], in_=S[:, g, :]), r=[s_dep[g]], w=[sb_dep[g]])
                if fwd:
                    ofs, ofsd = r_ofs.next()
                    for hd in range(HG):
                        pa, pd = po[hd]
                        vs = slice(hd * 512, (hd + 1) * 512)
                        if hd % 2 == 0:
                            P.op("act", lambda e, pa=pa, ofs=ofs, vs=vs: e.copy(out=ofs[:, vs], in_=pa), r=[pd], w=[ofsd])
                        else:
                            P.op("dve", lambda e, pa=pa, ofs=ofs, vs=vs: e.tensor_copy(out=ofs[:, vs], in_=pa),
                                 r=[pd], w=[ofsd])
                    P.dma("pool", OFW[t0:t0 + 128, :], ofs, r=[ofsd], w=[ofdep])
                else:
                    for hd in range(HG):
                        pa, pd = po[hd]
                        vs = slice(hd * 512, (hd + 1) * 512)
                        osm, osd = r_os.next()
                        P.op("dve", lambda e, pa=pa, osm=osm, of=of, vs=vs: e.tensor_tensor(out=osm, in0=pa, in1=of[:, vs],
                                                                                          op=ALU.add),
                             r=[pd, ofd], w=[osd])
                        ss, ssd = r_ss.next()
                        P.op("act", lambda e, osm=osm, ss=ss: e.activation(out=junkc[0], in_=osm, func=AF.Square,
                                                                           accum_out=ss), r=[osd], w=[ssd, junkc[1]])
                        rstd_from_ss(ss, 512, ssd)
                        gg, ggd = r_gg.next()
                        P.op("dve", lambda e, gg=gg, gr=gr, vs=vs: e.tensor_tensor(out=gg, in0=gr[:, vs], in1=ggr, op=ALU.mult),
                             r=[grd], w=[ggd])
                        ob, obd = r_ob.next()
                        P.op("dve", lambda e, ob=ob, osm=osm, ss=ss, gg=gg: e.scalar_tensor_tensor(
                            out=ob, in0=osm, scalar=ss, in1=gg, op0=ALU.mult, op1=ALU.mult), r=[osd, ssd, ggd], w=[obd])
                        pt_, ptd_ = PS_G.next()
                        pb = pt_.bitcast(BF16)
                        for i in range(4):
                            P.op("pe", lambda e, pb=pb, ob=ob, i=i: e.transpose(out=pb[:, i * 128:(i + 1) * 128],
                                                                                in_=ob[:, i * 128:(i + 1) * 128],
                                                                                identity=ident_b), r=[obd], w=[ptd_])
                        obst, obsd = r_obst.next()
                        P.op("act", lambda e, pb=pb, obst=obst: e.copy(out=obst, in_=pb[:, 0:512].rearrange(
                            "p (a b) -> p a b", b=128)), r=[ptd_], w=[obsd])
                        P.dma("pool", OBT[hd * 4:(hd + 1) * 4, :, t0:t0 + 128].rearrange("k p t -> p k t"), obst,
                              r=[obsd], w=[obdep])
            P.barrier()
    if "stopC" in dbg:
        P.finish()
        P.emit()
        es.close()
        return nc

    bump[0] = base_mark
    PSD = Ring(PS.aps)
    if True:
        CWD = min(512, D)
        NJ = CWD // 128
        oat = sb("d_oat", (128, KCB, TT), BF16)
        obt = sb("d_obt", (128, KCG, TT), BF16)
        mT = sb("d_mT", (128, KC, TT), BF16)
        oat_d, obt_d, mT_d = Dep(), Dep(), Dep()
        r_wbr = Ring([sb("d_wbr%d" % i, (128, max(KCB, KCG), CWD), BF16) for i in range(2)])
        r_wo = Ring([sb("d_wo%d" % i, (128, KC, CWD), BF16) for i in range(2)])
        r_sg = Ring([sb("d_sg%d" % i, (128, TT), BF16) for i in range(4)])
        r_t = Ring([sb("d_t%d" % i, (128, TT)) for i in range(4)])
        r_xp = Ring([sb("d_xp%d" % i, (128, CWD)) for i in range(3)])
        x1dep = Dep()
        for tt in range(NTT):
            t0 = tt * TT
            P.dma("sp", oat, OAT[:, :, t0:t0 + TT].rearrange("k p t -> p k t"), w=[oat_d])
            P.dma("sp", obt, OBT[:, :, t0:t0 + TT].rearrange("k p t -> p k t"), w=[obt_d])
            for cg in range(D // CWD):
                wa_, wad = r_wbr.next()
                wb_, wbd = r_wbr.next()
                P.dma("sp", wa_[:, 0:KCB, :], WB["bra"][cg], w=[wad])
                P.dma("sp", wb_[:, 0:KCG, :], WB["brb"][cg], w=[wbd])
                for j in range(NJ):
                    ct = cg * NJ + j
                    sga_, sgad = r_sg.next()
                    sgb_, sgbd = r_sg.next()
                    P.dma("sp", sga_, SGA[ct, :, t0:t0 + TT], w=[sgad])
                    P.dma("sp", sgb_, SGB[ct, :, t0:t0 + TT], w=[sgbd])
                    pa, pd = PSD.next()
                    mm_group(pa, pd, [wa_[:, kc, j * 128:(j + 1) * 128] for kc in range(KCB)],
                             [oat[:, kc, :] for kc in range(KCB)], [wad, oat_d])
                    pb_, pbd = PSD.next()
                    mm_group(pb_, pbd, [wb_[:, kc, j * 128:(j + 1) * 128] for kc in range(KCG)],
                             [obt[:, kc, :] for kc in range(KCG)], [wbd, obt_d])
                    t1, t1d = r_t.next()
                    t2, t2d = r_t.next()
                    P.op("dve", lambda e, t1=t1, pa=pa, sga_=sga_: e.tensor_tensor(out=t1, in0=pa, in1=sga_, op=ALU.mult),
                         r=[pd, sgad], w=[t1d])
                    P.op("dve", lambda e, t2=t2, pb_=pb_, sgb_=sgb_: e.tensor_tensor(out=t2, in0=pb_, in1=sgb_, op=ALU.mult),
                         r=[pbd, sgbd], w=[t2d])
                    P.op("pool", lambda e, t1=t1, t2=t2, ct=ct: e.tensor_tensor(out=mT[:, ct, :], in0=t1, in1=t2, op=ALU.add),
                         r=[t1d, t2d], w=[mT_d])
            for cg in range(D // CWD):
                wo_, wod = r_wo.next()
                P.dma("sp", wo_, WB["out"][cg], w=[wod])
                for b in range(TT // 128):
                    r0 = t0 + b * 128
                    xp, xpd = r_xp.next()
                    P.dma("sp", xp, x_in[r0:r0 + 128, cg * CWD:(cg + 1) * CWD], w=[xpd])
                    pa, pd = PSD.next()
                    mm_group(pa[:, 0:CWD], pd, [mT[:, kc, b * 128:(b + 1) * 128] for kc in range(KC)],
                             [wo_[:, kc, :] for kc in range(KC)], [wod, mT_d])
                    P.op("dve", lambda e, xp=xp, pa=pa: e.tensor_tensor(out=xp, in0=pa[:, 0:CWD], in1=xp, op=ALU.add),
                         r=[pd, xpd], w=[xpd])
                    P.dma("pool", X1[r0:r0 + 128, cg * CWD:(cg + 1) * CWD], xp, r=[xpd], w=[x1dep])
        P.barrier()
    if "stopD" in dbg:
        P.finish()
        P.emit()
        es.close()
        return nc

    bump[0] = base_mark
    if True:
        NH = NTT - 1
        OG = min(4, D // 128)
        act = sb("e_act", (128, FT, TT), BF16)
        h2T = sb("e_h2T", (128, KC, TT), BF16)
        h2Th = sb("e_h2Th", (128, KC, 16), BF16)
        gT = sb("e_gT", (128, KC))
        cw = sb("e_cw", (128, 4, FT))
        AH = sb("e_ah", (128, FT, 16))
        act_d, h2T_d, h2Th_d, ah_d = Dep(), Dep(), Dep(), Dep()
        r_mark = bump[0]
        crow = sb("e_crow", (FT, 4, 128))
        grow = sb("e_grow", (KC, 128))
        sd0 = Dep()
        for k in range(3):
            P.dma("sp", crow[:, k, :], conv_w[k:k + 1, :].rearrange("o (c p) -> (o c) p", p=128), w=[sd0])
        P.dma("sp", crow[:, 3, :], conv_b.rearrange("o (c p) -> (o c) p", p=128), w=[sd0])
        P.dma("sp", grow, g_ffn.rearrange("o (c p) -> (o c) p", p=128), w=[sd0])
        for k in range(4):
            pa, pd = PSD.next()
            P.op("pe", lambda e, pa=pa, k=k: e.transpose(out=pa[:, 0:FT], in_=crow[:, k, :], identity=ident_f[0:FT, 0:FT]),
                 r=[sd0], w=[pd])
            P.op("dve", lambda e, pa=pa, k=k: e.tensor_copy(out=cw[:, k, :], in_=pa[:, 0:FT]), r=[pd], w=[sd0])
        pa, pd = PSD.next()
        P.op("pe", lambda e, pa=pa: e.transpose(out=pa[:, 0:KC], in_=grow, identity=ident_f[0:KC, 0:KC]), r=[sd0], w=[pd])
        P.op("dve", lambda e, pa=pa: e.tensor_copy(out=gT, in_=pa[:, 0:KC]), r=[pd], w=[sd0])
        P.op("dve", lambda e: e.memset(AH, 0.0), w=[ah_d])
        P.barrier()
        bump[0] = r_mark
        xt1 = Ring([sb("e_xt", (128, D))])
        xn1 = Ring([sb("e_xn", (128, D), BF16)])
        ss1 = Ring([sb("e_ss%d" % i, (128, 1)) for i in range(2)])
        if NH > 0:
            X1v = X1.rearrange("(j t) d -> j t d", t=TT)

            def halo_loader(xa, xd):
                P.op("dve", lambda e: e.memset(xa[0:16, :], 0.0), w=[xd])
                P.dma("sp", xa[0:NH, :], X1v[0:NH, TT - 1, :], r=[x1dep], w=[xd])
                P.dma("sp", xa[8:8 + NH, :], X1v[1:NH + 1, 0, :], r=[x1dep], w=[xd])
            norm_transpose_block(halo_loader, None, gT, xt1, xn1, ss1, h2Th, h2Th_d, 0, width=16)
            P.barrier()
        ydep = Dep()
        for tt in range(NTT):
            t0 = tt * TT
            bump[0] = r_mark
            xt1 = Ring([sb("e_xt", (128, D))])
            xn1 = Ring([sb("e_xn", (128, D), BF16)])
            ss1 = Ring([sb("e_ss%d" % i, (128, 1)) for i in range(2)])
            for b in range(TT // 128):
                norm_transpose_block(lambda xa, xd, r0=t0 + b * 128: P.dma("sp", xa, X1[r0:r0 + 128, :], r=[x1dep], w=[xd]),
                                     None, gT, xt1, xn1, ss1, h2T, h2T_d, b * 128)
            P.barrier()
            bump[0] = r_mark
            r_wu = Ring([sb("e_wu%d" % i, (128, KC, 128), BF16) for i in range(4)])
            r_c = Ring([sb("e_c%d" % i, (128, TT)) for i in range(2)])
            r_u = Ring([sb("e_u%d" % i, (128, TT)) for i in range(2)])
            for ct in range(FT):
                wa_, wad = r_wu.next()
                wg_, wgd = r_wu.next()
                P.dma("sp", wa_, WB["upa"][ct], w=[wad])
                P.dma("sp", wg_, WB["upg"][ct], w=[wgd])
                pa, pd = PSD.next()
                mm_group(pa, pd, [wa_[:, kc, :] for kc in range(KC)], [h2T[:, kc, :] for kc in range(KC)], [wad, h2T_d])
                pg_, pgd = PSD.next()
                mm_group(pg_, pgd, [wg_[:, kc, :] for kc in range(KC)], [h2T[:, kc, :] for kc in range(KC)], [wgd, h2T_d])
                if tt == 0 and NH > 0:
                    ph_, phd = PSD.next()
                    mm_group(ph_[:, 0:16], phd, [wa_[:, kc, :] for kc in range(KC)], [h2Th[:, kc, :] for kc in range(KC)],
                             [wad, h2Th_d])
                    P.op("dve", lambda e, ph_=ph_, ct=ct: e.tensor_tensor(out=AH[:, ct, :], in0=ph_[:, 0:16],
                                                                         in1=flags[:, 16:32], op=ALU.mult),
                         r=[phd], w=[ah_d])
                c_, cd = r_c.next()
                u_, ud = r_u.next()
                P.op("act", lambda e, c_=c_, pa=pa, ct=ct: e.activation(out=c_, in_=pa, func=AF.Identity,
                                                                        bias=cw[:, 3, ct:ct + 1], scale=cw[:, 1, ct:ct + 1]),
                     r=[pd], w=[cd])
                P.op("dve", lambda e, c_=c_, pa=pa, ct=ct: e.scalar_tensor_tensor(
                    out=c_[:, 1:TT], in0=pa[:, 0:TT - 1], scalar=cw[:, 0, ct:ct + 1], in1=c_[:, 1:TT],
                    op0=ALU.mult, op1=ALU.add), r=[pd, cd], w=[cd])
                P.op("dve", lambda e, c_=c_, pa=pa, ct=ct: e.scalar_tensor_tensor(
                    out=c_[:, 0:TT - 1], in0=pa[:, 1:TT], scalar=cw[:, 2, ct:ct + 1], in1=c_[:, 0:TT - 1],
                    op0=ALU.mult, op1=ALU.add), r=[pd, cd], w=[cd])
                if tt >= 1:
                    P.op("dve", lambda e, c_=c_, ct=ct, i=tt - 1: e.scalar_tensor_tensor(
                        out=c_[:, 0:1], in0=AH[:, ct, i:i + 1], scalar=cw[:, 0, ct:ct + 1], in1=c_[:, 0:1],
                        op0=ALU.mult, op1=ALU.add), r=[ah_d, cd], w=[cd])
                if tt <= NTT - 2:
                    P.op("dve", lambda e, c_=c_, ct=ct, i=8 + tt: e.scalar_tensor_tensor(
                        out=c_[:, TT - 1:TT], in0=AH[:, ct, i:i + 1], scalar=cw[:, 2, ct:ct + 1], in1=c_[:, TT - 1:TT],
                        op0=ALU.mult, op1=ALU.add), r=[ah_d, cd], w=[cd])
                P.op("dve", lambda e, c_=c_, u_=u_: e.tensor_tensor(out=u_, in0=c_, in1=c_, op=ALU.mult), r=[cd], w=[ud])
                P.op("dve", lambda e, u_=u_: e.tensor_scalar(out=u_, in0=u_, scalar1=0.044715, scalar2=1.0,
                                                             op0=ALU.mult, op1=ALU.add), r=[ud], w=[ud])
                P.op("dve", lambda e, c_=c_, u_=u_: e.tensor_tensor(out=u_, in0=u_, in1=c_, op=ALU.mult), r=[cd, ud], w=[ud])
                P.op("act", lambda e, u_=u_: e.activation(out=u_, in_=u_, func=AF.Sigmoid, scale=1.5957691216057308),
                     r=[ud], w=[ud])
                P.op("dve", lambda e, c_=c_, u_=u_: e.tensor_tensor(out=u_, in0=u_, in1=c_, op=ALU.mult), r=[cd, ud], w=[ud])
                P.op("dve", lambda e, u_=u_, pg_=pg_, ct=ct: e.tensor_tensor(out=act[:, ct, :], in0=u_, in1=pg_, op=ALU.mult),
                     r=[ud, pgd], w=[act_d])
            P.barrier()
            bump[0] = r_mark
            r_wd = Ring([sb("e_wd%d" % i, (128, FT, 128), BF16) for i in range(2)])
            r_yt = Ring([sb("e_yt%d" % i, (128, TT)) for i in range(2)])
            r_yio = Ring([sb("e_yio%d" % i, (128, TT // 128, OG * 128)) for i in range(2)])
            for og in range(D // (OG * 128)):
                yio, yiod = r_yio.next()
                cs = slice(og * OG * 128, (og + 1) * OG * 128)
                P.dma("sp", yio, X1[t0:t0 + TT, cs].rearrange("(b p) c -> p b c", p=128), r=[x1dep], w=[yiod])
                for oi in range(OG):
                    ot = og * OG + oi
                    wd_, wdd = r_wd.next()
                    P.dma("sp", wd_, WB["down"][ot], w=[wdd])
                    py, pyd = PSD.next()
                    mm_group(py, pyd, [wd_[:, kc, :] for kc in range(FT)], [act[:, kc, :] for kc in range(FT)], [wdd, act_d])
                    yt, ytd = r_yt.next()
                    P.op("act", lambda e, yt=yt, py=py: e.copy(out=yt, in_=py), r=[pyd], w=[ytd])
                    pT, pTd = PSD.next()
                    for b in range(TT // 128):
                        P.op("pe", lambda e, pT=pT, yt=yt, b=b: e.transpose(out=pT[:, b * 128:(b + 1) * 128],
                                                                            in_=yt[:, b * 128:(b + 1) * 128],
                                                                            identity=ident_f), r=[ytd], w=[pTd])
                    P.op("dve", lambda e, pT=pT, yio=yio, oi=oi: e.tensor_tensor(
                        out=yio[:, :, oi * 128:(oi + 1) * 128], in0=pT.rearrange("p (b c) -> p b c", c=128),
                        in1=yio[:, :, oi * 128:(oi + 1) * 128], op=ALU.add), r=[pTd, yiod], w=[yiod])
                P.dma("pool", y_out[t0:t0 + TT, cs].rearrange("(b p) c -> p b c", p=128), yio, r=[yiod], w=[ydep])
            P.barrier()
    P.finish()
    P.emit()
    es.close()
    return nc


PHASES = {}


def core_flags_np(cfg, is_sample):
    fl = np.zeros((128, 32), np.float32)
    fl[:, 0] = NEG if is_sample else 0.0
    fl[:, 1] = 0.0 if is_sample else 1.0
    fl[:, 16:32] = 1.0
    ntt = cfg["NT"] // 512
    if is_sample:
        fl[:, 16 + ntt // 2 - 1] = 0.0
        fl[:, 24 + ntt // 2 - 1] = 0.0
    return fl


_NC_CACHE = {}


def kernel(x_prompt, x_sample, rel_bias, g_mix, w_in, q_norm_g, k_norm_g, lambda_q1, lambda_k1, lambda_q2, lambda_k2,
           da_subln_g, w_gate_fwd, b_gate_fwd, w_gate_bwd, b_gate_bwd, gla_norm_g, w_branch_a, w_branch_b, w_out,
           g_ffn, w_up, conv_w, conv_b, w_down):
    cfg = full_cfg()
    f = lambda a: np.ascontiguousarray(np.asarray(a, dtype=np.float32))
    shared = dict(
        rel_bias=f(rel_bias), g_mix=f(g_mix[0:1]), w_in=f(w_in[0]), q_norm_g=f(q_norm_g[0:1]), k_norm_g=f(k_norm_g[0:1]),
        lam4=f(np.concatenate([lambda_q1[0], lambda_k1[0], lambda_q2[0], lambda_k2[0]])[None, :]),
        da_subln_g=f(da_subln_g[0:1]), w_gate_fwd=f(w_gate_fwd[0]), b_gate_fwd=f(b_gate_fwd[0:1]),
        w_gate_bwd=f(w_gate_bwd[0]), b_gate_bwd=f(b_gate_bwd[0:1]), gla_norm_g=f(gla_norm_g[0:1]),
        w_branch_a=f(w_branch_a[0]), w_branch_b=f(w_branch_b[0]), w_out=f(w_out[0]), g_ffn=f(g_ffn[0:1]),
        w_up=f(w_up[0]), conv_w=f(conv_w[0]), conv_b=f(conv_b[0:1]), w_down=f(w_down[0]))
    shared.update(host_consts())
    xp = np.asarray(x_prompt, dtype=np.float32)
    xs = np.asarray(x_sample, dtype=np.float32)
    NT, D = cfg["NT"], cfg["D"]
    in_maps = []
    for c in range(8):
        m = dict(shared)
        if c < 4:
            m["x"] = np.ascontiguousarray(xp[c])
        else:
            m["x"] = np.ascontiguousarray(xs[2 * (c - 4):2 * (c - 4) + 2].reshape(NT, D))
        m["core_flags"] = core_flags_np(cfg, c >= 4)
        in_maps.append(m)
    if "nc" not in _NC_CACHE:
        _NC_CACHE["nc"] = build(cfg)
    res = run_bass_kernel_spmd(_NC_CACHE["nc"], in_maps, core_ids=list(range(8)))
    yp = np.stack([np.asarray(res.results[c]["y"], dtype=np.float32) for c in range(4)])
    ys = np.concatenate([np.asarray(res.results[c]["y"], dtype=np.float32).reshape(2, NT // 2, D) for c in range(4, 8)])
    return (yp, ys)
```

```python
import math
from contextlib import ExitStack
import numpy as np
import concourse.bass as bass
import concourse.mybir as mybir
from concourse.bass_utils import run_bass_kernel_spmd

F32 = mybir.dt.float32
BF16 = mybir.dt.bfloat16
AF = mybir.ActivationFunctionType
ALU = mybir.AluOpType
EPS = 1e-6
NEG = -30000.0
SHIFT = -8.0


class Dep:
    __slots__ = ("w", "r")

    def __init__(self):
        self.w = []
        self.r = []


class Op:
    __slots__ = ("eng", "fn", "deps", "signal", "isdma", "sigidx", "slot", "target")


class Prog:
    ENG = ("pe", "act", "dve", "pool", "sp")

    def __init__(self, nc, nslots=14):
        self.nc = nc
        self.ops = {e: [] for e in self.ENG}
        self.all = []
        self.pending = {e: set() for e in self.ENG}
        self.K = nslots
        self.last_compute = {}
        self.dma_since = []

    def _dep(self, op, x, raw):
        if x is op:
            return
        if x.eng == op.eng and not x.isdma and not op.isdma:
            if op.eng == "pe" or not raw:
                return
        op.deps.add(x)
        x.signal = True

    def _add(self, eng, fn, r, w, isdma):
        op = Op()
        op.eng, op.fn, op.isdma, op.signal, op.deps = eng, fn, isdma, isdma, set()
        op.sigidx = 0
        for d in r:
            for x in d.w:
                self._dep(op, x, True)
        for d in w:
            for x in d.w:
                self._dep(op, x, False)
            for x in d.r:
                self._dep(op, x, False)
        for x in self.pending[eng]:
            if x is not op:
                op.deps.add(x)
                x.signal = True
        self.pending[eng] = set()
        for d in r:
            if (not isdma) and d.r and d.r[-1].eng == eng and not d.r[-1].isdma:
                d.r[-1] = op
            else:
                d.r.append(op)
        for d in w:
            if d.r:
                d.w = [op]
                d.r = []
            elif (not isdma) and d.w and d.w[-1].eng == eng and not d.w[-1].isdma:
                d.w[-1] = op
            else:
                d.w.append(op)
        self.ops[eng].append(op)
        self.all.append(op)
        if isdma:
            self.dma_since.append(op)
        else:
            self.last_compute[eng] = op
        return op

    def op(self, eng, fn, r=(), w=()):
        return self._add(eng, fn, r, w, False)

    def dma(self, q, out, in_, r=(), w=(), **kw):
        return self._add(q, lambda e: e.dma_start(out=out, in_=in_, **kw), r, w, True)

    def barrier(self):
        last = set(self.last_compute.values()) | set(self.dma_since)
        self.dma_since = []
        for e in self.ENG:
            self.pending[e] |= last

    def finish(self):
        self.barrier()
        self._add("sp", None, (), (), False)

    def emit(self):
        nc = self.nc
        cnt = {e: 0 for e in self.ENG}
        dcount = {e: 0 for e in self.ENG}
        slot_last = {}
        for op in self.all:
            if op.isdma:
                s = dcount[op.eng] % self.K
                dcount[op.eng] += 1
                op.slot = (op.eng, s)
                prev = slot_last.get(op.slot)
                op.target = (prev.target if prev else 0) + 16
                if prev is not None:
                    op.deps.add(prev)
                slot_last[op.slot] = op
            elif op.signal:
                cnt[op.eng] += 1
                op.sigidx = cnt[op.eng]
        with ExitStack() as st:
            esem = {e: st.enter_context(nc.semaphore("s_" + e)) for e in self.ENG}
            dsem = {}
            for e in self.ENG:
                for s in range(min(self.K, dcount[e])):
                    dsem[(e, s)] = st.enter_context(nc.semaphore("d_%s%d" % (e, s)))
            block = st.enter_context(nc.Block())

            def replay(e, engine):
                waited = {}
                for op in self.ops[e]:
                    need = {}
                    for x in op.deps:
                        if x.isdma:
                            key, val, sem = ("d",) + x.slot, x.target, dsem[x.slot]
                        else:
                            key, val, sem = ("e", x.eng), x.sigidx, esem[x.eng]
                        if val > need.get(key, (0, None))[0]:
                            need[key] = (val, sem)
                    for key, (val, sem) in need.items():
                        if waited.get(key, 0) < val:
                            engine.wait_ge(sem, val)
                            waited[key] = val
                    if op.fn is None:
                        continue
                    ins = op.fn(engine)
                    if op.isdma:
                        ins.then_inc(dsem[op.slot], 16)
                    elif op.signal:
                        ins.then_inc(esem[e], 1)

            block.sync(lambda eng: replay("sp", eng))
            block.tensor(lambda eng: replay("pe", eng))
            block.scalar(lambda eng: replay("act", eng))
            block.vector(lambda eng: replay("dve", eng))
            block.gpsimd(lambda eng: replay("pool", eng))


class Ring:
    def __init__(self, aps):
        self.aps = aps
        self.deps = [Dep() for _ in aps]
        self.i = 0

    def next(self):
        k = self.i % len(self.aps)
        self.i += 1
        return self.aps[k], self.deps[k]


def full_cfg():
    return dict(D=4096, NT=4096, HA=8, HG=4, DFF=11008)


def t5_bucket_np(rel):
    half, max_exact = 16, 8
    ret = np.where(rel > 0, half, 0)
    n = np.abs(rel)
    nf = np.maximum(n, 1).astype(np.float32)
    large = max_exact + (np.log(nf / np.float32(max_exact)) / np.float32(math.log(128 / max_exact))
                         * np.float32(half - max_exact)).astype(np.int32)
    large = np.minimum(large, half - 1)
    return ret + np.where(n < max_exact, n, large)


def host_consts():
    c = {}
    c["c_ident"] = np.eye(128, dtype=np.float32)
    c["c_flip"] = np.eye(128, dtype=np.float32)[::-1].copy()
    s = np.arange(128)[:, None]
    t = np.arange(128)[None, :]
    same = (s // 64) == (t // 64)
    g = np.zeros((128, 6, 128), np.float32)
    g[:, 0] = (same & (s <= t)) * (-1.0 / 16)
    g[:, 1] = (same & (s > t)) * (-1.0 / 16)
    g[:, 2] = (same & (s >= t)) * (-1.0 / 16)
    g[:, 3] = (same & (s < t)) * (-1.0 / 16)
    g[:, 4] = (same & (s <= t)) * 1.0
    g[:, 5] = (same & (s >= t)) * 1.0
    c["c_gla"] = g
    oh = np.zeros((32, 6, 640), np.float32)
    for di, dl in enumerate(range(-1, 5)):
        i = np.arange(640)
        rel = 128 * dl + 127 - i
        b = t5_bucket_np(rel.astype(np.int32))
        oh[b, di, i] = 1.0
    c["c_onehot"] = oh.reshape(32, 6 * 640)
    return c


def in_groups(cfg):
    HA, HG, D = cfg["HA"], cfg["HG"], cfg["D"]
    QK, GK_, GV_ = HA * 256, HG * 256, HG * 512
    o = 0
    g = []
    for name, n, kind in (("q", QK, "fm_qk"), ("k", QK, "fm_qk"), ("v", QK, "tm"), ("gq", GK_, "fm_plain"),
                          ("gk", GK_, "fm_plain"), ("gv", GV_, "tm"), ("gr", GV_, "tm_silu"),
                          ("glr", 32, "fm_glr"), ("ga", D, "fm_sig"), ("gb", D, "fm_sig")):
        g.append((name, o, n, min(512, n), kind))
        o += n
    return g, o


def build(cfg, dbg=()):
    D, NT, HA, HG, DFF = cfg["D"], cfg["NT"], cfg["HA"], cfg["HG"], cfg["DFF"]
    KC = D // 128
    NB = NT // 128
    TT = 512
    NTT = NT // TT
    NQC = NT // 512
    NKT = NT // 128
    QK, GKW, GVW = HA * 256, HG * 256, HG * 512
    KCB = QK // 128
    KCG = GVW // 128
    FT = DFF // 128
    groups, DIN = in_groups(cfg)
    nc = bass.Bass("TRN2", target_bir_lowering=False)

    def din(name, shape):
        return nc.dram_tensor(name, list(shape), F32, kind="ExternalInput").ap()

    def dscr(name, shape, dt=BF16):
        kind = "ExternalOutput" if name in dbg else "Internal"
        return nc.dram_tensor(name, list(shape), dt, kind=kind).ap()

    x_in = din("x", (NT, D))
    y_out = nc.dram_tensor("y", [NT, D], F32, kind="ExternalOutput").ap()
    rel_bias = din("rel_bias", (32, 8))
    g_mix = din("g_mix", (1, D))
    w_in = din("w_in", (D, DIN))
    q_norm_g = din("q_norm_g", (1, 128))
    k_norm_g = din("k_norm_g", (1, 128))
    lam_in = din("lam4", (1, 512))
    da_subln_g = din("da_subln_g", (1, 256))
    w_gate_f = din("w_gate_fwd", (16, GKW))
    b_gate_f = din("b_gate_fwd", (1, GKW))
    w_gate_b = din("w_gate_bwd", (16, GKW))
    b_gate_b = din("b_gate_bwd", (1, GKW))
    gla_norm_g = din("gla_norm_g", (1, 512))
    w_br_a = din("w_branch_a", (QK, D))
    w_br_b = din("w_branch_b", (GVW, D))
    w_out = din("w_out", (D, D))
    g_ffn = din("g_ffn", (1, D))
    w_up = din("w_up", (D, 2 * DFF))
    conv_w = din("conv_w", (3, DFF))
    conv_b = din("conv_b", (1, DFF))
    w_down = din("w_down", (DFF, D))
    c_ident = din("c_ident", (128, 128))
    c_flip = din("c_flip", (128, 128))
    c_gla = din("c_gla", (128, 6, 128))
    c_onehot = din("c_onehot", (32, 6 * 640))
    core_flags = din("core_flags", (128, 32))

    WB = {}
    for (name, c0, n, CW, kind) in groups:
        WB[name] = dscr("wb_" + name, (n // CW, 128, KC, CW))
    WB["bra"] = dscr("wb_bra", (D // 512 if D >= 512 else 1, 128, KCB, min(512, D)))
    WB["brb"] = dscr("wb_brb", (D // 512 if D >= 512 else 1, 128, KCG, min(512, D)))
    WB["out"] = dscr("wb_out", (D // 512 if D >= 512 else 1, 128, KC, min(512, D)))
    WB["upa"] = dscr("wb_upa", (FT, 128, KC, 128))
    WB["upg"] = dscr("wb_upg", (FT, 128, KC, 128))
    WB["down"] = dscr("wb_down", (D // 128, 128, FT, 128))
    QT = dscr("s_qt", (2 * HA, 128, NT))
    KT = dscr("s_kt", (2 * HA, 128, NT))
    VV = dscr("s_v", (NT, QK))
    GQT = dscr("s_gqt", (2 * HG, 128, NT))
    GKT = dscr("s_gkt", (2 * HG, 128, NT))
    GKM = dscr("s_gk", (NT, GKW))
    GVM = dscr("s_gv", (NT, GVW))
    GRM = dscr("s_gr", (NT, GVW))
    GLT = dscr("s_glt", (32, NT), F32)
    SGA = dscr("s_sga", (KC, 128, NT))
    SGB = dscr("s_sgb", (KC, 128, NT))
    OAT = dscr("s_oat", (KCB, 128, NT))
    OBT = dscr("s_obt", (KCG, 128, NT))
    OFW = dscr("s_of", (NT, GVW), F32)
    X1 = dscr("s_x1", (NT, D), F32)
    UBI = dscr("s_ubias", (8, 6 * 640), F32)

    P = Prog(nc)
    es = ExitStack()

    SB_BYTES = 207 * 1024
    BIG = es.enter_context(nc.sbuf_tensor("big", [128, SB_BYTES // 2], BF16))
    bump = [0]

    def sb(name, shape, dt=F32):
        esz = 4 if dt == F32 else 2
        n = 1
        for v in shape[1:]:
            n *= v
        nb = (n * esz + 63) // 64 * 64
        off = bump[0]
        bump[0] += nb
        assert bump[0] <= SB_BYTES, ("SBUF overflow", name, bump[0])
        v = BIG[0:shape[0], off // 2: off // 2 + n * esz // 2]
        if dt == F32:
            v = v.bitcast(F32)
        if len(shape) == 3:
            v = v.rearrange("p (a b) -> p a b", b=shape[2])
        return v

    banks = [es.enter_context(nc.psum_tensor("ps%d" % i, [128, 512], F32)) for i in range(8)]
    PS = Ring([b[:] for b in banks])

    ident_f = sb("ident_f", (128, 128))
    ident_b = sb("ident_b", (128, 128), BF16)
    flip_f = sb("flip_f", (128, 128))
    ones_b = sb("ones_b", (128, 128), BF16)
    ones_f = sb("ones_f", (128, 128))
    gla_f = sb("gla_f", (128, 6, 128))
    gla_b = sb("gla_b", (128, 4, 128), BF16)
    flags = sb("flags", (128, 32))
    eps_col = sb("eps_col", (128, 1))
    gq_col = sb("gq_col", (128, 2))
    cdep = Dep()
    P.dma("sp", ident_f, c_ident, w=[cdep])
    P.dma("sp", flip_f, c_flip, w=[cdep])
    P.dma("sp", gla_f, c_gla, w=[cdep])
    P.dma("sp", flags, core_flags, w=[cdep])
    P.dma("sp", gq_col[:, 0:1], q_norm_g.rearrange("o d -> d o"), w=[cdep])
    P.dma("sp", gq_col[:, 1:2], k_norm_g.rearrange("o d -> d o"), w=[cdep])
    P.op("dve", lambda e: e.tensor_copy(out=ident_b, in_=ident_f), r=[cdep], w=[cdep])
    P.op("dve", lambda e: e.memset(ones_b, 1.0), w=[cdep])
    P.op("dve", lambda e: e.memset(ones_f, 1.0), w=[cdep])
    P.op("dve", lambda e: e.memset(eps_col, EPS), w=[cdep])
    P.op("dve", lambda e: e.tensor_copy(out=gla_b, in_=gla_f[:, 0:4, :]), r=[cdep], w=[cdep])
    P.op("dve", lambda e: e.tensor_scalar(out=gq_col[:, 0:1], in0=gq_col[:, 0:1], scalar1=128.0 ** -0.5,
                                          scalar2=None, op0=ALU.mult), r=[cdep], w=[cdep])
    P.barrier()

    UE = 2048
    base_mark0 = bump[0]
    p0_s32 = Ring([sb("p0a%d" % i, (128, UE)) for i in range(3)])
    p0_s16 = Ring([sb("p0b%d" % i, (128, UE), BF16) for i in range(3)])
    base_mark = bump[0]
    WBD = {}
    win_units = []
    rest_units = []

    def plan_weight(src2d, K, c0, ncols, CW, dst, name, out_lists):
        kcn = K // 128
        nk = max(1, min(kcn, UE // CW))
        for cg in range(ncols // CW):
            dep = WBD.setdefault((name, cg), Dep())
            lst = []
            for k0 in range(0, kcn, nk):
                n = min(nk, kcn - k0)
                src = src2d[k0 * 128:(k0 + n) * 128, c0 + cg * CW: c0 + (cg + 1) * CW].rearrange(
                    "(k p) c -> p k c", p=128)
                lst.append((src, dst[cg, :, k0:k0 + n, :], n, CW, dep))
            out_lists.append(lst)

    for (name, c0, n, CW, kind) in groups:
        plan_weight(w_in, D, c0, n, CW, WB[name], name, win_units)
    _tmp = []
    cwd = min(512, D)
    plan_weight(w_br_a, QK, 0, D, cwd, WB["bra"], "bra", _tmp)
    plan_weight(w_br_b, GVW, 0, D, cwd, WB["brb"], "brb", _tmp)
    plan_weight(w_out, D, 0, D, cwd, WB["out"], "out", _tmp)
    _ua, _ug = [], []
    plan_weight(w_up, D, 0, DFF, 128, WB["upa"], "upa", _ua)
    plan_weight(w_up, D, DFF, DFF, 128, WB["upg"], "upg", _ug)
    for a_, g_ in zip(_ua, _ug):
        _tmp.append(a_)
        _tmp.append(g_)
    plan_weight(w_down, DFF, 0, D, 128, WB["down"], "down", _tmp)
    for lst in _tmp:
        rest_units.extend(lst)
    pump_state = {"i": 0, "rest": 0, "win": 0}

    def emit_unit(u, engs):
        src, dstap, n, CW, dep = u
        a32, d32 = p0_s32.next()
        a16, d16 = p0_s16.next()
        P.dma("sp", a32[:, 0:n * CW].rearrange("p (k c) -> p k c", c=CW), src, w=[d32])
        eng = engs[pump_state["i"] % len(engs)]
        pump_state["i"] += 1
        if eng == "act":
            P.op("act", lambda e, o=a16[:, 0:n * CW], i=a32[:, 0:n * CW]: e.copy(out=o, in_=i), r=[d32], w=[d16])
        else:
            P.op(eng, lambda e, o=a16[:, 0:n * CW], i=a32[:, 0:n * CW]: e.tensor_copy(out=o, in_=i), r=[d32], w=[d16])
        P.dma("pool", dstap, a16[:, 0:n * CW].rearrange("p (k c) -> p k c", c=CW), r=[d16], w=[dep])

    def pump_win(upto, engs=("dve", "act")):
        while pump_state["win"] < min(upto, len(win_units)):
            for u in win_units[pump_state["win"]]:
                emit_unit(u, engs)
            pump_state["win"] += 1

    def pump_rest(k, engs=("dve",)):
        for _ in range(k):
            if pump_state["rest"] >= len(rest_units):
                return
            emit_unit(rest_units[pump_state["rest"]], engs)
            pump_state["rest"] += 1

    def rest_left():
        return len(rest_units) - pump_state["rest"]

    def rstd_from_ss(ss_ap, n, dep):
        P.op("act", lambda e: e.activation(out=ss_ap, in_=ss_ap, func=AF.Ln, bias=eps_col[0:ss_ap.shape[0], :],
                                           scale=1.0 / n), r=[dep], w=[dep])
        P.op("act", lambda e: e.activation(out=ss_ap, in_=ss_ap, func=AF.Exp, scale=-0.5), r=[dep], w=[dep])

    def norm_transpose_block(loader, g_rep, gT, xt_ring, xn_ring, ss_ring, hT, hT_dep, col0, width=128):
        xa, xd = xt_ring.next()
        na, nd = xn_ring.next()
        sa, sd = ss_ring.next()
        loader(xa, xd)
        P.op("act", lambda e: e.activation(out=na[0:width, :], in_=xa[0:width, :], func=AF.Square,
                                           accum_out=sa[0:width, :]), r=[xd], w=[sd, nd])
        rstd_from_ss(sa[0:width, :], D, sd)
        if g_rep is not None:
            P.op("dve", lambda e: e.scalar_tensor_tensor(out=na[0:width, :], in0=xa[0:width, :], scalar=sa[0:width, 0:1],
                                                         in1=g_rep[0:width, :], op0=ALU.mult, op1=ALU.mult),
                 r=[xd, sd], w=[nd])
        else:
            P.op("dve", lambda e: e.tensor_scalar(out=na[0:width, :], in0=xa[0:width, :], scalar1=sa[0:width, 0:1],
                                                  scalar2=None, op0=ALU.mult), r=[xd, sd], w=[nd])
        G = min(4, KC)
        for kg in range(KC // G):
            pa, pd = PS.next()
            pb = pa.bitcast(BF16)
            for i in range(G):
                kc = kg * G + i
                P.op("pe", lambda e, o=pb[:, i * 128:i * 128 + width], i_=na[0:width, kc * 128:(kc + 1) * 128]:
                     e.transpose(out=o, in_=i_, identity=ident_b[0:width, 0:width]), r=[nd], w=[pd])
            if g_rep is not None:
                src = pb[:, 0:G * 128].rearrange("p (g t) -> p g t", t=128)[:, :, 0:width]
                dst = hT[:, kg * G:(kg + 1) * G, col0:col0 + width]
                if kg % 2 == 0:
                    P.op("act", lambda e, o=dst, i_=src: e.copy(out=o, in_=i_), r=[pd], w=[hT_dep])
                else:
                    P.op("dve", lambda e, o=dst, i_=src: e.tensor_copy(out=o, in_=i_), r=[pd], w=[hT_dep])
            else:
                for i in range(G):
                    kc = kg * G + i
                    src = pb[:, i * 128:i * 128 + width]
                    dst = hT[:, kc, col0:col0 + width]
                    if i % 2 == 0:
                        P.op("act", lambda e, o=dst, i_=src, kc=kc: e.activation(out=o, in_=i_, func=AF.Copy,
                                                                                 scale=gT[:, kc:kc + 1]),
                             r=[pd], w=[hT_dep])
                    else:
                        P.op("dve", lambda e, o=dst, i_=src, kc=kc: e.tensor_scalar(out=o, in0=i_, scalar1=gT[:, kc:kc + 1],
                                                                                    scalar2=None, op0=ALU.mult),
                             r=[pd], w=[hT_dep])

    def mm_group(out_ps, pd, lhs_list, rhs_list, r):
        n = len(lhs_list)
        for i in range(n):
            P.op("pe", lambda e, l=lhs_list[i], rr=rhs_list[i], s=(i == 0), t=(i == n - 1):
                 e.matmul(out_ps, lhsT=l, rhs=rr, start=s, stop=t), r=r, w=[pd])

    bump[0] = base_mark
    if True:
        psb = sb
        g_rep = psb("a_grep", (128, D))
        xt_ring = Ring([psb("a_xt%d" % i, (128, D)) for i in range(1)])
        xn_ring = Ring([psb("a_xn%d" % i, (128, D), BF16) for i in range(2)])
        ss_ring = Ring([psb("a_ss%d" % i, (128, 1)) for i in range(2)])
        hT = psb("a_hT", (128, KC, TT), BF16)
        hT_dep = Dep()
        wt_ring = Ring([psb("a_wt%d" % i, (128, KC, 512), BF16) for i in range(2)])
        sq_ring = Ring([psb("a_sq%d" % i, (128, 512), BF16) for i in range(2)])
        rb_ring = Ring([psb("a_rb%d" % i, (128, 512)) for i in range(2)])
        st_ring = Ring([psb("a_st%d" % i, (128, 512), BF16) for i in range(4)])
        st32_ring = Ring([psb("a_st32%d" % i, (32, 512)) for i in range(2)])
        gdep = Dep()
        P.dma("sp", g_rep, g_mix.partition_broadcast(128)[:, 0, :], w=[gdep])
        P.barrier()
        sdep = Dep()
        a_iters = sum(n // CW for (_, _, n, CW, _) in groups)
        a_quota = 0 if NTT <= 1 else -(-(len(rest_units) // 2) // (a_iters * (NTT - 1)))
        a_it = [0]
        pump_win(3)
        for tt in range(NTT):
            t0 = tt * TT
            a_it[0] = 0
            for b in range(TT // 128):
                norm_transpose_block(lambda xa, xd, r0=t0 + b * 128: P.dma("sp", xa, x_in[r0:r0 + 128, :], w=[xd]),
                                     g_rep, None, xt_ring, xn_ring, ss_ring, hT, hT_dep, b * 128)
            for (name, c0, n, CW, kind) in groups:
                for cg in range(n // CW):
                    if tt == 0:
                        pump_win(a_it[0] + 3)
                    else:
                        pump_rest(a_quota)
                    a_it[0] += 1
                    wa, wd = wt_ring.next()
                    wv = wa[:, :, 0:CW]
                    P.dma("sp", wv, WB[name][cg], r=[WBD[(name, cg)]], w=[wd])
                    if kind.startswith("tm"):
                        for b in range(TT // 128):
                            pa, pd = PS.next()
                            mm_group(pa[:, 0:CW], pd, [hT[:, kc, b * 128:(b + 1) * 128] for kc in range(KC)],
                                     [wv[:, kc, :] for kc in range(KC)], [hT_dep, wd])
                            sa, sd = st_ring.next()
                            fn = AF.Silu if kind == "tm_silu" else AF.Copy
                            if kind == "tm_silu":
                                P.op("act", lambda e, o=sa[:, 0:CW], i_=pa[:, 0:CW]:
                                     e.activation(out=o, in_=i_, func=AF.Silu), r=[pd], w=[sd])
                            else:
                                P.op("dve", lambda e, o=sa[:, 0:CW], i_=pa[:, 0:CW]: e.tensor_copy(out=o, in_=i_),
                                     r=[pd], w=[sd])
                            dst = {"v": VV, "gv": GVM, "gr": GRM}[name]
                            P.dma("pool", dst[t0 + b * 128: t0 + (b + 1) * 128, cg * CW:(cg + 1) * CW], sa[:, 0:CW],
                                  r=[sd], w=[sdep])
                        continue
                    for j in range(max(1, CW // 128)):
                        M = min(128, CW)
                        ct = cg * max(1, CW // 128) + j
                        pa, pd = PS.next()
                        mm_group(pa[0:M, :], pd, [wv[:, kc, j * 128:j * 128 + M] for kc in range(KC)],
                                 [hT[:, kc, :] for kc in range(KC)], [hT_dep, wd])
                        if kind == "fm_qk":
                            qa, qd = sq_ring.next()
                            P.op("act", lambda e, o=qa, i_=pa: e.activation(out=o, in_=i_, func=AF.Square),
                                 r=[pd], w=[qd])
                            p2, p2d = PS.next()
                            P.op("pe", lambda e, o=p2, i_=qa: e.matmul(o, lhsT=ones_b, rhs=i_, start=True, stop=True),
                                 r=[qd], w=[p2d])
                            ra, rd = rb_ring.next()
                            P.op("act", lambda e, o=ra, i_=p2: e.activation(out=o, in_=i_, func=AF.Ln, bias=eps_col,
                                                                            scale=1.0 / 128), r=[p2d], w=[rd])
                            P.op("act", lambda e, o=ra: e.activation(out=o, in_=o, func=AF.Exp, scale=-0.5),
                                 r=[rd], w=[rd])
                            sa, sd = st_ring.next()
                            gcol = gq_col[:, 0:1] if name == "q" else gq_col[:, 1:2]
                            P.op("dve", lambda e, o=sa, i_=pa, g=gcol, r_=ra:
                                 e.scalar_tensor_tensor(out=o, in0=i_, scalar=g, in1=r_, op0=ALU.mult, op1=ALU.mult),
                                 r=[pd, rd], w=[sd])
                            dst = QT if name == "q" else KT
                            P.dma("pool", dst[ct, :, t0:t0 + TT], sa, r=[sd], w=[sdep])
                        elif kind == "fm_plain":
                            sa, sd = st_ring.next()
                            sc = 256.0 ** -0.5 if name == "gq" else 1.0
                            P.op("act", lambda e, o=sa, i_=pa, s=sc: e.activation(out=o, in_=i_, func=AF.Copy, scale=s),
                                 r=[pd], w=[sd])
                            dst = GQT if name == "gq" else GKT
                            P.dma("pool", dst[ct, :, t0:t0 + TT], sa, r=[sd], w=[sdep])
                        elif kind == "fm_sig":
                            sa, sd = st_ring.next()
                            P.op("act", lambda e, o=sa, i_=pa: e.activation(out=o, in_=i_, func=AF.Sigmoid),
                                 r=[pd], w=[sd])
                            dst = SGA if name == "ga" else SGB
                            P.dma("pool", dst[ct, :, t0:t0 + TT], sa, r=[sd], w=[sdep])
                        elif kind == "fm_glr":
                            sa, sd = st32_ring.next()
                            P.op("dve", lambda e, o=sa, i_=pa[0:32, :]: e.tensor_copy(out=o, in_=i_), r=[pd], w=[sd])
                            P.dma("pool", GLT[:, t0:t0 + TT], sa, r=[sd], w=[sdep])
            name, c0, n, CW, kind = [g for g in groups if g[0] == "gk"][0]
            for cg in range(n // CW):
                wa, wd = wt_ring.next()
                wv = wa[:, :, 0:CW]
                P.dma("sp", wv, WB[name][cg], r=[WBD[(name, cg)]], w=[wd])
                for b in range(TT // 128):
                    pa, pd = PS.next()
                    mm_group(pa[:, 0:CW], pd, [hT[:, kc, b * 128:(b + 1) * 128] for kc in range(KC)],
                             [wv[:, kc, :] for kc in range(KC)], [hT_dep, wd])
                    sa, sd = st_ring.next()
                    P.op("dve", lambda e, o=sa[:, 0:CW], i_=pa[:, 0:CW]: e.tensor_copy(out=o, in_=i_), r=[pd], w=[sd])
                    P.dma("pool", GKM[t0 + b * 128: t0 + (b + 1) * 128, cg * CW:(cg + 1) * CW], sa[:, 0:CW],
                          r=[sd], w=[sdep])
        P.barrier()
    if "stopA" in dbg:
        P.finish()
        P.emit()
        es.close()
        return nc

    bump[0] = base_mark
    if True:
        PS_O = Ring(PS.aps[0:4])
        PS_S = Ring(PS.aps[4:7])
        PS_T = Ring(PS.aps[7:8])
        lamv = sb("b_lamv", (1, 512))
        lamp = sb("b_lamp", (1, 512))
        lams = sb("b_lams", (1, 4))
        neglam = sb("b_neglam", (128, 1))
        gsub = sb("b_gsub", (128, 256))
        relb = sb("b_relb", (32, 8))
        fbl = sb("b_fbl", (128, 2, 8))
        fb4 = sb("b_fb4", (128, 8, 4))
        BT = sb("b_bt", (128, 8, 512))
        th_ring = Ring([sb("b_th%d" % i, (128, 512)) for i in range(2)])
        _mk = bump[0]
        onehot = sb("b_onehot", (32, 6 * 640))
        ub_sb = sb("b_ubsb", (8, 6 * 640))
        bump[0] = _mk
        q_ring = Ring([sb("b_q%d" % i, (128, 2, NT), BF16) for i in range(2)])
        k_ring = Ring([sb("b_k%d" % i, (128, 2, NT), BF16) for i in range(2)])
        v_ring = Ring([sb("b_v%d" % i, (128, NKT, 257), BF16) for i in range(2)])
        PT = Ring([sb("b_pt%d" % i, (128, 512), BF16) for i in range(3)])
        TMP = Ring([sb("b_tmp%d" % i, (128, 512)) for i in range(2)])
        OA = sb("b_oa", (128, 4, 256))
        oa_dep = [Dep() for _ in range(4)]
        rz_ring = Ring([sb("b_rz%d" % i, (128, 1)) for i in range(4)])
        ss2_ring = Ring([sb("b_ss%d" % i, (128, 1)) for i in range(4)])
        on_ring = Ring([sb("b_on%d" % i, (128, 256), BF16) for i in range(2)])
        ost_ring = Ring([sb("b_ost%d" % i, (128, 2, 128), BF16) for i in range(2)])
        junkb = (sb("b_junk", (128, 256), BF16), Dep())
        sd0 = Dep()
        P.dma("sp", lamv, lam_in, w=[sd0])
        P.dma("sp", gsub, da_subln_g.partition_broadcast(128)[:, 0, :], w=[sd0])
        P.dma("sp", relb, rel_bias, w=[sd0])
        P.dma("sp", onehot, c_onehot, w=[sd0])
        P.dma("sp", fbl[:, 0, :], rel_bias[15:16, :].partition_broadcast(128)[:, 0, :], w=[sd0])
        P.dma("sp", fbl[:, 1, :], rel_bias[31:32, :].partition_broadcast(128)[:, 0, :], w=[sd0])
        P.op("dve", lambda e: e.tensor_tensor(out=lamp[:, 0:128], in0=lamv[:, 0:128], in1=lamv[:, 128:256], op=ALU.mult),
             r=[sd0], w=[sd0])
        P.op("dve", lambda e: e.tensor_tensor(out=lamp[:, 128:256], in0=lamv[:, 256:384], in1=lamv[:, 384:512],
                                              op=ALU.mult), r=[sd0], w=[sd0])
        P.op("dve", lambda e: e.tensor_reduce(out=lams[:, 0:2], in_=lamp[:, 0:256].rearrange("p (a b) -> p a b", b=128),
                                              axis=mybir.AxisListType.X, op=ALU.add), r=[sd0], w=[sd0])
        P.op("act", lambda e: e.activation(out=lams[:, 0:2], in_=lams[:, 0:2], func=AF.Exp), r=[sd0], w=[sd0])
        P.op("dve", lambda e: e.tensor_tensor(out=lams[:, 2:3], in0=lams[:, 1:2], in1=lams[:, 0:1], op=ALU.subtract),
             r=[sd0], w=[sd0])
        P.op("dve", lambda e: e.tensor_scalar(out=lams[:, 2:3], in0=lams[:, 2:3], scalar1=-0.2, scalar2=None,
                                              op0=ALU.add), r=[sd0], w=[sd0])
        pa, pd = PS_T.next()
        P.op("pe", lambda e: e.matmul(pa[:, 0:1], lhsT=ones_f[0:1, :], rhs=lams[:, 2:3], start=True, stop=True),
             r=[sd0], w=[pd])
        P.op("dve", lambda e: e.tensor_copy(out=neglam, in_=pa[:, 0:1]), r=[pd], w=[sd0])
        P.op("dve", lambda e: e.tensor_scalar(out=gsub, in0=gsub, scalar1=0.8, scalar2=None, op0=ALU.mult),
             r=[sd0], w=[sd0])
        for v in range(4):
            P.op("dve", lambda e, v=v: e.tensor_scalar(out=fb4[:, :, v], in0=fbl[:, v % 2, :], scalar1=SHIFT,
                                                       scalar2=(flags[:, 0:1] if v >= 2 else 0.0),
                                                       op0=ALU.add, op1=ALU.add), r=[sd0], w=[sd0])
        for ch in range(8):
            pa, pd = PS_S.next()
            P.op("pe", lambda e, pa=pa, ch=ch: e.matmul(pa[0:8, 0:480], lhsT=relb, rhs=onehot[:, ch * 480:(ch + 1) * 480],
                                                        start=True, stop=True), r=[sd0], w=[pd])
            P.op("dve", lambda e, pa=pa, ch=ch: e.tensor_copy(out=ub_sb[:, ch * 480:(ch + 1) * 480], in_=pa[0:8, 0:480]),
                 r=[pd], w=[sd0])
        P.dma("pool", UBI, ub_sb, r=[sd0], w=[sd0])
        P.barrier()
        for i in range(2):
            P.op("dve", lambda e, i=i: e.memset(v_ring.aps[i][:, :, 256:257], 1.0), w=[v_ring.deps[i]])
        bt_dep = Dep()
        odep = Dep()
        b_quota = -(-(rest_left() // 2) // (HA * NQC))
        for h in range(HA):
            qa, qd = q_ring.next()
            ka, kd = k_ring.next()
            va, vd = v_ring.next()
            P.dma("sp", qa, QT[2 * h:2 * h + 2].rearrange("m p t -> p m t"), w=[qd])
            P.dma("sp", ka, KT[2 * h:2 * h + 2].rearrange("m p t -> p m t"), w=[kd])
            P.dma("sp", va[:, :, 0:256], VV[:, h * 256:(h + 1) * 256].rearrange("(j p) c -> p j c", p=128), w=[vd])
            for di in range(6):
                ta, td = th_ring.next()
                src = bass.AP(tensor=UBI.tensor, offset=h * 6 * 640 + di * 640, ap=[[1, 128], [1, 512]])
                P.dma("sp", ta, src, w=[td])
                pa, pd = PS_T.next()
                P.op("pe", lambda e, pa=pa, ta=ta: e.matmul(pa, lhsT=flip_f, rhs=ta, start=True, stop=True),
                     r=[td], w=[pd])
                P.op("dve", lambda e, pa=pa, di=di: e.tensor_scalar(out=BT[:, di, :], in0=pa, scalar1=SHIFT, scalar2=None,
                                                                    op0=ALU.add), r=[pd], w=[bt_dep])
            P.op("dve", lambda e: e.tensor_scalar(out=BT[:, 6, :], in0=BT[:, 5, :], scalar1=flags[:, 0:1], scalar2=None,
                                                  op0=ALU.add), r=[bt_dep], w=[bt_dep])
            P.op("dve", lambda e: e.tensor_scalar(out=BT[:, 7, :], in0=BT[:, 0, :], scalar1=flags[:, 0:1], scalar2=None,
                                                  op0=ALU.add), r=[bt_dep], w=[bt_dep])
            for c in range(NQC):
                pump_rest(b_quota)
                for m in range(2):
                    qm = qa[:, m, c * 512:(c + 1) * 512]
                    O = [PS_O.next() for _ in range(4)]

                    def st_mm(j):
                        sa, sd = PS_S.next()
                        P.op("pe", lambda e, sa=sa, j=j, ka=ka, m=m, qm=qm: e.matmul(sa, lhsT=ka[:, m, j * 128:(j + 1) * 128], rhs=qm,
                                                                  start=True, stop=True), r=[kd, qd], w=[sd])
                        return sa, sd
                    pend = [st_mm(0)]
                    if NKT > 1:
                        pend.append(st_mm(1))
                    for j in range(NKT):
                        sa, sd = pend.pop(0)
                        if j + 2 < NKT:
                            pend.append(st_mm(j + 2))
                        pt, ptd = PT.next()
                        dl = j - 4 * c
                        if -1 <= dl <= 4:
                            var = dl + 1
                            if j == NKT // 2 and c == NQC // 2 - 1:
                                var = 6
                            if j == NKT // 2 - 1 and c == NQC // 2:
                                var = 7
                            ta, td = TMP.next()
                            P.op("dve", lambda e, ta=ta, sa=sa, var=var: e.tensor_tensor(out=ta, in0=sa, in1=BT[:, var, :],
                                                                                        op=ALU.add),
                                 r=[sd, bt_dep], w=[td])
                            P.op("act", lambda e, pt=pt, ta=ta: e.activation(out=pt, in_=ta, func=AF.Exp),
                                 r=[td], w=[ptd])
                        else:
                            idx = (0 if dl < 0 else 1) + (2 if ((j < NKT // 2) != (c < NQC // 2)) else 0)
                            P.op("act", lambda e, pt=pt, sa=sa, idx=idx, h=h: e.activation(out=pt, in_=sa, func=AF.Exp,
                                                                                     bias=fb4[:, h, idx:idx + 1]),
                                 r=[sd], w=[ptd])
                        for qb in range(4):
                            P.op("pe", lambda e, qb=qb, pt=pt, j=j, O=O, va=va: e.matmul(O[qb][0][:, 0:257],
                                                                             lhsT=pt[:, qb * 128:(qb + 1) * 128],
                                                                             rhs=va[:, j, :], start=(j == 0),
                                                                             stop=(j == NKT - 1)),
                                 r=[ptd, vd], w=[O[qb][1]])
                    for qb in range(4):
                        oa_, od_ = O[qb]
                        rz, rzd = rz_ring.next()
                        P.op("dve", lambda e, rz=rz, oa_=oa_: e.reciprocal(out=rz, in_=oa_[:, 256:257]), r=[od_], w=[rzd])
                        if m == 0:
                            P.op("dve", lambda e, rz=rz, oa_=oa_, qb=qb: e.tensor_scalar(out=OA[:, qb, :], in0=oa_[:, 0:256],
                                                                                        scalar1=rz, scalar2=None,
                                                                                        op0=ALU.mult),
                                 r=[od_, rzd], w=[oa_dep[qb]])
                            continue
                        P.op("dve", lambda e, rz=rz: e.tensor_scalar(out=rz, in0=rz, scalar1=neglam, scalar2=None,
                                                                     op0=ALU.mult), r=[rzd], w=[rzd])
                        P.op("dve", lambda e, rz=rz, oa_=oa_, qb=qb: e.scalar_tensor_tensor(
                            out=OA[:, qb, :], in0=oa_[:, 0:256], scalar=rz, in1=OA[:, qb, :], op0=ALU.mult, op1=ALU.add),
                            r=[od_, rzd, oa_dep[qb]], w=[oa_dep[qb]])
                        ss, ssd = ss2_ring.next()
                        P.op("act", lambda e, ss=ss, qb=qb: e.activation(out=junkb[0], in_=OA[:, qb, :], func=AF.Square,
                                                                         accum_out=ss), r=[oa_dep[qb]], w=[ssd, junkb[1]])
                        rstd_from_ss(ss, 256, ssd)
                        on, ond = on_ring.next()
                        P.op("dve", lambda e, on=on, ss=ss, qb=qb: e.scalar_tensor_tensor(
                            out=on, in0=OA[:, qb, :], scalar=ss, in1=gsub, op0=ALU.mult, op1=ALU.mult),
                            r=[oa_dep[qb], ssd], w=[ond])
                        pa, pd = PS_T.next()
                        pb = pa.bitcast(BF16)
                        for i in range(2):
                            P.op("pe", lambda e, i=i, pb=pb, on=on: e.transpose(out=pb[:, i * 128:(i + 1) * 128],
                                                                                in_=on[:, i * 128:(i + 1) * 128],
                                                                                identity=ident_b), r=[ond], w=[pd])
                        osb, osd = ost_ring.next()
                        P.op("act", lambda e, osb=osb, pb=pb: e.copy(out=osb, in_=pb[:, 0:256].rearrange(
                            "p (a b) -> p a b", b=128)), r=[pd], w=[osd])
                        t0 = c * 512 + qb * 128
                        P.dma("pool", OAT[2 * h:2 * h + 2, :, t0:t0 + 128].rearrange("k p t -> p k t"), osb,
                              r=[osd], w=[odep])
        P.barrier()
    if "stopB" in dbg:
        P.finish()
        P.emit()
        es.close()
        return nc

    bump[0] = base_mark
    if True:
        G2 = 2 * HG
        NHB = max(1, GKW // 512)
        HW_ = min(512, GKW)
        PS_G = Ring(PS.aps[0:2])
        PS_PO = Ring(PS.aps[2:6])
        PS_KV = Ring(PS.aps[6:8])
        wgp = [sb("c_wgf", (32, GKW)), sb("c_wgb", (32, GKW))]
        bgp = [sb("c_bgf", (1, GKW)), sb("c_bgb", (1, GKW))]
        ggr = sb("c_ggr", (128, 512))
        S = sb("c_S", (128, G2, 512))
        Sb = sb("c_Sb", (128, G2, 512), BF16)
        s_dep = [Dep() for _ in range(G2)]
        sb_dep = [Dep() for _ in range(G2)]

        def ring2(name, shape, dt=F32, n=2):
            return Ring([sb("%s%d" % (name, i), shape, dt) for i in range(n)])
        r_glt = ring2("c_glt", (32, 128))
        r_gqt = ring2("c_gqt", (128, G2, 128), BF16)
        r_gkt = ring2("c_gkt", (128, G2, 128), BF16)
        r_gk = ring2("c_gk", (128, GKW), BF16)
        r_gv = ring2("c_gv", (128, GVW), BF16)
        r_gr = ring2("c_gr", (128, GVW), BF16)
        r_of = ring2("c_of", (128, GVW))
        r_ls = ring2("c_ls", (128, GKW), n=1)
        r_lb = ring2("c_lb", (128, GKW), BF16)
        r_ep = ring2("c_ep", (128, G2, 128))
        r_em = ring2("c_em", (128, G2, 128))
        r_qd = ring2("c_qd", (128, G2, 128), BF16)
        r_ki = ring2("c_ki", (128, G2, 128), BF16)
        r_ed = ring2("c_ed", (128, GKW), n=1)
        r_kd = ring2("c_kd", (128, GKW), BF16)
        r_at = ring2("c_at", (128, HG, 128), BF16)
        r_ofs = ring2("c_ofs", (128, GVW), n=1)
        r_os = ring2("c_os", (128, 512))
        r_gg = ring2("c_gg", (128, 512))
        r_ob = ring2("c_ob", (128, 512), BF16)
        r_obst = ring2("c_obst", (128, 4, 128), BF16)
        r_ss = ring2("c_ss", (128, 1), n=4)
        junkc = (sb("c_junk", (128, 512), BF16), Dep())
        sd0 = Dep()
        for t in wgp:
            P.op("dve", lambda e, t=t: e.memset(t, 0.0), w=[sd0])
        P.dma("sp", wgp[0][0:16, :], w_gate_f, r=[sd0], w=[sd0])
        P.dma("sp", wgp[1][16:32, :], w_gate_b, r=[sd0], w=[sd0])
        P.dma("sp", bgp[0], b_gate_f, w=[sd0])
        P.dma("sp", bgp[1], b_gate_b, w=[sd0])
        P.dma("sp", ggr, gla_norm_g.partition_broadcast(128)[:, 0, :], w=[sd0])
        P.barrier()
        ofdep = Dep()
        obdep = Dep()
        NCH = NT // 64
        for di in range(2):
            fwd = di == 0
            for g in range(G2):
                P.op("dve", lambda e, g=g: e.memset(S[:, g, :], 0.0), w=[s_dep[g]])
                P.op("dve", lambda e, g=g: e.memset(Sb[:, g, :], 0.0), w=[sb_dep[g]])
            blocks = range(NB) if fwd else range(NB - 1, -1, -1)
            c_quota = -(-rest_left() // ((2 - di) * NB))
            for blk in blocks:
                pump_rest(c_quota, engs=("pool",))
                t0 = blk * 128
                glt, gltd = r_glt.next()
                gqt, gqtd = r_gqt.next()
                gkt, gktd = r_gkt.next()
                gk, gkd = r_gk.next()
                gv, gvd = r_gv.next()
                P.dma("sp", glt, GLT[:, t0:t0 + 128], w=[gltd])
                P.dma("sp", gqt, GQT[:, :, t0:t0 + 128].rearrange("k p t -> p k t"), w=[gqtd])
                P.dma("sp", gkt, GKT[:, :, t0:t0 + 128].rearrange("k p t -> p k t"), w=[gktd])
                P.dma("sp", gk, GKM[t0:t0 + 128, :], w=[gkd])
                P.dma("sp", gv, GVM[t0:t0 + 128, :], w=[gvd])
                if not fwd:
                    of, ofd = r_of.next()
                    gr, grd = r_gr.next()
                    P.dma("sp", of, OFW[t0:t0 + 128, :], r=[ofdep], w=[ofd])
                    P.dma("sp", gr, GRM[t0:t0 + 128, :], w=[grd])
                ls, lsd = r_ls.next()
                lb, lbd = r_lb.next()
                for hb in range(NHB):
                    pa, pd = PS_G.next()
                    cs = slice(hb * HW_, (hb + 1) * HW_)
                    P.op("pe", lambda e, pa=pa, cs=cs, glt=glt, di=di: e.matmul(pa[:, 0:HW_], lhsT=glt, rhs=wgp[di][:, cs],
                                                                         start=True, stop=False), r=[gltd], w=[pd])
                    P.op("pe", lambda e, pa=pa, cs=cs, di=di: e.matmul(pa[:, 0:HW_], lhsT=ones_f[0:1, :], rhs=bgp[di][:, cs],
                                                                start=False, stop=True), w=[pd])
                    P.op("act", lambda e, pa=pa, cs=cs, ls=ls: e.activation(out=ls[:, cs], in_=pa[:, 0:HW_], func=AF.Exp,
                                                                            scale=-1.0), r=[pd], w=[lsd])
                P.op("act", lambda e, ls=ls, lb=lb: e.activation(out=lb, in_=ls, func=AF.Ln, bias=ones_f[:, 0:1]),
                     r=[lsd], w=[lbd])
                ep, epd = r_ep.next()
                em, emd = r_em.next()
                qd_, qdd = r_qd.next()
                ki, kid = r_ki.next()
                tri = 0 if fwd else 2
                for g0 in range(0, G2, 4):
                    ng = min(4, G2 - g0)
                    pa, pd = PS_G.next()
                    for i in range(ng):
                        P.op("pe", lambda e, pa=pa, i=i, g=g0 + i, lb=lb, tri=tri: e.matmul(
                            pa[:, i * 128:(i + 1) * 128], lhsT=lb[:, g * 128:(g + 1) * 128], rhs=gla_b[:, tri, :],
                            start=True, stop=True), r=[lbd], w=[pd])
                    pv = pa[:, 0:ng * 128].rearrange("p (a b) -> p a b", b=128)
                    P.op("act", lambda e, pv=pv, ep=ep, g0=g0, ng=ng: e.activation(out=ep[:, g0:g0 + ng, :], in_=pv,
                                                                                   func=AF.Exp), r=[pd], w=[epd])
                    P.op("act", lambda e, pv=pv, em=em, g0=g0, ng=ng: e.activation(out=em[:, g0:g0 + ng, :], in_=pv,
                                                                                   func=AF.Exp, scale=-1.0),
                         r=[pd], w=[emd])
                P.op("dve", lambda e, qd_=qd_, gqt=gqt, ep=ep: e.tensor_tensor(out=qd_, in0=gqt, in1=ep, op=ALU.mult),
                     r=[gqtd, epd], w=[qdd])
                P.op("dve", lambda e, ki=ki, gkt=gkt, em=em: e.tensor_tensor(out=ki, in0=gkt, in1=em, op=ALU.mult),
                     r=[gktd, emd], w=[kid])
                ed, edd = r_ed.next()
                kd_, kdd = r_kd.next()
                ut = 1 if fwd else 3
                for hb in range(NHB):
                    pa, pd = PS_G.next()
                    cs = slice(hb * HW_, (hb + 1) * HW_)
                    P.op("pe", lambda e, pa=pa, cs=cs, lb=lb, ut=ut: e.matmul(pa[:, 0:HW_], lhsT=gla_b[:, ut, :], rhs=lb[:, cs],
                                                                       start=True, stop=True), r=[lbd], w=[pd])
                    P.op("act", lambda e, pa=pa, cs=cs, ed=ed: e.activation(out=ed[:, cs], in_=pa[:, 0:HW_], func=AF.Exp),
                         r=[pd], w=[edd])
                P.op("dve", lambda e, kd_=kd_, gk=gk, ed=ed: e.tensor_tensor(out=kd_, in0=gk, in1=ed, op=ALU.mult),
                     r=[gkd, edd], w=[kdd])
                at, atd = r_at.next()
                pa, pd = PS_G.next()
                for hd in range(HG):
                    for dh in range(2):
                        P.op("pe", lambda e, pa=pa, hd=hd, dh=dh, ki=ki, qd_=qd_: e.matmul(
                            pa[:, hd * 128:(hd + 1) * 128], lhsT=ki[:, hd * 2 + dh, :], rhs=qd_[:, hd * 2 + dh, :],
                            start=(dh == 0), stop=(dh == 1)), r=[kid, qdd], w=[pd])
                mk = 4 if fwd else 5
                for hd in range(HG):
                    P.op("dve", lambda e, pa=pa, hd=hd, at=at, mk=mk: e.tensor_tensor(
                        out=at[:, hd, :], in0=pa[:, hd * 128:(hd + 1) * 128], in1=gla_f[:, mk, :], op=ALU.mult),
                        r=[pd], w=[atd])
                if fwd and blk == 0 and "d_lb" in dbg:
                    P.dma("pool", dscr("d_lb", (128, GKW)), lb, r=[lbd], w=[Dep()])
                    P.dma("pool", dscr("d_ep", (128, G2, 128), F32), ep, r=[epd], w=[Dep()])
                    P.dma("pool", dscr("d_at", (128, HG, 128)), at, r=[atd], w=[Dep()])
                    P.dma("pool", dscr("d_qd", (128, G2, 128)), qd_, r=[qdd], w=[Dep()])
                    P.dma("pool", dscr("d_kd", (128, GKW)), kd_, r=[kdd], w=[Dep()])
                po = [PS_PO.next() for _ in range(HG)]
                for ch in ((0, 1) if fwd else (1, 0)):
                    n = blk * 2 + ch
                    if (fwd and n == NCH // 2) or ((not fwd) and n == NCH // 2 - 1):
                        for g in range(G2):
                            P.op("dve", lambda e, g=g: e.tensor_scalar(out=S[:, g, :], in0=S[:, g, :], scalar1=flags[:, 1:2],
                                                                       scalar2=None, op0=ALU.mult),
                                 r=[s_dep[g]], w=[s_dep[g]])
                            P.op("act", lambda e, g=g: e.copy(out=Sb[:, g, :], in_=S[:, g, :]), r=[s_dep[g]], w=[sb_dep[g]])
                    rows = slice(ch * 64, ch * 64 + 64)
                    dcol = (ch * 64 + 63) if fwd else (ch * 64)
                    for hd in range(HG):
                        pa, pd = po[hd]
                        vs = slice(hd * 512, (hd + 1) * 512)
                        P.op("pe", lambda e, pa=pa, at=at, gv=gv, hd=hd, rows=rows, vs=vs: e.matmul(
                            pa[rows, :], lhsT=at[rows, hd, rows], rhs=gv[rows, vs], start=True, stop=False),
                            r=[atd, gvd], w=[pd])
                        for dh in range(2):
                            g = hd * 2 + dh
                            P.op("pe", lambda e, pa=pa, qd_=qd_, g=g, rows=rows, dh=dh: e.matmul(
                                pa[rows, :], lhsT=qd_[:, g, rows], rhs=Sb[:, g, :], start=False, stop=(dh == 1)),
                                r=[qdd, sb_dep[g]], w=[pd])
                        for dh in range(2):
                            g = hd * 2 + dh
                            ka_, kvd = PS_KV.next()
                            P.op("pe", lambda e, ka_=ka_, kd_=kd_, gv=gv, g=g, rows=rows, vs=vs: e.matmul(
                                ka_, lhsT=kd_[rows, g * 128:(g + 1) * 128], rhs=gv[rows, vs], start=True, stop=True),
                                r=[kdd, gvd], w=[kvd])
                            P.op("dve", lambda e, ka_=ka_, g=g, ep=ep, dcol=dcol: e.scalar_tensor_tensor(
                                out=S[:, g, :], in0=S[:, g, :], scalar=ep[:, g, dcol:dcol + 1], in1=ka_,
                                op0=ALU.mult, op1=ALU.add), r=[kvd, epd, s_dep[g], sb_dep[g]], w=[s_dep[g]])
                            P.op("act", lambda e, g=g: e.copy(out=Sb[:, g, :], in_=S[:, g, :]), r=[s_dep[g]], w=[sb_dep[g]])
                if fwd:
                    ofs, ofsd = r_ofs.next()
                    for hd in range(HG):
                        pa, pd = po[hd]
                        vs = slice(hd * 512, (hd + 1) * 512)
                        if hd % 2 == 0:
                            P.op("act", lambda e, pa=pa, ofs=ofs, vs=vs: e.copy(out=ofs[:, vs], in_=pa), r=[pd], w=[ofsd])
                        else:
                            P.op("dve", lambda e, pa=pa, ofs=ofs, vs=vs: e.tensor_copy(out=ofs[:, vs], in_=pa),
                                 r=[pd], w=[ofsd])
                    P.dma("pool", OFW[t0:t0 + 128, :], ofs, r=[ofsd], w=[ofdep])
                else:
                    for hd in range(HG):
                        pa, pd = po[hd]
                        vs = slice(hd * 512, (hd + 1) * 512)
                        osm, osd = r_os.next()
                        P.op("dve", lambda e, pa=pa, osm=osm, of=of, vs=vs: e.tensor_tensor(out=osm, in0=pa, in1=of[:, vs],
                                                                                          op=ALU.add),
                             r=[pd, ofd], w=[osd])
                        ss, ssd = r_ss.next()
                        P.op("act", lambda e, osm=osm, ss=ss: e.activation(out=junkc[0], in_=osm, func=AF.Square,
                                                                           accum_out=ss), r=[osd], w=[ssd, junkc[1]])
                        rstd_from_ss(ss, 512, ssd)
                        gg, ggd = r_gg.next()
                        P.op("dve", lambda e, gg=gg, gr=gr, vs=vs: e.tensor_tensor(out=gg, in0=gr[:, vs], in1=ggr, op=ALU.mult),
                             r=[grd], w=[ggd])
                        ob, obd = r_ob.next()
                        P.op("dve", lambda e, ob=ob, osm=osm, ss=ss, gg=gg: e.scalar_tensor_tensor(
                            out=ob, in0=osm, scalar=ss, in1=gg, op0=ALU.mult, op1=ALU.mult), r=[osd, ssd, ggd], w=[obd])
                        pt_, ptd_ = PS_G.next()
                        pb = pt_.bitcast(BF16)
                        for i in range(4):
                            P.op("pe", lambda e, pb=pb, ob=ob, i=i: e.transpose(out=pb[:, i * 128:(i + 1) * 128],
                                                                                in_=ob[:, i * 128:(i + 1) * 128],
                                                                                identity=ident_b), r=[obd], w=[ptd_])
                        obst, obsd = r_obst.next()
                        P.op("act", lambda e, pb=pb, obst=obst: e.copy(out=obst, in_=pb[:, 0:512].rearrange(
                            "p (a b) -> p a b", b=128)), r=[ptd_], w=[obsd])
                        P.dma("pool", OBT[hd * 4:(hd + 1) * 4, :, t0:t0 + 128].rearrange("k p t -> p k t"), obst,
                              r=[obsd], w=[obdep])
            P.barrier()
        pump_rest(rest_left())
        P.barrier()
    if "stopC" in dbg:
        P.finish()
        P.emit()
        es.close()
        return nc

    bump[0] = base_mark0
    PSD = Ring(PS.aps)
    if True:
        CWD = min(512, D)
        NJ = CWD // 128
        oat = sb("d_oat", (128, KCB, TT), BF16)
        obt = sb("d_obt", (128, KCG, TT), BF16)
        mT = sb("d_mT", (128, KC, TT), BF16)
        oat_d, obt_d, mT_d = Dep(), Dep(), Dep()
        r_wbr = Ring([sb("d_wbr%d" % i, (128, max(KCB, KCG), CWD), BF16) for i in range(2)])
        r_wo = Ring([sb("d_wo%d" % i, (128, KC, CWD), BF16) for i in range(2)])
        r_sg = Ring([sb("d_sg%d" % i, (128, TT), BF16) for i in range(4)])
        r_t = Ring([sb("d_t%d" % i, (128, TT)) for i in range(4)])
        r_xp = Ring([sb("d_xp%d" % i, (128, CWD)) for i in range(3)])
        x1dep = Dep()
        for tt in range(NTT):
            t0 = tt * TT
            P.dma("sp", oat, OAT[:, :, t0:t0 + TT].rearrange("k p t -> p k t"), w=[oat_d])
            P.dma("sp", obt, OBT[:, :, t0:t0 + TT].rearrange("k p t -> p k t"), w=[obt_d])
            for cg in range(D // CWD):
                wa_, wad = r_wbr.next()
                wb_, wbd = r_wbr.next()
                P.dma("sp", wa_[:, 0:KCB, :], WB["bra"][cg], r=[WBD[("bra", cg)]], w=[wad])
                P.dma("sp", wb_[:, 0:KCG, :], WB["brb"][cg], r=[WBD[("brb", cg)]], w=[wbd])
                for j in range(NJ):
                    ct = cg * NJ + j
                    sga_, sgad = r_sg.next()
                    sgb_, sgbd = r_sg.next()
                    P.dma("sp", sga_, SGA[ct, :, t0:t0 + TT], w=[sgad])
                    P.dma("sp", sgb_, SGB[ct, :, t0:t0 + TT], w=[sgbd])
                    pa, pd = PSD.next()
                    mm_group(pa, pd, [wa_[:, kc, j * 128:(j + 1) * 128] for kc in range(KCB)],
                             [oat[:, kc, :] for kc in range(KCB)], [wad, oat_d])
                    pb_, pbd = PSD.next()
                    mm_group(pb_, pbd, [wb_[:, kc, j * 128:(j + 1) * 128] for kc in range(KCG)],
                             [obt[:, kc, :] for kc in range(KCG)], [wbd, obt_d])
                    t1, t1d = r_t.next()
                    t2, t2d = r_t.next()
                    P.op("dve", lambda e, t1=t1, pa=pa, sga_=sga_: e.tensor_tensor(out=t1, in0=pa, in1=sga_, op=ALU.mult),
                         r=[pd, sgad], w=[t1d])
                    P.op("dve", lambda e, t2=t2, pb_=pb_, sgb_=sgb_: e.tensor_tensor(out=t2, in0=pb_, in1=sgb_, op=ALU.mult),
                         r=[pbd, sgbd], w=[t2d])
                    P.op("pool", lambda e, t1=t1, t2=t2, ct=ct: e.tensor_tensor(out=mT[:, ct, :], in0=t1, in1=t2, op=ALU.add),
                         r=[t1d, t2d], w=[mT_d])
            for cg in range(D // CWD):
                wo_, wod = r_wo.next()
                P.dma("sp", wo_, WB["out"][cg], r=[WBD[("out", cg)]], w=[wod])
                for b in range(TT // 128):
                    r0 = t0 + b * 128
                    xp, xpd = r_xp.next()
                    P.dma("sp", xp, x_in[r0:r0 + 128, cg * CWD:(cg + 1) * CWD], w=[xpd])
                    pa, pd = PSD.next()
                    mm_group(pa[:, 0:CWD], pd, [mT[:, kc, b * 128:(b + 1) * 128] for kc in range(KC)],
                             [wo_[:, kc, :] for kc in range(KC)], [wod, mT_d])
                    P.op("dve", lambda e, xp=xp, pa=pa: e.tensor_tensor(out=xp, in0=pa[:, 0:CWD], in1=xp, op=ALU.add),
                         r=[pd, xpd], w=[xpd])
                    P.dma("pool", X1[r0:r0 + 128, cg * CWD:(cg + 1) * CWD], xp, r=[xpd], w=[x1dep])
        P.barrier()
    if "stopD" in dbg:
        P.finish()
        P.emit()
        es.close()
        return nc

    bump[0] = base_mark0
    if True:
        NH = NTT - 1
        OG = min(4, D // 128)
        act = sb("e_act", (128, FT, TT), BF16)
        h2T = sb("e_h2T", (128, KC, TT), BF16)
        h2Th = sb("e_h2Th", (128, KC, 16), BF16)
        gT = sb("e_gT", (128, KC))
        cw = sb("e_cw", (128, 4, FT))
        AH = sb("e_ah", (128, FT, 16))
        act_d, h2T_d, h2Th_d, ah_d = Dep(), Dep(), Dep(), Dep()
        r_mark = bump[0]
        crow = sb("e_crow", (FT, 4, 128))
        grow = sb("e_grow", (KC, 128))
        sd0 = Dep()
        for k in range(3):
            P.dma("sp", crow[:, k, :], conv_w[k:k + 1, :].rearrange("o (c p) -> (o c) p", p=128), w=[sd0])
        P.dma("sp", crow[:, 3, :], conv_b.rearrange("o (c p) -> (o c) p", p=128), w=[sd0])
        P.dma("sp", grow, g_ffn.rearrange("o (c p) -> (o c) p", p=128), w=[sd0])
        for k in range(4):
            pa, pd = PSD.next()
            P.op("pe", lambda e, pa=pa, k=k: e.transpose(out=pa[:, 0:FT], in_=crow[:, k, :], identity=ident_f[0:FT, 0:FT]),
                 r=[sd0], w=[pd])
            P.op("dve", lambda e, pa=pa, k=k: e.tensor_copy(out=cw[:, k, :], in_=pa[:, 0:FT]), r=[pd], w=[sd0])
        pa, pd = PSD.next()
        P.op("pe", lambda e, pa=pa: e.transpose(out=pa[:, 0:KC], in_=grow, identity=ident_f[0:KC, 0:KC]), r=[sd0], w=[pd])
        P.op("dve", lambda e, pa=pa: e.tensor_copy(out=gT, in_=pa[:, 0:KC]), r=[pd], w=[sd0])
        P.op("dve", lambda e: e.memset(AH, 0.0), w=[ah_d])
        P.barrier()
        bump[0] = r_mark
        xt1 = Ring([sb("e_xt", (128, D))])
        xn1 = Ring([sb("e_xn", (128, D), BF16)])
        ss1 = Ring([sb("e_ss%d" % i, (128, 1)) for i in range(2)])
        if NH > 0:
            X1v = X1.rearrange("(j t) d -> j t d", t=TT)

            def halo_loader(xa, xd):
                P.op("dve", lambda e: e.memset(xa[0:16, :], 0.0), w=[xd])
                P.dma("sp", xa[0:NH, :], X1v[0:NH, TT - 1, :], r=[x1dep], w=[xd])
                P.dma("sp", xa[8:8 + NH, :], X1v[1:NH + 1, 0, :], r=[x1dep], w=[xd])
            norm_transpose_block(halo_loader, None, gT, xt1, xn1, ss1, h2Th, h2Th_d, 0, width=16)
            P.barrier()
        ydep = Dep()
        for tt in range(NTT):
            t0 = tt * TT
            bump[0] = r_mark
            r_wu = Ring([sb("e_wu%d" % i, (128, KC, 128), BF16) for i in range(4)])
            r_c = Ring([sb("e_c%d" % i, (128, TT)) for i in range(2)])
            r_u = Ring([sb("e_u%d" % i, (128, TT)) for i in range(2)])
            xt1 = Ring([sb("e_xt", (128, D))])
            xn1 = Ring([sb("e_xn", (128, D), BF16)])
            ss1 = Ring([sb("e_ss%d" % i, (128, 1)) for i in range(2)])
            for b in range(TT // 128):
                norm_transpose_block(lambda xa, xd, r0=t0 + b * 128: P.dma("sp", xa, X1[r0:r0 + 128, :], r=[x1dep], w=[xd]),
                                     None, gT, xt1, xn1, ss1, h2T, h2T_d, b * 128)
            for ct in range(FT):
                wa_, wad = r_wu.next()
                wg_, wgd = r_wu.next()
                P.dma("sp", wa_, WB["upa"][ct], r=[WBD[("upa", ct)]], w=[wad])
                P.dma("sp", wg_, WB["upg"][ct], r=[WBD[("upg", ct)]], w=[wgd])
                pa, pd = PSD.next()
                mm_group(pa, pd, [wa_[:, kc, :] for kc in range(KC)], [h2T[:, kc, :] for kc in range(KC)], [wad, h2T_d])
                pg_, pgd = PSD.next()
                mm_group(pg_, pgd, [wg_[:, kc, :] for kc in range(KC)], [h2T[:, kc, :] for kc in range(KC)], [wgd, h2T_d])
                if tt == 0 and NH > 0:
                    ph_, phd = PSD.next()
                    mm_group(ph_[:, 0:16], phd, [wa_[:, kc, :] for kc in range(KC)], [h2Th[:, kc, :] for kc in range(KC)],
                             [wad, h2Th_d])
                    P.op("dve", lambda e, ph_=ph_, ct=ct: e.tensor_tensor(out=AH[:, ct, :], in0=ph_[:, 0:16],
                                                                         in1=flags[:, 16:32], op=ALU.mult),
                         r=[phd], w=[ah_d])
                c_, cd = r_c.next()
                u_, ud = r_u.next()
                P.op("act", lambda e, c_=c_, pa=pa, ct=ct: e.activation(out=c_, in_=pa, func=AF.Identity,
                                                                        bias=cw[:, 3, ct:ct + 1], scale=cw[:, 1, ct:ct + 1]),
                     r=[pd], w=[cd])
                P.op("dve", lambda e, c_=c_, pa=pa, ct=ct: e.scalar_tensor_tensor(
                    out=c_[:, 1:TT], in0=pa[:, 0:TT - 1], scalar=cw[:, 0, ct:ct + 1], in1=c_[:, 1:TT],
                    op0=ALU.mult, op1=ALU.add), r=[pd, cd], w=[cd])
                P.op("dve", lambda e, c_=c_, pa=pa, ct=ct: e.scalar_tensor_tensor(
                    out=c_[:, 0:TT - 1], in0=pa[:, 1:TT], scalar=cw[:, 2, ct:ct + 1], in1=c_[:, 0:TT - 1],
                    op0=ALU.mult, op1=ALU.add), r=[pd, cd], w=[cd])
                if tt >= 1:
                    P.op("dve", lambda e, c_=c_, ct=ct, i=tt - 1: e.scalar_tensor_tensor(
                        out=c_[:, 0:1], in0=AH[:, ct, i:i + 1], scalar=cw[:, 0, ct:ct + 1], in1=c_[:, 0:1],
                        op0=ALU.mult, op1=ALU.add), r=[ah_d, cd], w=[cd])
                if tt <= NTT - 2:
                    P.op("dve", lambda e, c_=c_, ct=ct, i=8 + tt: e.scalar_tensor_tensor(
                        out=c_[:, TT - 1:TT], in0=AH[:, ct, i:i + 1], scalar=cw[:, 2, ct:ct + 1], in1=c_[:, TT - 1:TT],
                        op0=ALU.mult, op1=ALU.add), r=[ah_d, cd], w=[cd])
                P.op("dve", lambda e, c_=c_, u_=u_: e.tensor_tensor(out=u_, in0=c_, in1=c_, op=ALU.mult), r=[cd], w=[ud])
                P.op("dve", lambda e, u_=u_: e.tensor_scalar(out=u_, in0=u_, scalar1=0.044715, scalar2=1.0,
                                                             op0=ALU.mult, op1=ALU.add), r=[ud], w=[ud])
                P.op("dve", lambda e, c_=c_, u_=u_: e.tensor_tensor(out=u_, in0=u_, in1=c_, op=ALU.mult), r=[cd, ud], w=[ud])
                P.op("act", lambda e, u_=u_: e.activation(out=u_, in_=u_, func=AF.Sigmoid, scale=1.5957691216057308),
                     r=[ud], w=[ud])
                P.op("dve", lambda e, c_=c_, u_=u_: e.tensor_tensor(out=u_, in0=u_, in1=c_, op=ALU.mult), r=[cd, ud], w=[ud])
                P.op("dve", lambda e, u_=u_, pg_=pg_, ct=ct: e.tensor_tensor(out=act[:, ct, :], in0=u_, in1=pg_, op=ALU.mult),
                     r=[ud, pgd], w=[act_d])
            P.barrier()
            bump[0] = r_mark
            r_wd = Ring([sb("e_wd%d" % i, (128, FT, 128), BF16) for i in range(2)])
            r_yt = Ring([sb("e_yt%d" % i, (128, TT)) for i in range(2)])
            r_yio = Ring([sb("e_yio%d" % i, (128, TT // 128, OG * 128)) for i in range(2)])
            for og in range(D // (OG * 128)):
                yio, yiod = r_yio.next()
                cs = slice(og * OG * 128, (og + 1) * OG * 128)
                P.dma("sp", yio, X1[t0:t0 + TT, cs].rearrange("(b p) c -> p b c", p=128), r=[x1dep], w=[yiod])
                for oi in range(OG):
                    ot = og * OG + oi
                    wd_, wdd = r_wd.next()
                    P.dma("sp", wd_, WB["down"][ot], r=[WBD[("down", ot)]], w=[wdd])
                    py, pyd = PSD.next()
                    mm_group(py, pyd, [wd_[:, kc, :] for kc in range(FT)], [act[:, kc, :] for kc in range(FT)], [wdd, act_d])
                    yt, ytd = r_yt.next()
                    P.op("act", lambda e, yt=yt, py=py: e.copy(out=yt, in_=py), r=[pyd], w=[ytd])
                    pT, pTd = PSD.next()
                    for b in range(TT // 128):
                        P.op("pe", lambda e, pT=pT, yt=yt, b=b: e.transpose(out=pT[:, b * 128:(b + 1) * 128],
                                                                            in_=yt[:, b * 128:(b + 1) * 128],
                                                                            identity=ident_f), r=[ytd], w=[pTd])
                    P.op("dve", lambda e, pT=pT, yio=yio, oi=oi: e.tensor_tensor(
                        out=yio[:, :, oi * 128:(oi + 1) * 128], in0=pT.rearrange("p (b c) -> p b c", c=128),
                        in1=yio[:, :, oi * 128:(oi + 1) * 128], op=ALU.add), r=[pTd, yiod], w=[yiod])
                P.dma("pool", y_out[t0:t0 + TT, cs].rearrange("(b p) c -> p b c", p=128), yio, r=[yiod], w=[ydep])
            P.barrier()
    P.finish()
    P.emit()
    es.close()
    return nc


PHASES = {}


def core_flags_np(cfg, is_sample):
    fl = np.zeros((128, 32), np.float32)
    fl[:, 0] = NEG if is_sample else 0.0
    fl[:, 1] = 0.0 if is_sample else 1.0
    fl[:, 16:32] = 1.0
    ntt = cfg["NT"] // 512
    if is_sample:
        fl[:, 16 + ntt // 2 - 1] = 0.0
        fl[:, 24 + ntt // 2 - 1] = 0.0
    return fl


_NC_CACHE = {}


def kernel(x_prompt, x_sample, rel_bias, g_mix, w_in, q_norm_g, k_norm_g, lambda_q1, lambda_k1, lambda_q2, lambda_k2,
           da_subln_g, w_gate_fwd, b_gate_fwd, w_gate_bwd, b_gate_bwd, gla_norm_g, w_branch_a, w_branch_b, w_out,
           g_ffn, w_up, conv_w, conv_b, w_down):
    cfg = full_cfg()
    f = lambda a: np.ascontiguousarray(np.asarray(a, dtype=np.float32))
    shared = dict(
        rel_bias=f(rel_bias), g_mix=f(g_mix[0:1]), w_in=f(w_in[0]), q_norm_g=f(q_norm_g[0:1]), k_norm_g=f(k_norm_g[0:1]),
        lam4=f(np.concatenate([lambda_q1[0], lambda_k1[0], lambda_q2[0], lambda_k2[0]])[None, :]),
        da_subln_g=f(da_subln_g[0:1]), w_gate_fwd=f(w_gate_fwd[0]), b_gate_fwd=f(b_gate_fwd[0:1]),
        w_gate_bwd=f(w_gate_bwd[0]), b_gate_bwd=f(b_gate_bwd[0:1]), gla_norm_g=f(gla_norm_g[0:1]),
        w_branch_a=f(w_branch_a[0]), w_branch_b=f(w_branch_b[0]), w_out=f(w_out[0]), g_ffn=f(g_ffn[0:1]),
        w_up=f(w_up[0]), conv_w=f(conv_w[0]), conv_b=f(conv_b[0:1]), w_down=f(w_down[0]))
    shared.update(host_consts())
    xp = np.asarray(x_prompt, dtype=np.float32)
    xs = np.asarray(x_sample, dtype=np.float32)
    NT, D = cfg["NT"], cfg["D"]
    in_maps = []
    for c in range(8):
        m = dict(shared)
        if c < 4:
            m["x"] = np.ascontiguousarray(xp[c])
        else:
            m["x"] = np.ascontiguousarray(xs[2 * (c - 4):2 * (c - 4) + 2].reshape(NT, D))
        m["core_flags"] = core_flags_np(cfg, c >= 4)
        in_maps.append(m)
    if "nc" not in _NC_CACHE:
        _NC_CACHE["nc"] = build(cfg)
    res = run_bass_kernel_spmd(_NC_CACHE["nc"], in_maps, core_ids=list(range(8)))
    yp = np.stack([np.asarray(res.results[c]["y"], dtype=np.float32) for c in range(4)])
    ys = np.concatenate([np.asarray(res.results[c]["y"], dtype=np.float32).reshape(2, NT // 2, D) for c in range(4, 8)])
    return (yp, ys)
```

```python
import math
from contextlib import ExitStack
import numpy as np
import concourse.bass as bass
import concourse.mybir as mybir
from concourse.bass_utils import run_bass_kernel_spmd

F32 = mybir.dt.float32
BF16 = mybir.dt.bfloat16
AF = mybir.ActivationFunctionType
ALU = mybir.AluOpType
EPS = 1e-6
NEG = -30000.0
SHIFT = -8.0


class Dep:
    __slots__ = ("w", "r")

    def __init__(self):
        self.w = []
        self.r = []


class Op:
    __slots__ = ("eng", "fn", "deps", "signal", "isdma", "sigidx", "slot", "target")


class Prog:
    ENG = ("pe", "act", "dve", "pool", "sp")

    def __init__(self, nc, nslots=14):
        self.nc = nc
        self.ops = {e: [] for e in self.ENG}
        self.all = []
        self.pending = {e: set() for e in self.ENG}
        self.K = nslots
        self.last_compute = {}
        self.dma_since = []

    def _dep(self, op, x, raw):
        if x is op:
            return
        if x.eng == op.eng and not x.isdma and not op.isdma:
            if op.eng == "pe" or not raw:
                return
        op.deps.add(x)
        x.signal = True

    def _add(self, eng, fn, r, w, isdma):
        op = Op()
        op.eng, op.fn, op.isdma, op.signal, op.deps = eng, fn, isdma, isdma, set()
        op.sigidx = 0
        for d in r:
            for x in d.w:
                self._dep(op, x, True)
        for d in w:
            for x in d.w:
                self._dep(op, x, False)
            for x in d.r:
                self._dep(op, x, False)
        for x in self.pending[eng]:
            if x is not op:
                op.deps.add(x)
                x.signal = True
        self.pending[eng] = set()
        for d in r:
            if (not isdma) and d.r and d.r[-1].eng == eng and not d.r[-1].isdma:
                d.r[-1] = op
            else:
                d.r.append(op)
        for d in w:
            if d.r:
                d.w = [op]
                d.r = []
            elif (not isdma) and d.w and d.w[-1].eng == eng and not d.w[-1].isdma:
                d.w[-1] = op
            else:
                d.w.append(op)
        self.ops[eng].append(op)
        self.all.append(op)
        if isdma:
            self.dma_since.append(op)
        else:
            self.last_compute[eng] = op
        return op

    def op(self, eng, fn, r=(), w=()):
        return self._add(eng, fn, r, w, False)

    def dma(self, q, out, in_, r=(), w=(), **kw):
        return self._add(q, lambda e: e.dma_start(out=out, in_=in_, **kw), r, w, True)

    def barrier(self):
        last = set(self.last_compute.values()) | set(self.dma_since)
        self.dma_since = []
        for e in self.ENG:
            self.pending[e] |= last

    def finish(self):
        self.barrier()
        self._add("sp", None, (), (), False)

    def emit(self):
        nc = self.nc
        cnt = {e: 0 for e in self.ENG}
        dcount = {e: 0 for e in self.ENG}
        slot_last = {}
        for op in self.all:
            if op.isdma:
                s = dcount[op.eng] % self.K
                dcount[op.eng] += 1
                op.slot = (op.eng, s)
                prev = slot_last.get(op.slot)
                op.target = (prev.target if prev else 0) + 16
                if prev is not None:
                    op.deps.add(prev)
                slot_last[op.slot] = op
            elif op.signal:
                cnt[op.eng] += 1
                op.sigidx = cnt[op.eng]
        with ExitStack() as st:
            esem = {e: st.enter_context(nc.semaphore("s_" + e)) for e in self.ENG}
            dsem = {}
            for e in self.ENG:
                for s in range(min(self.K, dcount[e])):
                    dsem[(e, s)] = st.enter_context(nc.semaphore("d_%s%d" % (e, s)))
            block = st.enter_context(nc.Block())

            def replay(e, engine):
                waited = {}
                for op in self.ops[e]:
                    need = {}
                    for x in op.deps:
                        if x.isdma:
                            key, val, sem = ("d",) + x.slot, x.target, dsem[x.slot]
                        else:
                            key, val, sem = ("e", x.eng), x.sigidx, esem[x.eng]
                        if val > need.get(key, (0, None))[0]:
                            need[key] = (val, sem)
                    for key, (val, sem) in need.items():
                        if waited.get(key, 0) < val:
                            engine.wait_ge(sem, val)
                            waited[key] = val
                    if op.fn is None:
                        continue
                    ins = op.fn(engine)
                    if op.isdma:
                        ins.then_inc(dsem[op.slot], 16)
                    elif op.signal:
                        ins.then_inc(esem[e], 1)

            block.sync(lambda eng: replay("sp", eng))
            block.tensor(lambda eng: replay("pe", eng))
            block.scalar(lambda eng: replay("act", eng))
            block.vector(lambda eng: replay("dve", eng))
            block.gpsimd(lambda eng: replay("pool", eng))


class Ring:
    def __init__(self, aps):
        self.aps = aps
        self.deps = [Dep() for _ in aps]
        self.i = 0

    def next(self):
        k = self.i % len(self.aps)
        self.i += 1
        return self.aps[k], self.deps[k]


def full_cfg():
    return dict(D=4096, NT=4096, HA=8, HG=4, DFF=11008)


def t5_bucket_np(rel):
    half, max_exact = 16, 8
    ret = np.where(rel > 0, half, 0)
    n = np.abs(rel)
    nf = np.maximum(n, 1).astype(np.float32)
    large = max_exact + (np.log(nf / np.float32(max_exact)) / np.float32(math.log(128 / max_exact))
                         * np.float32(half - max_exact)).astype(np.int32)
    large = np.minimum(large, half - 1)
    return ret + np.where(n < max_exact, n, large)


def host_consts():
    c = {}
    c["c_ident"] = np.eye(128, dtype=np.float32)
    c["c_flip"] = np.eye(128, dtype=np.float32)[::-1].copy()
    s = np.arange(128)[:, None]
    t = np.arange(128)[None, :]
    same = (s // 64) == (t // 64)
    g = np.zeros((128, 6, 128), np.float32)
    g[:, 0] = (same & (s <= t)) * (-1.0 / 16)
    g[:, 1] = (same & (s > t)) * (-1.0 / 16)
    g[:, 2] = (same & (s >= t)) * (-1.0 / 16)
    g[:, 3] = (same & (s < t)) * (-1.0 / 16)
    g[:, 4] = (same & (s <= t)) * 1.0
    g[:, 5] = (same & (s >= t)) * 1.0
    c["c_gla"] = g
    oh = np.zeros((32, 6, 640), np.float32)
    for di, dl in enumerate(range(-1, 5)):
        i = np.arange(640)
        rel = 128 * dl + 127 - i
        b = t5_bucket_np(rel.astype(np.int32))
        oh[b, di, i] = 1.0
    c["c_onehot"] = oh.reshape(32, 6 * 640)
    return c


def in_groups(cfg):
    HA, HG, D = cfg["HA"], cfg["HG"], cfg["D"]
    QK, GK_, GV_ = HA * 256, HG * 256, HG * 512
    o = 0
    g = []
    for name, n, kind in (("q", QK, "fm_qk"), ("k", QK, "fm_qk"), ("v", QK, "tm"), ("gq", GK_, "fm_plain"),
                          ("gk", GK_, "fm_plain"), ("gv", GV_, "tm"), ("gr", GV_, "tm_silu"),
                          ("glr", 32, "fm_glr"), ("ga", D, "fm_sig"), ("gb", D, "fm_sig")):
        g.append((name, o, n, min(512, n), kind))
        o += n
    return g, o


def build(cfg, dbg=()):
    D, NT, HA, HG, DFF = cfg["D"], cfg["NT"], cfg["HA"], cfg["HG"], cfg["DFF"]
    KC = D // 128
    NB = NT // 128
    TT = 512
    NTT = NT // TT
    NQC = NT // 512
    NKT = NT // 128
    QK, GKW, GVW = HA * 256, HG * 256, HG * 512
    KCB = QK // 128
    KCG = GVW // 128
    FT = DFF // 128
    groups, DIN = in_groups(cfg)
    nc = bass.Bass("TRN2", target_bir_lowering=False)

    def din(name, shape):
        return nc.dram_tensor(name, list(shape), F32, kind="ExternalInput").ap()

    def dscr(name, shape, dt=BF16):
        kind = "ExternalOutput" if name in dbg else "Internal"
        return nc.dram_tensor(name, list(shape), dt, kind=kind).ap()

    x_in = din("x", (NT, D))
    y_out = nc.dram_tensor("y", [NT, D], F32, kind="ExternalOutput").ap()
    rel_bias = din("rel_bias", (32, 8))
    g_mix = din("g_mix", (1, D))
    w_in = din("w_in", (D, DIN))
    q_norm_g = din("q_norm_g", (1, 128))
    k_norm_g = din("k_norm_g", (1, 128))
    lam_in = din("lam4", (1, 512))
    da_subln_g = din("da_subln_g", (1, 256))
    w_gate_f = din("w_gate_fwd", (16, GKW))
    b_gate_f = din("b_gate_fwd", (1, GKW))
    w_gate_b = din("w_gate_bwd", (16, GKW))
    b_gate_b = din("b_gate_bwd", (1, GKW))
    gla_norm_g = din("gla_norm_g", (1, 512))
    w_br_a = din("w_branch_a", (QK, D))
    w_br_b = din("w_branch_b", (GVW, D))
    w_out = din("w_out", (D, D))
    g_ffn = din("g_ffn", (1, D))
    w_up = din("w_up", (D, 2 * DFF))
    conv_w = din("conv_w", (3, DFF))
    conv_b = din("conv_b", (1, DFF))
    w_down = din("w_down", (DFF, D))
    c_ident = din("c_ident", (128, 128))
    c_flip = din("c_flip", (128, 128))
    c_gla = din("c_gla", (128, 6, 128))
    c_onehot = din("c_onehot", (32, 6 * 640))
    core_flags = din("core_flags", (128, 32))

    WB = {}
    for (name, c0, n, CW, kind) in groups:
        WB[name] = dscr("wb_" + name, (n // CW, 128, KC, CW))
    WB["bra"] = dscr("wb_bra", (D // 512 if D >= 512 else 1, 128, KCB, min(512, D)))
    WB["brb"] = dscr("wb_brb", (D // 512 if D >= 512 else 1, 128, KCG, min(512, D)))
    WB["out"] = dscr("wb_out", (D // 512 if D >= 512 else 1, 128, KC, min(512, D)))
    WB["upa"] = dscr("wb_upa", (FT, 128, KC, 128))
    WB["upg"] = dscr("wb_upg", (FT, 128, KC, 128))
    WB["down"] = dscr("wb_down", (D // 128, 128, FT, 128))
    QT = dscr("s_qt", (2 * HA, 128, NT))
    KT = dscr("s_kt", (2 * HA, 128, NT))
    VV = dscr("s_v", (NT, QK))
    GQT = dscr("s_gqt", (2 * HG, 128, NT))
    GKT = dscr("s_gkt", (2 * HG, 128, NT))
    GKM = dscr("s_gk", (NT, GKW))
    GVM = dscr("s_gv", (NT, GVW))
    GRM = dscr("s_gr", (NT, GVW))
    GLT = dscr("s_glt", (32, NT), F32)
    SGA = dscr("s_sga", (KC, 128, NT))
    SGB = dscr("s_sgb", (KC, 128, NT))
    OAT = dscr("s_oat", (KCB, 128, NT))
    OBT = dscr("s_obt", (KCG, 128, NT))
    OFW = dscr("s_of", (NT, GVW), F32)
    X1 = dscr("s_x1", (NT, D), F32)
    UBI = dscr("s_ubias", (8, 6 * 640), F32)

    P = Prog(nc)
    es = ExitStack()

    SB_BYTES = 207 * 1024
    BIG = es.enter_context(nc.sbuf_tensor("big", [128, SB_BYTES // 2], BF16))
    bump = [0]

    def sb(name, shape, dt=F32):
        esz = 4 if dt == F32 else 2
        n = 1
        for v in shape[1:]:
            n *= v
        nb = (n * esz + 63) // 64 * 64
        off = bump[0]
        bump[0] += nb
        assert bump[0] <= SB_BYTES, ("SBUF overflow", name, bump[0])
        v = BIG[0:shape[0], off // 2: off // 2 + n * esz // 2]
        if dt == F32:
            v = v.bitcast(F32)
        if len(shape) == 3:
            v = v.rearrange("p (a b) -> p a b", b=shape[2])
        return v

    banks = [es.enter_context(nc.psum_tensor("ps%d" % i, [128, 512], F32)) for i in range(8)]
    PS = Ring([b[:] for b in banks])

    ident_f = sb("ident_f", (128, 128))
    ident_b = sb("ident_b", (128, 128), BF16)
    flip_f = sb("flip_f", (128, 128))
    ones_b = sb("ones_b", (128, 128), BF16)
    ones_f = sb("ones_f", (128, 128))
    gla_f = sb("gla_f", (128, 6, 128))
    gla_b = sb("gla_b", (128, 4, 128), BF16)
    flags = sb("flags", (128, 32))
    eps_col = sb("eps_col", (128, 1))
    gq_col = sb("gq_col", (128, 2))
    cdep = Dep()
    P.dma("sp", ident_f, c_ident, w=[cdep])
    P.dma("sp", flip_f, c_flip, w=[cdep])
    P.dma("sp", gla_f, c_gla, w=[cdep])
    P.dma("sp", flags, core_flags, w=[cdep])
    P.dma("sp", gq_col[:, 0:1], q_norm_g.rearrange("o d -> d o"), w=[cdep])
    P.dma("sp", gq_col[:, 1:2], k_norm_g.rearrange("o d -> d o"), w=[cdep])
    P.op("dve", lambda e: e.tensor_copy(out=ident_b, in_=ident_f), r=[cdep], w=[cdep])
    P.op("dve", lambda e: e.memset(ones_b, 1.0), w=[cdep])
    P.op("dve", lambda e: e.memset(ones_f, 1.0), w=[cdep])
    P.op("dve", lambda e: e.memset(eps_col, EPS), w=[cdep])
    P.op("dve", lambda e: e.tensor_copy(out=gla_b, in_=gla_f[:, 0:4, :]), r=[cdep], w=[cdep])
    P.op("dve", lambda e: e.tensor_scalar(out=gq_col[:, 0:1], in0=gq_col[:, 0:1], scalar1=128.0 ** -0.5,
                                          scalar2=None, op0=ALU.mult), r=[cdep], w=[cdep])
    P.barrier()

    UE = 2048
    base_mark0 = bump[0]
    p0_s32 = Ring([sb("p0a%d" % i, (128, UE)) for i in range(3)])
    p0_s16 = Ring([sb("p0b%d" % i, (128, UE), BF16) for i in range(3)])
    base_mark = bump[0]
    WBD = {}
    win_units = []
    rest_units = []

    def plan_weight(src2d, K, c0, ncols, CW, dst, name, out_lists):
        kcn = K // 128
        nk = max(1, min(kcn, UE // CW))
        for cg in range(ncols // CW):
            dep = WBD.setdefault((name, cg), Dep())
            lst = []
            for k0 in range(0, kcn, nk):
                n = min(nk, kcn - k0)
                src = src2d[k0 * 128:(k0 + n) * 128, c0 + cg * CW: c0 + (cg + 1) * CW].rearrange(
                    "(k p) c -> p k c", p=128)
                lst.append((src, dst[cg, :, k0:k0 + n, :], n, CW, dep))
            out_lists.append(lst)

    for (name, c0, n, CW, kind) in groups:
        plan_weight(w_in, D, c0, n, CW, WB[name], name, win_units)
    _tmp = []
    cwd = min(512, D)
    plan_weight(w_br_a, QK, 0, D, cwd, WB["bra"], "bra", _tmp)
    plan_weight(w_br_b, GVW, 0, D, cwd, WB["brb"], "brb", _tmp)
    plan_weight(w_out, D, 0, D, cwd, WB["out"], "out", _tmp)
    _ua, _ug = [], []
    plan_weight(w_up, D, 0, DFF, 128, WB["upa"], "upa", _ua)
    plan_weight(w_up, D, DFF, DFF, 128, WB["upg"], "upg", _ug)
    for a_, g_ in zip(_ua, _ug):
        _tmp.append(a_)
        _tmp.append(g_)
    plan_weight(w_down, DFF, 0, D, 128, WB["down"], "down", _tmp)
    for lst in _tmp:
        rest_units.extend(lst)
    pump_state = {"i": 0, "rest": 0, "win": 0}

    def emit_unit(u, engs):
        src, dstap, n, CW, dep = u
        a32, d32 = p0_s32.next()
        a16, d16 = p0_s16.next()
        P.dma("sp", a32[:, 0:n * CW].rearrange("p (k c) -> p k c", c=CW), src, w=[d32])
        eng = engs[pump_state["i"] % len(engs)]
        pump_state["i"] += 1
        if eng == "act":
            P.op("act", lambda e, o=a16[:, 0:n * CW], i=a32[:, 0:n * CW]: e.copy(out=o, in_=i), r=[d32], w=[d16])
        else:
            P.op(eng, lambda e, o=a16[:, 0:n * CW], i=a32[:, 0:n * CW]: e.tensor_copy(out=o, in_=i), r=[d32], w=[d16])
        P.dma("pool", dstap, a16[:, 0:n * CW].rearrange("p (k c) -> p k c", c=CW), r=[d16], w=[dep])

    def pump_win(upto, engs=("dve", "act")):
        while pump_state["win"] < min(upto, len(win_units)):
            for u in win_units[pump_state["win"]]:
                emit_unit(u, engs)
            pump_state["win"] += 1

    def pump_rest(k, engs=("dve",)):
        for _ in range(k):
            if pump_state["rest"] >= len(rest_units):
                return
            emit_unit(rest_units[pump_state["rest"]], engs)
            pump_state["rest"] += 1

    def rest_left():
        return len(rest_units) - pump_state["rest"]

    def rstd_from_ss(ss_ap, n, dep):
        P.op("act", lambda e: e.activation(out=ss_ap, in_=ss_ap, func=AF.Ln, bias=eps_col[0:ss_ap.shape[0], :],
                                           scale=1.0 / n), r=[dep], w=[dep])
        P.op("act", lambda e: e.activation(out=ss_ap, in_=ss_ap, func=AF.Exp, scale=-0.5), r=[dep], w=[dep])

    def norm_transpose_block(loader, g_rep, gT, xt_ring, xn_ring, ss_ring, hT, hT_dep, col0, width=128, psring=None):
        xa, xd = xt_ring.next()
        na, nd = xn_ring.next()
        sa, sd = ss_ring.next()
        loader(xa, xd)
        P.op("act", lambda e: e.activation(out=na[0:width, :], in_=xa[0:width, :], func=AF.Square,
                                           accum_out=sa[0:width, :]), r=[xd], w=[sd, nd])
        rstd_from_ss(sa[0:width, :], D, sd)
        if g_rep is not None:
            P.op("dve", lambda e: e.scalar_tensor_tensor(out=na[0:width, :], in0=xa[0:width, :], scalar=sa[0:width, 0:1],
                                                         in1=g_rep[0:width, :], op0=ALU.mult, op1=ALU.mult),
                 r=[xd, sd], w=[nd])
        else:
            P.op("dve", lambda e: e.tensor_scalar(out=na[0:width, :], in0=xa[0:width, :], scalar1=sa[0:width, 0:1],
                                                  scalar2=None, op0=ALU.mult), r=[xd, sd], w=[nd])
        G = min(4, KC)
        for kg in range(KC // G):
            pa, pd = (psring or PS).next()
            pb = pa.bitcast(BF16)
            for i in range(G):
                kc = kg * G + i
                P.op("pe", lambda e, o=pb[:, i * 128:i * 128 + width], i_=na[0:width, kc * 128:(kc + 1) * 128]:
                     e.transpose(out=o, in_=i_, identity=ident_b[0:width, 0:width]), r=[nd], w=[pd])
            if g_rep is not None:
                src = pb[:, 0:G * 128].rearrange("p (g t) -> p g t", t=128)[:, :, 0:width]
                dst = hT[:, kg * G:(kg + 1) * G, col0:col0 + width]
                if kg % 2 == 0:
                    P.op("act", lambda e, o=dst, i_=src: e.copy(out=o, in_=i_), r=[pd], w=[hT_dep])
                else:
                    P.op("dve", lambda e, o=dst, i_=src: e.tensor_copy(out=o, in_=i_), r=[pd], w=[hT_dep])
            else:
                for i in range(G):
                    kc = kg * G + i
                    src = pb[:, i * 128:i * 128 + width]
                    dst = hT[:, kc, col0:col0 + width]
                    if i % 2 == 0:
                        P.op("act", lambda e, o=dst, i_=src, kc=kc: e.activation(out=o, in_=i_, func=AF.Copy,
                                                                                 scale=gT[:, kc:kc + 1]),
                             r=[pd], w=[hT_dep])
                    else:
                        P.op("dve", lambda e, o=dst, i_=src, kc=kc: e.tensor_scalar(out=o, in0=i_, scalar1=gT[:, kc:kc + 1],
                                                                                    scalar2=None, op0=ALU.mult),
                             r=[pd], w=[hT_dep])

    def mm_group(out_ps, pd, lhs_list, rhs_list, r):
        n = len(lhs_list)
        for i in range(n):
            P.op("pe", lambda e, l=lhs_list[i], rr=rhs_list[i], s=(i == 0), t=(i == n - 1):
                 e.matmul(out_ps, lhsT=l, rhs=rr, start=s, stop=t), r=r, w=[pd])

    bump[0] = base_mark
    if True:
        psb = sb
        g_rep = psb("a_grep", (128, D))
        xt_ring = Ring([psb("a_xt%d" % i, (128, D)) for i in range(1)])
        xn_ring = Ring([psb("a_xn%d" % i, (128, D), BF16) for i in range(2)])
        ss_ring = Ring([psb("a_ss%d" % i, (128, 1)) for i in range(2)])
        hT = psb("a_hT", (128, KC, TT), BF16)
        hT_dep = Dep()
        wt_ring = Ring([psb("a_wt%d" % i, (128, KC, 512), BF16) for i in range(2)])
        wt_slices = {id(a): [Dep() for _ in range(8)] for a in wt_ring.aps}

        def cast_cg_into(lst, wa):
            k0 = 0
            sl = wt_slices[id(wa)]
            for ui, (src, dstap, n, CW, dep) in enumerate(lst):
                a32, d32 = p0_s32.next()
                v32 = a32[:, 0:n * CW].rearrange("p (k c) -> p k c", c=CW)
                P.dma("sp", v32, src, w=[d32])
                eng = ("dve", "act", "pool")[pump_state["i"] % 3]
                pump_state["i"] += 1
                o = wa[:, k0:k0 + n, 0:CW]
                if eng == "act":
                    P.op("act", lambda e, o=o, i=v32: e.copy(out=o, in_=i), r=[d32], w=[sl[ui]])
                else:
                    P.op(eng, lambda e, o=o, i=v32: e.tensor_copy(out=o, in_=i), r=[d32], w=[sl[ui]])
                P.dma("pool", dstap, o, r=[sl[ui]], w=[dep])
                k0 += n
        sq_ring = Ring([psb("a_sq%d" % i, (128, 512), BF16) for i in range(2)])
        rb_ring = Ring([psb("a_rb%d" % i, (128, 512)) for i in range(2)])
        st_ring = Ring([psb("a_st%d" % i, (128, 512), BF16) for i in range(4)])
        st32_ring = Ring([psb("a_st32%d" % i, (32, 512)) for i in range(2)])
        gdep = Dep()
        P.dma("sp", g_rep, g_mix.partition_broadcast(128)[:, 0, :], w=[gdep])
        P.barrier()
        sdep = Dep()
        a_iters = sum(n // CW for (_, _, n, CW, _) in groups)
        a_quota = 0 if NTT <= 1 else -(-(len(rest_units) // 2) // (a_iters * (NTT - 1)))
        a_it = [0]
        for tt in range(NTT):
            t0 = tt * TT
            a_it[0] = 0
            for b in range(TT // 128):
                norm_transpose_block(lambda xa, xd, r0=t0 + b * 128: P.dma("sp", xa, x_in[r0:r0 + 128, :], w=[xd]),
                                     g_rep, None, xt_ring, xn_ring, ss_ring, hT, hT_dep, b * 128)
            for (name, c0, n, CW, kind) in groups:
                for cg in range(n // CW):
                    wa, wd = wt_ring.next()
                    wv = wa[:, :, 0:CW]
                    wsl = wt_slices[id(wa)]
                    if tt == 0:
                        cast_cg_into(win_units[a_it[0]], wa)
                    else:
                        pump_rest(a_quota)
                        P.dma("sp", wv, WB[name][cg], r=[WBD[(name, cg)]], w=[wd] + wsl)
                    a_it[0] += 1
                    wd = [wd] + wsl
                    if kind.startswith("tm"):
                        for b in range(TT // 128):
                            pa, pd = PS.next()
                            mm_group(pa[:, 0:CW], pd, [hT[:, kc, b * 128:(b + 1) * 128] for kc in range(KC)],
                                     [wv[:, kc, :] for kc in range(KC)], [hT_dep] + wd)
                            sa, sd = st_ring.next()
                            fn = AF.Silu if kind == "tm_silu" else AF.Copy
                            if kind == "tm_silu":
                                P.op("act", lambda e, o=sa[:, 0:CW], i_=pa[:, 0:CW]:
                                     e.activation(out=o, in_=i_, func=AF.Silu), r=[pd], w=[sd])
                            else:
                                P.op("dve", lambda e, o=sa[:, 0:CW], i_=pa[:, 0:CW]: e.tensor_copy(out=o, in_=i_),
                                     r=[pd], w=[sd])
                            dst = {"v": VV, "gv": GVM, "gr": GRM}[name]
                            P.dma("pool", dst[t0 + b * 128: t0 + (b + 1) * 128, cg * CW:(cg + 1) * CW], sa[:, 0:CW],
                                  r=[sd], w=[sdep])
                        continue
                    for j in range(max(1, CW // 128)):
                        M = min(128, CW)
                        ct = cg * max(1, CW // 128) + j
                        pa, pd = PS.next()
                        mm_group(pa[0:M, :], pd, [wv[:, kc, j * 128:j * 128 + M] for kc in range(KC)],
                                 [hT[:, kc, :] for kc in range(KC)], [hT_dep] + wd)
                        if kind == "fm_qk":
                            qa, qd = sq_ring.next()
                            P.op("act", lambda e, o=qa, i_=pa: e.activation(out=o, in_=i_, func=AF.Square),
                                 r=[pd], w=[qd])
                            p2, p2d = PS.next()
                            P.op("pe", lambda e, o=p2, i_=qa: e.matmul(o, lhsT=ones_b, rhs=i_, start=True, stop=True),
                                 r=[qd], w=[p2d])
                            ra, rd = rb_ring.next()
                            P.op("act", lambda e, o=ra, i_=p2: e.activation(out=o, in_=i_, func=AF.Ln, bias=eps_col,
                                                                            scale=1.0 / 128), r=[p2d], w=[rd])
                            P.op("act", lambda e, o=ra: e.activation(out=o, in_=o, func=AF.Exp, scale=-0.5),
                                 r=[rd], w=[rd])
                            sa, sd = st_ring.next()
                            gcol = gq_col[:, 0:1] if name == "q" else gq_col[:, 1:2]
                            P.op("dve", lambda e, o=sa, i_=pa, g=gcol, r_=ra:
                                 e.scalar_tensor_tensor(out=o, in0=i_, scalar=g, in1=r_, op0=ALU.mult, op1=ALU.mult),
                                 r=[pd, rd], w=[sd])
                            dst = QT if name == "q" else KT
                            P.dma("pool", dst[ct, :, t0:t0 + TT], sa, r=[sd], w=[sdep])
                        elif kind == "fm_plain":
                            sa, sd = st_ring.next()
                            sc = 256.0 ** -0.5 if name == "gq" else 1.0
                            P.op("act", lambda e, o=sa, i_=pa, s=sc: e.activation(out=o, in_=i_, func=AF.Copy, scale=s),
                                 r=[pd], w=[sd])
                            dst = GQT if name == "gq" else GKT
                            P.dma("pool", dst[ct, :, t0:t0 + TT], sa, r=[sd], w=[sdep])
                        elif kind == "fm_sig":
                            sa, sd = st_ring.next()
                            P.op("act", lambda e, o=sa, i_=pa: e.activation(out=o, in_=i_, func=AF.Sigmoid),
                                 r=[pd], w=[sd])
                            dst = SGA if name == "ga" else SGB
                            P.dma("pool", dst[ct, :, t0:t0 + TT], sa, r=[sd], w=[sdep])
                        elif kind == "fm_glr":
                            sa, sd = st32_ring.next()
                            P.op("dve", lambda e, o=sa, i_=pa[0:32, :]: e.tensor_copy(out=o, in_=i_), r=[pd], w=[sd])
                            P.dma("pool", GLT[:, t0:t0 + TT], sa, r=[sd], w=[sdep])
            name, c0, n, CW, kind = [g for g in groups if g[0] == "gk"][0]
            for cg in range(n // CW):
                wa, wd = wt_ring.next()
                wv = wa[:, :, 0:CW]
                P.dma("sp", wv, WB[name][cg], r=[WBD[(name, cg)]], w=[wd] + wt_slices[id(wa)])
                for b in range(TT // 128):
                    pa, pd = PS.next()
                    mm_group(pa[:, 0:CW], pd, [hT[:, kc, b * 128:(b + 1) * 128] for kc in range(KC)],
                             [wv[:, kc, :] for kc in range(KC)], [hT_dep, wd] + wt_slices[id(wa)])
                    sa, sd = st_ring.next()
                    P.op("dve", lambda e, o=sa[:, 0:CW], i_=pa[:, 0:CW]: e.tensor_copy(out=o, in_=i_), r=[pd], w=[sd])
                    P.dma("pool", GKM[t0 + b * 128: t0 + (b + 1) * 128, cg * CW:(cg + 1) * CW], sa[:, 0:CW],
                          r=[sd], w=[sdep])
        P.barrier()
    if "stopA" in dbg:
        P.finish()
        P.emit()
        es.close()
        return nc

    bump[0] = base_mark
    if True:
        PS_O = Ring(PS.aps[0:4])
        PS_S = Ring(PS.aps[4:7])
        PS_T = Ring(PS.aps[7:8])
        lamv = sb("b_lamv", (1, 512))
        lamp = sb("b_lamp", (1, 512))
        lams = sb("b_lams", (1, 4))
        neglam = sb("b_neglam", (128, 1))
        gsub = sb("b_gsub", (128, 256))
        relb = sb("b_relb", (32, 8))
        fbl = sb("b_fbl", (128, 2, 8))
        fb4 = sb("b_fb4", (128, 8, 4))
        BT = sb("b_bt", (128, 8, 512))
        th_ring = Ring([sb("b_th%d" % i, (128, 512)) for i in range(2)])
        _mk = bump[0]
        onehot = sb("b_onehot", (32, 6 * 640))
        ub_sb = sb("b_ubsb", (8, 6 * 640))
        bump[0] = _mk
        q_ring = Ring([sb("b_q%d" % i, (128, 2, NT), BF16) for i in range(2)])
        k_ring = Ring([sb("b_k%d" % i, (128, 2, NT), BF16) for i in range(2)])
        v_ring = Ring([sb("b_v%d" % i, (128, NKT, 257), BF16) for i in range(2)])
        PT = Ring([sb("b_pt%d" % i, (128, 512), BF16) for i in range(3)])
        TMP = Ring([sb("b_tmp%d" % i, (128, 512)) for i in range(2)])
        OA = sb("b_oa", (128, 4, 256))
        oa_dep = [Dep() for _ in range(4)]
        rz_ring = Ring([sb("b_rz%d" % i, (128, 1)) for i in range(4)])
        ss2_ring = Ring([sb("b_ss%d" % i, (128, 1)) for i in range(4)])
        on_ring = Ring([sb("b_on%d" % i, (128, 256), BF16) for i in range(2)])
        ost_ring = Ring([sb("b_ost%d" % i, (128, 2, 128), BF16) for i in range(2)])
        junkb = (sb("b_junk", (128, 256), BF16), Dep())
        sd0 = Dep()
        P.dma("sp", lamv, lam_in, w=[sd0])
        P.dma("sp", gsub, da_subln_g.partition_broadcast(128)[:, 0, :], w=[sd0])
        P.dma("sp", relb, rel_bias, w=[sd0])
        P.dma("sp", onehot, c_onehot, w=[sd0])
        P.dma("sp", fbl[:, 0, :], rel_bias[15:16, :].partition_broadcast(128)[:, 0, :], w=[sd0])
        P.dma("sp", fbl[:, 1, :], rel_bias[31:32, :].partition_broadcast(128)[:, 0, :], w=[sd0])
        P.op("dve", lambda e: e.tensor_tensor(out=lamp[:, 0:128], in0=lamv[:, 0:128], in1=lamv[:, 128:256], op=ALU.mult),
             r=[sd0], w=[sd0])
        P.op("dve", lambda e: e.tensor_tensor(out=lamp[:, 128:256], in0=lamv[:, 256:384], in1=lamv[:, 384:512],
                                              op=ALU.mult), r=[sd0], w=[sd0])
        P.op("dve", lambda e: e.tensor_reduce(out=lams[:, 0:2], in_=lamp[:, 0:256].rearrange("p (a b) -> p a b", b=128),
                                              axis=mybir.AxisListType.X, op=ALU.add), r=[sd0], w=[sd0])
        P.op("act", lambda e: e.activation(out=lams[:, 0:2], in_=lams[:, 0:2], func=AF.Exp), r=[sd0], w=[sd0])
        P.op("dve", lambda e: e.tensor_tensor(out=lams[:, 2:3], in0=lams[:, 1:2], in1=lams[:, 0:1], op=ALU.subtract),
             r=[sd0], w=[sd0])
        P.op("dve", lambda e: e.tensor_scalar(out=lams[:, 2:3], in0=lams[:, 2:3], scalar1=-0.2, scalar2=None,
                                              op0=ALU.add), r=[sd0], w=[sd0])
        pa, pd = PS_T.next()
        P.op("pe", lambda e: e.matmul(pa[:, 0:1], lhsT=ones_f[0:1, :], rhs=lams[:, 2:3], start=True, stop=True),
             r=[sd0], w=[pd])
        P.op("dve", lambda e: e.tensor_copy(out=neglam, in_=pa[:, 0:1]), r=[pd], w=[sd0])
        P.op("dve", lambda e: e.tensor_scalar(out=gsub, in0=gsub, scalar1=0.8, scalar2=None, op0=ALU.mult),
             r=[sd0], w=[sd0])
        for v in range(4):
            P.op("dve", lambda e, v=v: e.tensor_scalar(out=fb4[:, :, v], in0=fbl[:, v % 2, :], scalar1=SHIFT,
                                                       scalar2=(flags[:, 0:1] if v >= 2 else 0.0),
                                                       op0=ALU.add, op1=ALU.add), r=[sd0], w=[sd0])
        for ch in range(8):
            pa, pd = PS_S.next()
            P.op("pe", lambda e, pa=pa, ch=ch: e.matmul(pa[0:8, 0:480], lhsT=relb, rhs=onehot[:, ch * 480:(ch + 1) * 480],
                                                        start=True, stop=True), r=[sd0], w=[pd])
            P.op("dve", lambda e, pa=pa, ch=ch: e.tensor_copy(out=ub_sb[:, ch * 480:(ch + 1) * 480], in_=pa[0:8, 0:480]),
                 r=[pd], w=[sd0])
        P.dma("pool", UBI, ub_sb, r=[sd0], w=[sd0])
        P.barrier()
        for i in range(2):
            P.op("dve", lambda e, i=i: e.memset(v_ring.aps[i][:, :, 256:257], 1.0), w=[v_ring.deps[i]])
        bt_dep = Dep()
        odep = Dep()
        b_quota = -(-(rest_left() // 2) // (HA * NQC))
        for h in range(HA):
            qa, qd = q_ring.next()
            ka, kd = k_ring.next()
            va, vd = v_ring.next()
            P.dma("sp", qa, QT[2 * h:2 * h + 2].rearrange("m p t -> p m t"), w=[qd])
            P.dma("sp", ka, KT[2 * h:2 * h + 2].rearrange("m p t -> p m t"), w=[kd])
            P.dma("sp", va[:, :, 0:256], VV[:, h * 256:(h + 1) * 256].rearrange("(j p) c -> p j c", p=128), w=[vd])
            for di in range(6):
                ta, td = th_ring.next()
                src = bass.AP(tensor=UBI.tensor, offset=h * 6 * 640 + di * 640, ap=[[1, 128], [1, 512]])
                P.dma("sp", ta, src, w=[td])
                pa, pd = PS_T.next()
                P.op("pe", lambda e, pa=pa, ta=ta: e.matmul(pa, lhsT=flip_f, rhs=ta, start=True, stop=True),
                     r=[td], w=[pd])
                P.op("dve", lambda e, pa=pa, di=di: e.tensor_scalar(out=BT[:, di, :], in0=pa, scalar1=SHIFT, scalar2=None,
                                                                    op0=ALU.add), r=[pd], w=[bt_dep])
            P.op("dve", lambda e: e.tensor_scalar(out=BT[:, 6, :], in0=BT[:, 5, :], scalar1=flags[:, 0:1], scalar2=None,
                                                  op0=ALU.add), r=[bt_dep], w=[bt_dep])
            P.op("dve", lambda e: e.tensor_scalar(out=BT[:, 7, :], in0=BT[:, 0, :], scalar1=flags[:, 0:1], scalar2=None,
                                                  op0=ALU.add), r=[bt_dep], w=[bt_dep])
            for c in range(NQC):
                pump_rest(b_quota)
                for m in range(2):
                    qm = qa[:, m, c * 512:(c + 1) * 512]
                    O = [PS_O.next() for _ in range(4)]

                    def st_mm(j):
                        sa, sd = PS_S.next()
                        P.op("pe", lambda e, sa=sa, j=j, ka=ka, m=m, qm=qm: e.matmul(sa, lhsT=ka[:, m, j * 128:(j + 1) * 128], rhs=qm,
                                                                  start=True, stop=True), r=[kd, qd], w=[sd])
                        return sa, sd
                    pend = [st_mm(0)]
                    if NKT > 1:
                        pend.append(st_mm(1))
                    for j in range(NKT):
                        sa, sd = pend.pop(0)
                        if j + 2 < NKT:
                            pend.append(st_mm(j + 2))
                        pt, ptd = PT.next()
                        dl = j - 4 * c
                        if -1 <= dl <= 4:
                            var = dl + 1
                            if j == NKT // 2 and c == NQC // 2 - 1:
                                var = 6
                            if j == NKT // 2 - 1 and c == NQC // 2:
                                var = 7
                            ta, td = TMP.next()
                            P.op("dve", lambda e, ta=ta, sa=sa, var=var: e.tensor_tensor(out=ta, in0=sa, in1=BT[:, var, :],
                                                                                        op=ALU.add),
                                 r=[sd, bt_dep], w=[td])
                            P.op("act", lambda e, pt=pt, ta=ta: e.activation(out=pt, in_=ta, func=AF.Exp),
                                 r=[td], w=[ptd])
                        else:
                            idx = (0 if dl < 0 else 1) + (2 if ((j < NKT // 2) != (c < NQC // 2)) else 0)
                            P.op("act", lambda e, pt=pt, sa=sa, idx=idx, h=h: e.activation(out=pt, in_=sa, func=AF.Exp,
                                                                                     bias=fb4[:, h, idx:idx + 1]),
                                 r=[sd], w=[ptd])
                        for qb in range(4):
                            P.op("pe", lambda e, qb=qb, pt=pt, j=j, O=O, va=va: e.matmul(O[qb][0][:, 0:257],
                                                                             lhsT=pt[:, qb * 128:(qb + 1) * 128],
                                                                             rhs=va[:, j, :], start=(j == 0),
                                                                             stop=(j == NKT - 1)),
                                 r=[ptd, vd], w=[O[qb][1]])
                    for qb in range(4):
                        oa_, od_ = O[qb]
                        rz, rzd = rz_ring.next()
                        P.op("dve", lambda e, rz=rz, oa_=oa_: e.reciprocal(out=rz, in_=oa_[:, 256:257]), r=[od_], w=[rzd])
                        if m == 0:
                            P.op("dve", lambda e, rz=rz, oa_=oa_, qb=qb: e.tensor_scalar(out=OA[:, qb, :], in0=oa_[:, 0:256],
                                                                                        scalar1=rz, scalar2=None,
                                                                                        op0=ALU.mult),
                                 r=[od_, rzd], w=[oa_dep[qb]])
                            continue
                        P.op("dve", lambda e, rz=rz: e.tensor_scalar(out=rz, in0=rz, scalar1=neglam, scalar2=None,
                                                                     op0=ALU.mult), r=[rzd], w=[rzd])
                        P.op("dve", lambda e, rz=rz, oa_=oa_, qb=qb: e.scalar_tensor_tensor(
                            out=OA[:, qb, :], in0=oa_[:, 0:256], scalar=rz, in1=OA[:, qb, :], op0=ALU.mult, op1=ALU.add),
                            r=[od_, rzd, oa_dep[qb]], w=[oa_dep[qb]])
                        ss, ssd = ss2_ring.next()
                        P.op("act", lambda e, ss=ss, qb=qb: e.activation(out=junkb[0], in_=OA[:, qb, :], func=AF.Square,
                                                                         accum_out=ss), r=[oa_dep[qb]], w=[ssd, junkb[1]])
                        rstd_from_ss(ss, 256, ssd)
                        on, ond = on_ring.next()
                        P.op("dve", lambda e, on=on, ss=ss, qb=qb: e.scalar_tensor_tensor(
                            out=on, in0=OA[:, qb, :], scalar=ss, in1=gsub, op0=ALU.mult, op1=ALU.mult),
                            r=[oa_dep[qb], ssd], w=[ond])
                        pa, pd = PS_T.next()
                        pb = pa.bitcast(BF16)
                        for i in range(2):
                            P.op("pe", lambda e, i=i, pb=pb, on=on: e.transpose(out=pb[:, i * 128:(i + 1) * 128],
                                                                                in_=on[:, i * 128:(i + 1) * 128],
                                                                                identity=ident_b), r=[ond], w=[pd])
                        osb, osd = ost_ring.next()
                        P.op("act", lambda e, osb=osb, pb=pb: e.copy(out=osb, in_=pb[:, 0:256].rearrange(
                            "p (a b) -> p a b", b=128)), r=[pd], w=[osd])
                        t0 = c * 512 + qb * 128
                        P.dma("pool", OAT[2 * h:2 * h + 2, :, t0:t0 + 128].rearrange("k p t -> p k t"), osb,
                              r=[osd], w=[odep])
        P.barrier()
    if "stopB" in dbg:
        P.finish()
        P.emit()
        es.close()
        return nc

    bump[0] = base_mark
    if True:
        G2 = 2 * HG
        NHB = max(1, GKW // 512)
        HW_ = min(512, GKW)
        PS_G = Ring(PS.aps[0:2])
        PS_PO = Ring(PS.aps[2:6])
        PS_KV = Ring(PS.aps[6:8])
        wgp = [sb("c_wgf", (32, GKW)), sb("c_wgb", (32, GKW))]
        bgp = [sb("c_bgf", (1, GKW)), sb("c_bgb", (1, GKW))]
        ggr = sb("c_ggr", (128, 512))
        S = sb("c_S", (128, G2, 512))
        Sb = sb("c_Sb", (128, G2, 512), BF16)
        s_dep = [Dep() for _ in range(G2)]
        sb_dep = [Dep() for _ in range(G2)]

        def ring2(name, shape, dt=F32, n=2):
            return Ring([sb("%s%d" % (name, i), shape, dt) for i in range(n)])
        r_glt = ring2("c_glt", (32, 128))
        r_gqt = ring2("c_gqt", (128, G2, 128), BF16)
        r_gkt = ring2("c_gkt", (128, G2, 128), BF16)
        r_gk = ring2("c_gk", (128, GKW), BF16)
        r_gv = ring2("c_gv", (128, GVW), BF16)
        r_gr = ring2("c_gr", (128, GVW), BF16)
        r_of = ring2("c_of", (128, GVW))
        r_ls = ring2("c_ls", (128, GKW), n=1)
        r_lb = ring2("c_lb", (128, GKW), BF16)
        r_ep = ring2("c_ep", (128, G2, 128))
        r_em = ring2("c_em", (128, G2, 128))
        r_qd = ring2("c_qd", (128, G2, 128), BF16)
        r_ki = ring2("c_ki", (128, G2, 128), BF16)
        r_ed = ring2("c_ed", (128, GKW), n=1)
        r_kd = ring2("c_kd", (128, GKW), BF16)
        r_at = ring2("c_at", (128, HG, 128), BF16)
        r_ofs = ring2("c_ofs", (128, GVW), n=1)
        r_os = ring2("c_os", (128, 512))
        r_gg = ring2("c_gg", (128, 512))
        r_ob = ring2("c_ob", (128, 512), BF16)
        r_obst = ring2("c_obst", (128, 4, 128), BF16)
        r_ss = ring2("c_ss", (128, 1), n=4)
        junkc = (sb("c_junk", (128, 512), BF16), Dep())
        sd0 = Dep()
        for t in wgp:
            P.op("dve", lambda e, t=t: e.memset(t, 0.0), w=[sd0])
        P.dma("sp", wgp[0][0:16, :], w_gate_f, r=[sd0], w=[sd0])
        P.dma("sp", wgp[1][16:32, :], w_gate_b, r=[sd0], w=[sd0])
        P.dma("sp", bgp[0], b_gate_f, w=[sd0])
        P.dma("sp", bgp[1], b_gate_b, w=[sd0])
        P.dma("sp", ggr, gla_norm_g.partition_broadcast(128)[:, 0, :], w=[sd0])
        P.barrier()
        ofdep = Dep()
        obdep = Dep()
        NCH = NT // 64
        for di in range(2):
            fwd = di == 0
            for g in range(G2):
                P.op("dve", lambda e, g=g: e.memset(S[:, g, :], 0.0), w=[s_dep[g]])
                P.op("dve", lambda e, g=g: e.memset(Sb[:, g, :], 0.0), w=[sb_dep[g]])
            blocks = range(NB) if fwd else range(NB - 1, -1, -1)
            c_quota = -(-rest_left() // ((2 - di) * NB))
            for blk in blocks:
                pump_rest(c_quota, engs=("pool",))
                t0 = blk * 128
                glt, gltd = r_glt.next()
                gqt, gqtd = r_gqt.next()
                gkt, gktd = r_gkt.next()
                gk, gkd = r_gk.next()
                gv, gvd = r_gv.next()
                P.dma("sp", glt, GLT[:, t0:t0 + 128], w=[gltd])
                P.dma("sp", gqt, GQT[:, :, t0:t0 + 128].rearrange("k p t -> p k t"), w=[gqtd])
                P.dma("sp", gkt, GKT[:, :, t0:t0 + 128].rearrange("k p t -> p k t"), w=[gktd])
                P.dma("sp", gk, GKM[t0:t0 + 128, :], w=[gkd])
                P.dma("sp", gv, GVM[t0:t0 + 128, :], w=[gvd])
                if not fwd:
                    of, ofd = r_of.next()
                    gr, grd = r_gr.next()
                    P.dma("sp", of, OFW[t0:t0 + 128, :], r=[ofdep], w=[ofd])
                    P.dma("sp", gr, GRM[t0:t0 + 128, :], w=[grd])
                ls, lsd = r_ls.next()
                lb, lbd = r_lb.next()
                for hb in range(NHB):
                    pa, pd = PS_G.next()
                    cs = slice(hb * HW_, (hb + 1) * HW_)
                    P.op("pe", lambda e, pa=pa, cs=cs, glt=glt, di=di: e.matmul(pa[:, 0:HW_], lhsT=glt, rhs=wgp[di][:, cs],
                                                                         start=True, stop=False), r=[gltd], w=[pd])
                    P.op("pe", lambda e, pa=pa, cs=cs, di=di: e.matmul(pa[:, 0:HW_], lhsT=ones_f[0:1, :], rhs=bgp[di][:, cs],
                                                                start=False, stop=True), w=[pd])
                    P.op("act", lambda e, pa=pa, cs=cs, ls=ls: e.activation(out=ls[:, cs], in_=pa[:, 0:HW_], func=AF.Exp,
                                                                            scale=-1.0), r=[pd], w=[lsd])
                P.op("act", lambda e, ls=ls, lb=lb: e.activation(out=lb, in_=ls, func=AF.Ln, bias=ones_f[:, 0:1]),
                     r=[lsd], w=[lbd])
                ep, epd = r_ep.next()
                em, emd = r_em.next()
                qd_, qdd = r_qd.next()
                ki, kid = r_ki.next()
                tri = 0 if fwd else 2
                for g0 in range(0, G2, 4):
                    ng = min(4, G2 - g0)
                    pa, pd = PS_G.next()
                    for i in range(ng):
                        P.op("pe", lambda e, pa=pa, i=i, g=g0 + i, lb=lb, tri=tri: e.matmul(
                            pa[:, i * 128:(i + 1) * 128], lhsT=lb[:, g * 128:(g + 1) * 128], rhs=gla_b[:, tri, :],
                            start=True, stop=True), r=[lbd], w=[pd])
                    pv = pa[:, 0:ng * 128].rearrange("p (a b) -> p a b", b=128)
                    P.op("act", lambda e, pv=pv, ep=ep, g0=g0, ng=ng: e.activation(out=ep[:, g0:g0 + ng, :], in_=pv,
                                                                                   func=AF.Exp), r=[pd], w=[epd])
                    P.op("act", lambda e, pv=pv, em=em, g0=g0, ng=ng: e.activation(out=em[:, g0:g0 + ng, :], in_=pv,
                                                                                   func=AF.Exp, scale=-1.0),
                         r=[pd], w=[emd])
                P.op("dve", lambda e, qd_=qd_, gqt=gqt, ep=ep: e.tensor_tensor(out=qd_, in0=gqt, in1=ep, op=ALU.mult),
                     r=[gqtd, epd], w=[qdd])
                P.op("dve", lambda e, ki=ki, gkt=gkt, em=em: e.tensor_tensor(out=ki, in0=gkt, in1=em, op=ALU.mult),
                     r=[gktd, emd], w=[kid])
                ed, edd = r_ed.next()
                kd_, kdd = r_kd.next()
                ut = 1 if fwd else 3
                for hb in range(NHB):
                    pa, pd = PS_G.next()
                    cs = slice(hb * HW_, (hb + 1) * HW_)
                    P.op("pe", lambda e, pa=pa, cs=cs, lb=lb, ut=ut: e.matmul(pa[:, 0:HW_], lhsT=gla_b[:, ut, :], rhs=lb[:, cs],
                                                                       start=True, stop=True), r=[lbd], w=[pd])
                    P.op("act", lambda e, pa=pa, cs=cs, ed=ed: e.activation(out=ed[:, cs], in_=pa[:, 0:HW_], func=AF.Exp),
                         r=[pd], w=[edd])
                P.op("dve", lambda e, kd_=kd_, gk=gk, ed=ed: e.tensor_tensor(out=kd_, in0=gk, in1=ed, op=ALU.mult),
                     r=[gkd, edd], w=[kdd])
                at, atd = r_at.next()
                pa, pd = PS_G.next()
                for hd in range(HG):
                    for dh in range(2):
                        P.op("pe", lambda e, pa=pa, hd=hd, dh=dh, ki=ki, qd_=qd_: e.matmul(
                            pa[:, hd * 128:(hd + 1) * 128], lhsT=ki[:, hd * 2 + dh, :], rhs=qd_[:, hd * 2 + dh, :],
                            start=(dh == 0), stop=(dh == 1)), r=[kid, qdd], w=[pd])
                mk = 4 if fwd else 5
                for hd in range(HG):
                    P.op("dve", lambda e, pa=pa, hd=hd, at=at, mk=mk: e.tensor_tensor(
                        out=at[:, hd, :], in0=pa[:, hd * 128:(hd + 1) * 128], in1=gla_f[:, mk, :], op=ALU.mult),
                        r=[pd], w=[atd])
                if fwd and blk == 0 and "d_lb" in dbg:
                    P.dma("pool", dscr("d_lb", (128, GKW)), lb, r=[lbd], w=[Dep()])
                    P.dma("pool", dscr("d_ep", (128, G2, 128), F32), ep, r=[epd], w=[Dep()])
                    P.dma("pool", dscr("d_at", (128, HG, 128)), at, r=[atd], w=[Dep()])
                    P.dma("pool", dscr("d_qd", (128, G2, 128)), qd_, r=[qdd], w=[Dep()])
                    P.dma("pool", dscr("d_kd", (128, GKW)), kd_, r=[kdd], w=[Dep()])
                po = [PS_PO.next() for _ in range(HG)]
                for ch in ((0, 1) if fwd else (1, 0)):
                    n = blk * 2 + ch
                    if (fwd and n == NCH // 2) or ((not fwd) and n == NCH // 2 - 1):
                        for g in range(G2):
                            P.op("dve", lambda e, g=g: e.tensor_scalar(out=S[:, g, :], in0=S[:, g, :], scalar1=flags[:, 1:2],
                                                                       scalar2=None, op0=ALU.mult),
                                 r=[s_dep[g]], w=[s_dep[g]])
                            P.op("act", lambda e, g=g: e.copy(out=Sb[:, g, :], in_=S[:, g, :]), r=[s_dep[g]], w=[sb_dep[g]])
                    rows = slice(ch * 64, ch * 64 + 64)
                    dcol = (ch * 64 + 63) if fwd else (ch * 64)
                    for hd in range(HG):
                        pa, pd = po[hd]
                        vs = slice(hd * 512, (hd + 1) * 512)
                        P.op("pe", lambda e, pa=pa, at=at, gv=gv, hd=hd, rows=rows, vs=vs: e.matmul(
                            pa[rows, :], lhsT=at[rows, hd, rows], rhs=gv[rows, vs], start=True, stop=False),
                            r=[atd, gvd], w=[pd])
                        for dh in range(2):
                            g = hd * 2 + dh
                            P.op("pe", lambda e, pa=pa, qd_=qd_, g=g, rows=rows, dh=dh: e.matmul(
                                pa[rows, :], lhsT=qd_[:, g, rows], rhs=Sb[:, g, :], start=False, stop=(dh == 1)),
                                r=[qdd, sb_dep[g]], w=[pd])
                        for dh in range(2):
                            g = hd * 2 + dh
                            ka_, kvd = PS_KV.next()
                            P.op("pe", lambda e, ka_=ka_, kd_=kd_, gv=gv, g=g, rows=rows, vs=vs: e.matmul(
                                ka_, lhsT=kd_[rows, g * 128:(g + 1) * 128], rhs=gv[rows, vs], start=True, stop=True),
                                r=[kdd, gvd], w=[kvd])
                            P.op("dve", lambda e, ka_=ka_, g=g, ep=ep, dcol=dcol: e.scalar_tensor_tensor(
                                out=S[:, g, :], in0=S[:, g, :], scalar=ep[:, g, dcol:dcol + 1], in1=ka_,
                                op0=ALU.mult, op1=ALU.add), r=[kvd, epd, s_dep[g], sb_dep[g]], w=[s_dep[g]])
                            P.op("act", lambda e, g=g: e.copy(out=Sb[:, g, :], in_=S[:, g, :]), r=[s_dep[g]], w=[sb_dep[g]])
                if fwd:
                    ofs, ofsd = r_ofs.next()
                    for hd in range(HG):
                        pa, pd = po[hd]
                        vs = slice(hd * 512, (hd + 1) * 512)
                        if hd % 2 == 0:
                            P.op("act", lambda e, pa=pa, ofs=ofs, vs=vs: e.copy(out=ofs[:, vs], in_=pa), r=[pd], w=[ofsd])
                        else:
                            P.op("dve", lambda e, pa=pa, ofs=ofs, vs=vs: e.tensor_copy(out=ofs[:, vs], in_=pa),
                                 r=[pd], w=[ofsd])
                    P.dma("pool", OFW[t0:t0 + 128, :], ofs, r=[ofsd], w=[ofdep])
                else:
                    for hd in range(HG):
                        pa, pd = po[hd]
                        vs = slice(hd * 512, (hd + 1) * 512)
                        osm, osd = r_os.next()
                        P.op("dve", lambda e, pa=pa, osm=osm, of=of, vs=vs: e.tensor_tensor(out=osm, in0=pa, in1=of[:, vs],
                                                                                          op=ALU.add),
                             r=[pd, ofd], w=[osd])
                        ss, ssd = r_ss.next()
                        P.op("act", lambda e, osm=osm, ss=ss: e.activation(out=junkc[0], in_=osm, func=AF.Square,
                                                                           accum_out=ss), r=[osd], w=[ssd, junkc[1]])
                        rstd_from_ss(ss, 512, ssd)
                        gg, ggd = r_gg.next()
                        P.op("dve", lambda e, gg=gg, gr=gr, vs=vs: e.tensor_tensor(out=gg, in0=gr[:, vs], in1=ggr, op=ALU.mult),
                             r=[grd], w=[ggd])
                        ob, obd = r_ob.next()
                        P.op("dve", lambda e, ob=ob, osm=osm, ss=ss, gg=gg: e.scalar_tensor_tensor(
                            out=ob, in0=osm, scalar=ss, in1=gg, op0=ALU.mult, op1=ALU.mult), r=[osd, ssd, ggd], w=[obd])
                        pt_, ptd_ = PS_G.next()
                        pb = pt_.bitcast(BF16)
                        for i in range(4):
                            P.op("pe", lambda e, pb=pb, ob=ob, i=i: e.transpose(out=pb[:, i * 128:(i + 1) * 128],
                                                                                in_=ob[:, i * 128:(i + 1) * 128],
                                                                                identity=ident_b), r=[obd], w=[ptd_])
                        obst, obsd = r_obst.next()
                        P.op("act", lambda e, pb=pb, obst=obst: e.copy(out=obst, in_=pb[:, 0:512].rearrange(
                            "p (a b) -> p a b", b=128)), r=[ptd_], w=[obsd])
                        P.dma("pool", OBT[hd * 4:(hd + 1) * 4, :, t0:t0 + 128].rearrange("k p t -> p k t"), obst,
                              r=[obsd], w=[obdep])
            P.barrier()
        pump_rest(rest_left())
        P.barrier()
    if "stopC" in dbg:
        P.finish()
        P.emit()
        es.close()
        return nc

    bump[0] = base_mark0
    PSD = Ring(PS.aps)
    if True:
        CWD = min(512, D)
        NJ = CWD // 128
        oat = sb("d_oat", (128, KCB, TT), BF16)
        obt = sb("d_obt", (128, KCG, TT), BF16)
        mT = sb("d_mT", (128, KC, TT), BF16)
        oat_d, obt_d, mT_d = Dep(), Dep(), Dep()
        r_wbr = Ring([sb("d_wbr%d" % i, (128, max(KCB, KCG), CWD), BF16) for i in range(2)])
        r_wo = Ring([sb("d_wo%d" % i, (128, KC, CWD), BF16) for i in range(2)])
        r_sg = Ring([sb("d_sg%d" % i, (128, TT), BF16) for i in range(4)])
        r_t = Ring([sb("d_t%d" % i, (128, TT)) for i in range(4)])
        r_xp = Ring([sb("d_xp%d" % i, (128, CWD)) for i in range(3)])
        x1dep = Dep()
        for tt in range(NTT):
            t0 = tt * TT
            P.dma("sp", oat, OAT[:, :, t0:t0 + TT].rearrange("k p t -> p k t"), w=[oat_d])
            P.dma("sp", obt, OBT[:, :, t0:t0 + TT].rearrange("k p t -> p k t"), w=[obt_d])
            for cg in range(D // CWD):
                wa_, wad = r_wbr.next()
                wb_, wbd = r_wbr.next()
                P.dma("sp", wa_[:, 0:KCB, :], WB["bra"][cg], r=[WBD[("bra", cg)]], w=[wad])
                P.dma("sp", wb_[:, 0:KCG, :], WB["brb"][cg], r=[WBD[("brb", cg)]], w=[wbd])
                for j in range(NJ):
                    ct = cg * NJ + j
                    sga_, sgad = r_sg.next()
                    sgb_, sgbd = r_sg.next()
                    P.dma("sp", sga_, SGA[ct, :, t0:t0 + TT], w=[sgad])
                    P.dma("sp", sgb_, SGB[ct, :, t0:t0 + TT], w=[sgbd])
                    pa, pd = PSD.next()
                    mm_group(pa, pd, [wa_[:, kc, j * 128:(j + 1) * 128] for kc in range(KCB)],
                             [oat[:, kc, :] for kc in range(KCB)], [wad, oat_d])
                    pb_, pbd = PSD.next()
                    mm_group(pb_, pbd, [wb_[:, kc, j * 128:(j + 1) * 128] for kc in range(KCG)],
                             [obt[:, kc, :] for kc in range(KCG)], [wbd, obt_d])
                    t1, t1d = r_t.next()
                    t2, t2d = r_t.next()
                    P.op("dve", lambda e, t1=t1, pa=pa, sga_=sga_: e.tensor_tensor(out=t1, in0=pa, in1=sga_, op=ALU.mult),
                         r=[pd, sgad], w=[t1d])
                    P.op("dve", lambda e, t2=t2, pb_=pb_, sgb_=sgb_: e.tensor_tensor(out=t2, in0=pb_, in1=sgb_, op=ALU.mult),
                         r=[pbd, sgbd], w=[t2d])
                    P.op("pool", lambda e, t1=t1, t2=t2, ct=ct: e.tensor_tensor(out=mT[:, ct, :], in0=t1, in1=t2, op=ALU.add),
                         r=[t1d, t2d], w=[mT_d])
            for cg in range(D // CWD):
                wo_, wod = r_wo.next()
                P.dma("sp", wo_, WB["out"][cg], r=[WBD[("out", cg)]], w=[wod])
                for b in range(TT // 128):
                    r0 = t0 + b * 128
                    xp, xpd = r_xp.next()
                    P.dma("sp", xp, x_in[r0:r0 + 128, cg * CWD:(cg + 1) * CWD], w=[xpd])
                    pa, pd = PSD.next()
                    mm_group(pa[:, 0:CWD], pd, [mT[:, kc, b * 128:(b + 1) * 128] for kc in range(KC)],
                             [wo_[:, kc, :] for kc in range(KC)], [wod, mT_d])
                    P.op("dve", lambda e, xp=xp, pa=pa: e.tensor_tensor(out=xp, in0=pa[:, 0:CWD], in1=xp, op=ALU.add),
                         r=[pd, xpd], w=[xpd])
                    P.dma("pool", X1[r0:r0 + 128, cg * CWD:(cg + 1) * CWD], xp, r=[xpd], w=[x1dep])
        P.barrier()
    if "stopD" in dbg:
        P.finish()
        P.emit()
        es.close()
        return nc

    bump[0] = base_mark0
    if True:
        NH = NTT - 1
        OG = min(4, D // 128)
        act = sb("e_act", (128, FT, TT), BF16)
        h2T = sb("e_h2T", (128, KC, TT), BF16)
        h2Th = sb("e_h2Th", (128, KC, 16), BF16)
        gT = sb("e_gT", (128, KC))
        cw = sb("e_cw", (128, 4, FT))
        AH = sb("e_ah", (128, FT, 16))
        act_d, h2T_d, h2Th_d, ah_d = Dep(), Dep(), Dep(), Dep()
        r_mark = bump[0]
        crow = sb("e_crow", (FT, 4, 128))
        grow = sb("e_grow", (KC, 128))
        sd0 = Dep()
        for k in range(3):
            P.dma("sp", crow[:, k, :], conv_w[k:k + 1, :].rearrange("o (c p) -> (o c) p", p=128), w=[sd0])
        P.dma("sp", crow[:, 3, :], conv_b.rearrange("o (c p) -> (o c) p", p=128), w=[sd0])
        P.dma("sp", grow, g_ffn.rearrange("o (c p) -> (o c) p", p=128), w=[sd0])
        for k in range(4):
            pa, pd = PSD.next()
            P.op("pe", lambda e, pa=pa, k=k: e.transpose(out=pa[:, 0:FT], in_=crow[:, k, :], identity=ident_f[0:FT, 0:FT]),
                 r=[sd0], w=[pd])
            P.op("dve", lambda e, pa=pa, k=k: e.tensor_copy(out=cw[:, k, :], in_=pa[:, 0:FT]), r=[pd], w=[sd0])
        pa, pd = PSD.next()
        P.op("pe", lambda e, pa=pa: e.transpose(out=pa[:, 0:KC], in_=grow, identity=ident_f[0:KC, 0:KC]), r=[sd0], w=[pd])
        P.op("dve", lambda e, pa=pa: e.tensor_copy(out=gT, in_=pa[:, 0:KC]), r=[pd], w=[sd0])
        P.op("dve", lambda e: e.memset(AH, 0.0), w=[ah_d])
        P.barrier()
        bump[0] = r_mark + 46 * 1024
        xt1 = Ring([sb("e_xt", (128, D))])
        xn1 = Ring([sb("e_xn", (128, D), BF16)])
        ss1 = Ring([sb("e_ss%d" % i, (128, 1)) for i in range(2)])
        e_top = bump[0]
        if NH > 0:
            X1v = X1.rearrange("(j t) d -> j t d", t=TT)

            def halo_loader(xa, xd):
                P.op("dve", lambda e: e.memset(xa[0:16, :], 0.0), w=[xd])
                P.dma("sp", xa[0:NH, :], X1v[0:NH, TT - 1, :], r=[x1dep], w=[xd])
                P.dma("sp", xa[8:8 + NH, :], X1v[1:NH + 1, 0, :], r=[x1dep], w=[xd])
            norm_transpose_block(halo_loader, None, gT, xt1, xn1, ss1, h2Th, h2Th_d, 0, width=16, psring=PSD)
        ydep = Dep()

        def emit_E1(tt, blocks=None, q="sp"):
            t0_ = tt * TT
            for b in (range(TT // 128) if blocks is None else blocks):
                norm_transpose_block(lambda xa, xd, r0=t0_ + b * 128: P.dma(q, xa, X1[r0:r0 + 128, :], r=[x1dep], w=[xd]),
                                     None, gT, xt1, xn1, ss1, h2T, h2T_d, b * 128, psring=PSD)
        emit_E1(0)
        HF = FT // 2 if FT % 2 == 0 else FT
        NHF = FT // HF
        OG = min(2, D // 128)
        for tt in range(NTT):
            t0 = tt * TT
            bump[0] = r_mark
            r_wu = Ring([sb("e_wu%d" % i, (128, KC, 128), BF16) for i in range(4)])
            r_c = Ring([sb("e_c%d" % i, (128, TT)) for i in range(2)])
            r_u = Ring([sb("e_u%d" % i, (128, TT)) for i in range(2)])
            assert bump[0] <= r_mark + 46 * 1024
            for ct in range(FT):
                wa_, wad = r_wu.next()
                wg_, wgd = r_wu.next()
                P.dma("sp", wa_, WB["upa"][ct], r=[WBD[("upa", ct)]], w=[wad])
                P.dma("sp", wg_, WB["upg"][ct], r=[WBD[("upg", ct)]], w=[wgd])
                pa, pd = PSD.next()
                mm_group(pa, pd, [wa_[:, kc, :] for kc in range(KC)], [h2T[:, kc, :] for kc in range(KC)], [wad, h2T_d])
                pg_, pgd = PSD.next()
                mm_group(pg_, pgd, [wg_[:, kc, :] for kc in range(KC)], [h2T[:, kc, :] for kc in range(KC)], [wgd, h2T_d])
                if tt == 0 and NH > 0:
                    ph_, phd = PSD.next()
                    mm_group(ph_[:, 0:16], phd, [wa_[:, kc, :] for kc in range(KC)], [h2Th[:, kc, :] for kc in range(KC)],
                             [wad, h2Th_d])
                    P.op("dve", lambda e, ph_=ph_, ct=ct: e.tensor_tensor(out=AH[:, ct, :], in0=ph_[:, 0:16],
                                                                         in1=flags[:, 16:32], op=ALU.mult),
                         r=[phd], w=[ah_d])
                c_, cd = r_c.next()
                u_, ud = r_u.next()
                P.op("act", lambda e, c_=c_, pa=pa, ct=ct: e.activation(out=c_, in_=pa, func=AF.Identity,
                                                                        bias=cw[:, 3, ct:ct + 1], scale=cw[:, 1, ct:ct + 1]),
                     r=[pd], w=[cd])
                P.op("dve", lambda e, c_=c_, pa=pa, ct=ct: e.scalar_tensor_tensor(
                    out=c_[:, 1:TT], in0=pa[:, 0:TT - 1], scalar=cw[:, 0, ct:ct + 1], in1=c_[:, 1:TT],
                    op0=ALU.mult, op1=ALU.add), r=[pd, cd], w=[cd])
                P.op("dve", lambda e, c_=c_, pa=pa, ct=ct: e.scalar_tensor_tensor(
                    out=c_[:, 0:TT - 1], in0=pa[:, 1:TT], scalar=cw[:, 2, ct:ct + 1], in1=c_[:, 0:TT - 1],
                    op0=ALU.mult, op1=ALU.add), r=[pd, cd], w=[cd])
                if tt >= 1:
                    P.op("dve", lambda e, c_=c_, ct=ct, i=tt - 1: e.scalar_tensor_tensor(
                        out=c_[:, 0:1], in0=AH[:, ct, i:i + 1], scalar=cw[:, 0, ct:ct + 1], in1=c_[:, 0:1],
                        op0=ALU.mult, op1=ALU.add), r=[ah_d, cd], w=[cd])
                if tt <= NTT - 2:
                    P.op("dve", lambda e, c_=c_, ct=ct, i=8 + tt: e.scalar_tensor_tensor(
                        out=c_[:, TT - 1:TT], in0=AH[:, ct, i:i + 1], scalar=cw[:, 2, ct:ct + 1], in1=c_[:, TT - 1:TT],
                        op0=ALU.mult, op1=ALU.add), r=[ah_d, cd], w=[cd])
                P.op("dve", lambda e, c_=c_, u_=u_: e.tensor_tensor(out=u_, in0=c_, in1=c_, op=ALU.mult), r=[cd], w=[ud])
                P.op("dve", lambda e, u_=u_: e.tensor_scalar(out=u_, in0=u_, scalar1=0.044715, scalar2=1.0,
                                                             op0=ALU.mult, op1=ALU.add), r=[ud], w=[ud])
                P.op("dve", lambda e, c_=c_, u_=u_: e.tensor_tensor(out=u_, in0=u_, in1=c_, op=ALU.mult), r=[cd, ud], w=[ud])
                P.op("act", lambda e, u_=u_: e.activation(out=u_, in_=u_, func=AF.Sigmoid, scale=1.5957691216057308),
                     r=[ud], w=[ud])
                P.op("dve", lambda e, c_=c_, u_=u_: e.tensor_tensor(out=u_, in0=u_, in1=c_, op=ALU.mult), r=[cd, ud], w=[ud])
                P.op("dve", lambda e, u_=u_, pg_=pg_, ct=ct: e.tensor_tensor(out=act[:, ct, :], in0=u_, in1=pg_, op=ALU.mult),
                     r=[ud, pgd], w=[act_d])
            P.barrier()
            bump[0] = r_mark
            r_wd = Ring([sb("e_wd%d" % i, (128, HF, 128), BF16) for i in range(3 if NHF == 2 else 2)])
            r_yt = Ring([sb("e_yt%d" % i, (128, TT)) for i in range(2)])
            r_yio = Ring([sb("e_yio%d" % i, (128, TT // 128, OG * 128)) for i in range(2)])
            assert bump[0] <= r_mark + 46 * 1024, bump[0] - r_mark
            NOG = D // (OG * 128)
            for og in range(NOG):
                if tt + 1 < NTT and NOG >= 1 + TT // 128 and 1 <= og <= TT // 128:
                    emit_E1(tt + 1, blocks=[og - 1], q="act")
                yio, yiod = r_yio.next()
                cs = slice(og * OG * 128, (og + 1) * OG * 128)
                P.dma("sp", yio, X1[t0:t0 + TT, cs].rearrange("(b p) c -> p b c", p=128), r=[x1dep], w=[yiod])
                for oi in range(OG):
                    ot = og * OG + oi
                    halves = []
                    for hf in range(NHF):
                        wd_, wdd = r_wd.next()
                        P.dma("sp", wd_, WB["down"][ot][:, hf * HF:(hf + 1) * HF, :], r=[WBD[("down", ot)]], w=[wdd])
                        halves.append((wd_, wdd))
                    py, pyd = PSD.next()
                    mm_group(py, pyd, [halves[kc // HF][0][:, kc % HF, :] for kc in range(FT)],
                             [act[:, kc, :] for kc in range(FT)], [hh[1] for hh in halves] + [act_d])
                    yt, ytd = r_yt.next()
                    P.op("act", lambda e, yt=yt, py=py: e.copy(out=yt, in_=py), r=[pyd], w=[ytd])
                    pT, pTd = PSD.next()
                    for b in range(TT // 128):
                        P.op("pe", lambda e, pT=pT, yt=yt, b=b: e.transpose(out=pT[:, b * 128:(b + 1) * 128],
                                                                            in_=yt[:, b * 128:(b + 1) * 128],
                                                                            identity=ident_f), r=[ytd], w=[pTd])
                    P.op("dve", lambda e, pT=pT, yio=yio, oi=oi: e.tensor_tensor(
                        out=yio[:, :, oi * 128:(oi + 1) * 128], in0=pT.rearrange("p (b c) -> p b c", c=128),
                        in1=yio[:, :, oi * 128:(oi + 1) * 128], op=ALU.add), r=[pTd, yiod], w=[yiod])
                P.dma("pool", y_out[t0:t0 + TT, cs].rearrange("(b p) c -> p b c", p=128), yio, r=[yiod], w=[ydep])
            if tt + 1 < NTT and NOG < 1 + TT // 128:
                emit_E1(tt + 1)
            P.barrier()
    P.finish()
    P.emit()
    es.close()
    return nc


PHASES = {}


def core_flags_np(cfg, is_sample):
    fl = np.zeros((128, 32), np.float32)
    fl[:, 0] = NEG if is_sample else 0.0
    fl[:, 1] = 0.0 if is_sample else 1.0
    fl[:, 16:32] = 1.0
    ntt = cfg["NT"] // 512
    if is_sample:
        fl[:, 16 + ntt // 2 - 1] = 0.0
        fl[:, 24 + ntt // 2 - 1] = 0.0
    return fl


_NC_CACHE = {}


def kernel(x_prompt, x_sample, rel_bias, g_mix, w_in, q_norm_g, k_norm_g, lambda_q1, lambda_k1, lambda_q2, lambda_k2,
           da_subln_g, w_gate_fwd, b_gate_fwd, w_gate_bwd, b_gate_bwd, gla_norm_g, w_branch_a, w_branch_b, w_out,
           g_ffn, w_up, conv_w, conv_b, w_down):
    cfg = full_cfg()
    f = lambda a: np.ascontiguousarray(np.asarray(a, dtype=np.float32))
    shared = dict(
        rel_bias=f(rel_bias), g_mix=f(g_mix[0:1]), w_in=f(w_in[0]), q_norm_g=f(q_norm_g[0:1]), k_norm_g=f(k_norm_g[0:1]),
        lam4=f(np.concatenate([lambda_q1[0], lambda_k1[0], lambda_q2[0], lambda_k2[0]])[None, :]),
        da_subln_g=f(da_subln_g[0:1]), w_gate_fwd=f(w_gate_fwd[0]), b_gate_fwd=f(b_gate_fwd[0:1]),
        w_gate_bwd=f(w_gate_bwd[0]), b_gate_bwd=f(b_gate_bwd[0:1]), gla_norm_g=f(gla_norm_g[0:1]),
        w_branch_a=f(w_branch_a[0]), w_branch_b=f(w_branch_b[0]), w_out=f(w_out[0]), g_ffn=f(g_ffn[0:1]),
        w_up=f(w_up[0]), conv_w=f(conv_w[0]), conv_b=f(conv_b[0:1]), w_down=f(w_down[0]))
    shared.update(host_consts())
    xp = np.asarray(x_prompt, dtype=np.float32)
    xs = np.asarray(x_sample, dtype=np.float32)
    NT, D = cfg["NT"], cfg["D"]
    in_maps = []
    for c in range(8):
        m = dict(shared)
        if c < 4:
            m["x"] = np.ascontiguousarray(xp[c])
        else:
            m["x"] = np.ascontiguousarray(xs[2 * (c - 4):2 * (c - 4) + 2].reshape(NT, D))
        m["core_flags"] = core_flags_np(cfg, c >= 4)
        in_maps.append(m)
    if "nc" not in _NC_CACHE:
        _NC_CACHE["nc"] = build(cfg)
    res = run_bass_kernel_spmd(_NC_CACHE["nc"], in_maps, core_ids=list(range(8)))
    yp = np.stack([np.asarray(res.results[c]["y"], dtype=np.float32) for c in range(4)])
    ys = np.concatenate([np.asarray(res.results[c]["y"], dtype=np.float32).reshape(2, NT // 2, D) for c in range(4, 8)])
    return (yp, ys)
```

```python
import math
from contextlib import ExitStack
import numpy as np
import concourse.bass as bass
import concourse.mybir as mybir
from concourse.bass_utils import run_bass_kernel_spmd

F32 = mybir.dt.float32
BF16 = mybir.dt.bfloat16
AF = mybir.ActivationFunctionType
ALU = mybir.AluOpType
EPS = 1e-6
NEG = -30000.0
SHIFT = -8.0


class Dep:
    __slots__ = ("w", "r")

    def __init__(self):
        self.w = []
        self.r = []


class Op:
    __slots__ = ("eng", "fn", "deps", "signal", "isdma", "sigidx", "slot", "target")


class Prog:
    ENG = ("pe", "act", "dve", "pool", "sp")

    def __init__(self, nc, nslots=14):
        self.nc = nc
        self.ops = {e: [] for e in self.ENG}
        self.all = []
        self.pending = {e: set() for e in self.ENG}
        self.K = nslots
        self.last_compute = {}
        self.dma_since = []

    def _dep(self, op, x, raw):
        if x is op:
            return
        if x.eng == op.eng and not x.isdma and not op.isdma:
            if op.eng == "pe" or not raw:
                return
        op.deps.add(x)
        x.signal = True

    def _add(self, eng, fn, r, w, isdma):
        op = Op()
        op.eng, op.fn, op.isdma, op.signal, op.deps = eng, fn, isdma, isdma, set()
        op.sigidx = 0
        for d in r:
            for x in d.w:
                self._dep(op, x, True)
        for d in w:
            for x in d.w:
                self._dep(op, x, False)
            for x in d.r:
                self._dep(op, x, False)
        for x in self.pending[eng]:
            if x is not op:
                op.deps.add(x)
                x.signal = True
        self.pending[eng] = set()
        for d in r:
            if (not isdma) and d.r and d.r[-1].eng == eng and not d.r[-1].isdma:
                d.r[-1] = op
            else:
                d.r.append(op)
        for d in w:
            if d.r:
                d.w = [op]
                d.r = []
            elif (not isdma) and d.w and d.w[-1].eng == eng and not d.w[-1].isdma:
                d.w[-1] = op
            else:
                d.w.append(op)
        self.ops[eng].append(op)
        self.all.append(op)
        if isdma:
            self.dma_since.append(op)
        else:
            self.last_compute[eng] = op
        return op

    def op(self, eng, fn, r=(), w=()):
        return self._add(eng, fn, r, w, False)

    def dma(self, q, out, in_, r=(), w=(), **kw):
        return self._add(q, lambda e: e.dma_start(out=out, in_=in_, **kw), r, w, True)

    def barrier(self):
        last = set(self.last_compute.values()) | set(self.dma_since)
        self.dma_since = []
        for e in self.ENG:
            self.pending[e] |= last

    def finish(self):
        self.barrier()
        self._add("sp", None, (), (), False)

    def emit(self):
        nc = self.nc
        cnt = {e: 0 for e in self.ENG}
        dcount = {e: 0 for e in self.ENG}
        slot_last = {}
        for op in self.all:
            if op.isdma:
                s = dcount[op.eng] % self.K
                dcount[op.eng] += 1
                op.slot = (op.eng, s)
                prev = slot_last.get(op.slot)
                op.target = (prev.target if prev else 0) + 16
                if prev is not None:
                    op.deps.add(prev)
                slot_last[op.slot] = op
            elif op.signal:
                cnt[op.eng] += 1
                op.sigidx = cnt[op.eng]
        with ExitStack() as st:
            esem = {e: st.enter_context(nc.semaphore("s_" + e)) for e in self.ENG}
            dsem = {}
            for e in self.ENG:
                for s in range(min(self.K, dcount[e])):
                    dsem[(e, s)] = st.enter_context(nc.semaphore("d_%s%d" % (e, s)))
            block = st.enter_context(nc.Block())

            def replay(e, engine):
                waited = {}
                for op in self.ops[e]:
                    need = {}
                    for x in op.deps:
                        if x.isdma:
                            key, val, sem = ("d",) + x.slot, x.target, dsem[x.slot]
                        else:
                            key, val, sem = ("e", x.eng), x.sigidx, esem[x.eng]
                        if val > need.get(key, (0, None))[0]:
                            need[key] = (val, sem)
                    for key, (val, sem) in need.items():
                        if waited.get(key, 0) < val:
                            engine.wait_ge(sem, val)
                            waited[key] = val
                    if op.fn is None:
                        continue
                    ins = op.fn(engine)
                    if op.isdma:
                        ins.then_inc(dsem[op.slot], 16)
                    elif op.signal:
                        ins.then_inc(esem[e], 1)

            block.sync(lambda eng: replay("sp", eng))
            block.tensor(lambda eng: replay("pe", eng))
            block.scalar(lambda eng: replay("act", eng))
            block.vector(lambda eng: replay("dve", eng))
            block.gpsimd(lambda eng: replay("pool", eng))


class Ring:
    def __init__(self, aps):
        self.aps = aps
        self.deps = [Dep() for _ in aps]
        self.i = 0

    def next(self):
        k = self.i % len(self.aps)
        self.i += 1
        return self.aps[k], self.deps[k]


def full_cfg():
    return dict(D=4096, NT=4096, HA=8, HG=4, DFF=11008)


def t5_bucket_np(rel):
    half, max_exact = 16, 8
    ret = np.where(rel > 0, half, 0)
    n = np.abs(rel)
    nf = np.maximum(n, 1).astype(np.float32)
    large = max_exact + (np.log(nf / np.float32(max_exact)) / np.float32(math.log(128 / max_exact))
                         * np.float32(half - max_exact)).astype(np.int32)
    large = np.minimum(large, half - 1)
    return ret + np.where(n < max_exact, n, large)


def host_consts():
    c = {}
    c["c_ident"] = np.eye(128, dtype=np.float32)
    c["c_flip"] = np.eye(128, dtype=np.float32)[::-1].copy()
    s = np.arange(128)[:, None]
    t = np.arange(128)[None, :]
    same = (s // 64) == (t // 64)
    g = np.zeros((128, 6, 128), np.float32)
    g[:, 0] = (same & (s <= t)) * (-1.0 / 16)
    g[:, 1] = (same & (s > t)) * (-1.0 / 16)
    g[:, 2] = (same & (s >= t)) * (-1.0 / 16)
    g[:, 3] = (same & (s < t)) * (-1.0 / 16)
    g[:, 4] = (same & (s <= t)) * 1.0
    g[:, 5] = (same & (s >= t)) * 1.0
    c["c_gla"] = g
    oh = np.zeros((32, 6, 640), np.float32)
    for di, dl in enumerate(range(-1, 5)):
        i = np.arange(640)
        rel = 128 * dl + 127 - i
        b = t5_bucket_np(rel.astype(np.int32))
        oh[b, di, i] = 1.0
    c["c_onehot"] = oh.reshape(32, 6 * 640)
    return c


def in_groups(cfg):
    HA, HG, D = cfg["HA"], cfg["HG"], cfg["D"]
    QK, GK_, GV_ = HA * 256, HG * 256, HG * 512
    o = 0
    g = []
    for name, n, kind in (("q", QK, "fm_qk"), ("k", QK, "fm_qk"), ("v", QK, "tm"), ("gq", GK_, "fm_plain"),
                          ("gk", GK_, "fm_plain"), ("gv", GV_, "tm"), ("gr", GV_, "tm_silu"),
                          ("glr", 32, "fm_glr"), ("ga", D, "fm_sig"), ("gb", D, "fm_sig")):
        g.append((name, o, n, min(512, n), kind))
        o += n
    return g, o


def build(cfg, dbg=()):
    D, NT, HA, HG, DFF = cfg["D"], cfg["NT"], cfg["HA"], cfg["HG"], cfg["DFF"]
    KC = D // 128
    NB = NT // 128
    TT = 512
    NTT = NT // TT
    NQC = NT // 512
    NKT = NT // 128
    QK, GKW, GVW = HA * 256, HG * 256, HG * 512
    KCB = QK // 128
    KCG = GVW // 128
    FT = DFF // 128
    groups, DIN = in_groups(cfg)
    nc = bass.Bass("TRN2", target_bir_lowering=False)

    def din(name, shape):
        return nc.dram_tensor(name, list(shape), F32, kind="ExternalInput").ap()

    def dscr(name, shape, dt=BF16):
        kind = "ExternalOutput" if name in dbg else "Internal"
        return nc.dram_tensor(name, list(shape), dt, kind=kind).ap()

    x_in = din("x", (NT, D))
    y_out = nc.dram_tensor("y", [NT, D], F32, kind="ExternalOutput").ap()
    rel_bias = din("rel_bias", (32, 8))
    g_mix = din("g_mix", (1, D))
    w_in = din("w_in", (D, DIN))
    q_norm_g = din("q_norm_g", (1, 128))
    k_norm_g = din("k_norm_g", (1, 128))
    lam_in = din("lam4", (1, 512))
    da_subln_g = din("da_subln_g", (1, 256))
    w_gate_f = din("w_gate_fwd", (16, GKW))
    b_gate_f = din("b_gate_fwd", (1, GKW))
    w_gate_b = din("w_gate_bwd", (16, GKW))
    b_gate_b = din("b_gate_bwd", (1, GKW))
    gla_norm_g = din("gla_norm_g", (1, 512))
    w_br_a = din("w_branch_a", (QK, D))
    w_br_b = din("w_branch_b", (GVW, D))
    w_out = din("w_out", (D, D))
    g_ffn = din("g_ffn", (1, D))
    w_up = din("w_up", (D, 2 * DFF))
    conv_w = din("conv_w", (3, DFF))
    conv_b = din("conv_b", (1, DFF))
    w_down = din("w_down", (DFF, D))
    c_ident = din("c_ident", (128, 128))
    c_flip = din("c_flip", (128, 128))
    c_gla = din("c_gla", (128, 6, 128))
    c_onehot = din("c_onehot", (32, 6 * 640))
    core_flags = din("core_flags", (128, 32))

    WB = {}
    for (name, c0, n, CW, kind) in groups:
        WB[name] = dscr("wb_" + name, (n // CW, 128, KC, CW))
    WB["bra"] = dscr("wb_bra", (D // 512 if D >= 512 else 1, 128, KCB, min(512, D)))
    WB["brb"] = dscr("wb_brb", (D // 512 if D >= 512 else 1, 128, KCG, min(512, D)))
    WB["out"] = dscr("wb_out", (D // 512 if D >= 512 else 1, 128, KC, min(512, D)))
    WB["upa"] = dscr("wb_upa", (FT, 128, KC, 128))
    WB["upg"] = dscr("wb_upg", (FT, 128, KC, 128))
    WB["down"] = dscr("wb_down", (D // 128, 128, FT, 128))
    QT = dscr("s_qt", (2 * HA, 128, NT))
    KT = dscr("s_kt", (2 * HA, 128, NT))
    VV = dscr("s_v", (NT, QK))
    GQT = dscr("s_gqt", (2 * HG, 128, NT))
    GKT = dscr("s_gkt", (2 * HG, 128, NT))
    GKM = dscr("s_gk", (NT, GKW))
    GVM = dscr("s_gv", (NT, GVW))
    GRM = dscr("s_gr", (NT, GVW))
    GLT = dscr("s_glt", (32, NT), F32)
    SGA = dscr("s_sga", (KC, 128, NT))
    SGB = dscr("s_sgb", (KC, 128, NT))
    OAT = dscr("s_oat", (KCB, 128, NT))
    OBT = dscr("s_obt", (KCG, 128, NT))
    OFW = dscr("s_of", (NT, GVW), F32)
    X1 = dscr("s_x1", (NT, D), F32)
    UBI = dscr("s_ubias", (8, 6 * 640), F32)

    P = Prog(nc)
    es = ExitStack()

    SB_BYTES = 207 * 1024
    BIG = es.enter_context(nc.sbuf_tensor("big", [128, SB_BYTES // 2], BF16))
    bump = [0]

    def sb(name, shape, dt=F32):
        esz = 4 if dt == F32 else 2
        n = 1
        for v in shape[1:]:
            n *= v
        nb = (n * esz + 63) // 64 * 64
        off = bump[0]
        bump[0] += nb
        assert bump[0] <= SB_BYTES, ("SBUF overflow", name, bump[0])
        v = BIG[0:shape[0], off // 2: off // 2 + n * esz // 2]
        if dt == F32:
            v = v.bitcast(F32)
        if len(shape) == 3:
            v = v.rearrange("p (a b) -> p a b", b=shape[2])
        return v

    banks = [es.enter_context(nc.psum_tensor("ps%d" % i, [128, 512], F32)) for i in range(8)]
    PS = Ring([b[:] for b in banks])

    ident_f = sb("ident_f", (128, 128))
    ident_b = sb("ident_b", (128, 128), BF16)
    flip_f = sb("flip_f", (128, 128))
    ones_b = sb("ones_b", (128, 128), BF16)
    ones_f = sb("ones_f", (128, 128))
    gla_f = sb("gla_f", (128, 6, 128))
    gla_b = sb("gla_b", (128, 4, 128), BF16)
    flags = sb("flags", (128, 32))
    eps_col = sb("eps_col", (128, 1))
    gq_col = sb("gq_col", (128, 2))
    cdep = Dep()
    P.dma("sp", ident_f, c_ident, w=[cdep])
    P.dma("sp", flip_f, c_flip, w=[cdep])
    P.dma("sp", gla_f, c_gla, w=[cdep])
    P.dma("sp", flags, core_flags, w=[cdep])
    P.dma("sp", gq_col[:, 0:1], q_norm_g.rearrange("o d -> d o"), w=[cdep])
    P.dma("sp", gq_col[:, 1:2], k_norm_g.rearrange("o d -> d o"), w=[cdep])
    P.op("dve", lambda e: e.tensor_copy(out=ident_b, in_=ident_f), r=[cdep], w=[cdep])
    P.op("dve", lambda e: e.memset(ones_b, 1.0), w=[cdep])
    P.op("dve", lambda e: e.memset(ones_f, 1.0), w=[cdep])
    P.op("dve", lambda e: e.memset(eps_col, EPS), w=[cdep])
    P.op("dve", lambda e: e.tensor_copy(out=gla_b, in_=gla_f[:, 0:4, :]), r=[cdep], w=[cdep])
    P.op("dve", lambda e: e.tensor_scalar(out=gq_col[:, 0:1], in0=gq_col[:, 0:1], scalar1=128.0 ** -0.5,
                                          scalar2=None, op0=ALU.mult), r=[cdep], w=[cdep])
    P.barrier()

    UE = 2048
    base_mark0 = bump[0]
    p0_s32 = Ring([sb("p0a%d" % i, (128, UE)) for i in range(3)])
    p0_s16 = Ring([sb("p0b%d" % i, (128, UE), BF16) for i in range(3)])
    base_mark = bump[0]
    d4 = Dep()
    _o4 = (base_mark0 + 3 * UE * 4) // 2
    p0_t0 = Ring(list(p0_s32.aps) + [BIG[:, _o4:_o4 + UE * 2].bitcast(F32)])
    p0_t0.deps = list(p0_s32.deps) + [d4]
    WBD = {}
    win_units = []
    rest_units = []

    def plan_weight(src2d, K, c0, ncols, CW, dst, name, out_lists):
        kcn = K // 128
        nk = max(1, min(kcn, UE // CW))
        for cg in range(ncols // CW):
            dep = WBD.setdefault((name, cg), Dep())
            lst = []
            for k0 in range(0, kcn, nk):
                n = min(nk, kcn - k0)
                src = src2d[k0 * 128:(k0 + n) * 128, c0 + cg * CW: c0 + (cg + 1) * CW].rearrange(
                    "(k p) c -> p k c", p=128)
                lst.append((src, dst[cg, :, k0:k0 + n, :], n, CW, dep))
            out_lists.append(lst)

    for (name, c0, n, CW, kind) in groups:
        plan_weight(w_in, D, c0, n, CW, WB[name], name, win_units)
    _tmp = []
    cwd = min(512, D)
    plan_weight(w_br_a, QK, 0, D, cwd, WB["bra"], "bra", _tmp)
    plan_weight(w_br_b, GVW, 0, D, cwd, WB["brb"], "brb", _tmp)
    plan_weight(w_out, D, 0, D, cwd, WB["out"], "out", _tmp)
    _ua, _ug = [], []
    plan_weight(w_up, D, 0, DFF, 128, WB["upa"], "upa", _ua)
    plan_weight(w_up, D, DFF, DFF, 128, WB["upg"], "upg", _ug)
    for a_, g_ in zip(_ua, _ug):
        _tmp.append(a_)
        _tmp.append(g_)
    plan_weight(w_down, DFF, 0, D, 128, WB["down"], "down", _tmp)
    for lst in _tmp:
        rest_units.extend(lst)
    pump_state = {"i": 0, "rest": 0, "win": 0}

    def emit_unit(u, engs):
        src, dstap, n, CW, dep = u
        a32, d32 = p0_s32.next()
        a16, d16 = p0_s16.next()
        P.dma("sp", a32[:, 0:n * CW].rearrange("p (k c) -> p k c", c=CW), src, w=[d32])
        eng = engs[pump_state["i"] % len(engs)]
        pump_state["i"] += 1
        if eng == "act":
            P.op("act", lambda e, o=a16[:, 0:n * CW], i=a32[:, 0:n * CW]: e.copy(out=o, in_=i), r=[d32], w=[d16, d4])
        else:
            P.op(eng, lambda e, o=a16[:, 0:n * CW], i=a32[:, 0:n * CW]: e.tensor_copy(out=o, in_=i), r=[d32], w=[d16, d4])
        P.dma("pool", dstap, a16[:, 0:n * CW].rearrange("p (k c) -> p k c", c=CW), r=[d16], w=[dep])

    def pump_win(upto, engs=("dve", "act")):
        while pump_state["win"] < min(upto, len(win_units)):
            for u in win_units[pump_state["win"]]:
                emit_unit(u, engs)
            pump_state["win"] += 1

    def pump_rest(k, engs=("dve",)):
        for _ in range(k):
            if pump_state["rest"] >= len(rest_units):
                return
            emit_unit(rest_units[pump_state["rest"]], engs)
            pump_state["rest"] += 1

    def rest_left():
        return len(rest_units) - pump_state["rest"]

    def rstd_from_ss(ss_ap, n, dep):
        P.op("act", lambda e: e.activation(out=ss_ap, in_=ss_ap, func=AF.Ln, bias=eps_col[0:ss_ap.shape[0], :],
                                           scale=1.0 / n), r=[dep], w=[dep])
        P.op("act", lambda e: e.activation(out=ss_ap, in_=ss_ap, func=AF.Exp, scale=-0.5), r=[dep], w=[dep])

    def norm_transpose_block(loader, g_rep, gT, xt_ring, xn_ring, ss_ring, hT, hT_dep, col0, width=128):
        xa, xd = xt_ring.next()
        na, nd = xn_ring.next()
        sa, sd = ss_ring.next()
        loader(xa, xd)
        P.op("act", lambda e: e.activation(out=na[0:width, :], in_=xa[0:width, :], func=AF.Square,
                                           accum_out=sa[0:width, :]), r=[xd], w=[sd, nd])
        rstd_from_ss(sa[0:width, :], D, sd)
        if g_rep is not None:
            P.op("dve", lambda e: e.scalar_tensor_tensor(out=na[0:width, :], in0=xa[0:width, :], scalar=sa[0:width, 0:1],
                                                         in1=g_rep[0:width, :], op0=ALU.mult, op1=ALU.mult),
                 r=[xd, sd], w=[nd])
        else:
            P.op("dve", lambda e: e.tensor_scalar(out=na[0:width, :], in0=xa[0:width, :], scalar1=sa[0:width, 0:1],
                                                  scalar2=None, op0=ALU.mult), r=[xd, sd], w=[nd])
        G = min(4, KC)
        for kg in range(KC // G):
            pa, pd = PS.next()
            pb = pa.bitcast(BF16)
            for i in range(G):
                kc = kg * G + i
                P.op("pe", lambda e, o=pb[:, i * 128:i * 128 + width], i_=na[0:width, kc * 128:(kc + 1) * 128]:
                     e.transpose(out=o, in_=i_, identity=ident_b[0:width, 0:width]), r=[nd], w=[pd])
            if g_rep is not None:
                src = pb[:, 0:G * 128].rearrange("p (g t) -> p g t", t=128)[:, :, 0:width]
                dst = hT[:, kg * G:(kg + 1) * G, col0:col0 + width]
                if kg % 2 == 0:
                    P.op("act", lambda e, o=dst, i_=src: e.copy(out=o, in_=i_), r=[pd], w=[hT_dep])
                else:
                    P.op("dve", lambda e, o=dst, i_=src: e.tensor_copy(out=o, in_=i_), r=[pd], w=[hT_dep])
            else:
                for i in range(G):
                    kc = kg * G + i
                    src = pb[:, i * 128:i * 128 + width]
                    dst = hT[:, kc, col0:col0 + width]
                    if i % 2 == 0:
                        P.op("act", lambda e, o=dst, i_=src, kc=kc: e.activation(out=o, in_=i_, func=AF.Copy,
                                                                                 scale=gT[:, kc:kc + 1]),
                             r=[pd], w=[hT_dep])
                    else:
                        P.op("dve", lambda e, o=dst, i_=src, kc=kc: e.tensor_scalar(out=o, in0=i_, scalar1=gT[:, kc:kc + 1],
                                                                                    scalar2=None, op0=ALU.mult),
                             r=[pd], w=[hT_dep])

    def mm_group(out_ps, pd, lhs_list, rhs_list, r):
        n = len(lhs_list)
        for i in range(n):
            P.op("pe", lambda e, l=lhs_list[i], rr=rhs_list[i], s=(i == 0), t=(i == n - 1):
                 e.matmul(out_ps, lhsT=l, rhs=rr, start=s, stop=t), r=r, w=[pd])

    bump[0] = base_mark
    if True:
        psb = sb
        g_rep = psb("a_grep", (128, D))
        xt_ring = Ring([psb("a_xt%d" % i, (128, D)) for i in range(1)])
        xn_ring = Ring([psb("a_xn%d" % i, (128, D), BF16) for i in range(2)])
        ss_ring = Ring([psb("a_ss%d" % i, (128, 1)) for i in range(2)])
        hT = psb("a_hT", (128, KC, TT), BF16)
        hT_dep = Dep()
        wt_ring = Ring([psb("a_wt%d" % i, (128, KC, 512), BF16) for i in range(2)])
        wt_slices = {id(a): [Dep() for _ in range(KC)] for a in wt_ring.aps}

        def cast_cg_into(lst, wa):
            k0 = 0
            sl = wt_slices[id(wa)]
            for ui, (src, dstap, n, CW, dep) in enumerate(lst):
                a32, d32 = p0_t0.next()
                v32 = a32[:, 0:n * CW].rearrange("p (k c) -> p k c", c=CW)
                P.dma("sp", v32, src, w=[d32])
                eng = ("dve", "act")[pump_state["i"] % 2]
                pump_state["i"] += 1
                o = wa[:, k0:k0 + n, 0:CW]
                kd = sl[k0:k0 + n]
                if eng == "act":
                    P.op("act", lambda e, o=o, i=v32: e.copy(out=o, in_=i), r=[d32], w=kd)
                else:
                    P.op(eng, lambda e, o=o, i=v32: e.tensor_copy(out=o, in_=i), r=[d32], w=kd)
                P.dma("pool", dstap, o, r=kd, w=[dep])
                k0 += n
        sq_ring = Ring([psb("a_sq%d" % i, (128, 512), BF16) for i in range(2)])
        rb_ring = Ring([psb("a_rb%d" % i, (128, 512)) for i in range(2)])
        st_ring = Ring([psb("a_st%d" % i, (128, 512), BF16) for i in range(4)])
        st32_ring = Ring([psb("a_st32%d" % i, (32, 512)) for i in range(2)])
        gdep = Dep()
        P.dma("sp", g_rep, g_mix.partition_broadcast(128)[:, 0, :], w=[gdep])
        P.barrier()
        sdep = Dep()
        a_iters = sum(n // CW for (_, _, n, CW, _) in groups)
        a_quota = 0 if NTT <= 1 else -(-(len(rest_units) // 4) // (a_iters * (NTT - 1)))
        a_it = [0]
        for tt in range(NTT):
            t0 = tt * TT
            a_it[0] = 0
            for b in range(TT // 128):
                norm_transpose_block(lambda xa, xd, r0=t0 + b * 128: P.dma("sp", xa, x_in[r0:r0 + 128, :], w=[xd]),
                                     g_rep, None, xt_ring, xn_ring, ss_ring, hT, hT_dep, b * 128)
            for (name, c0, n, CW, kind) in groups:
                for cg in range(n // CW):
                    wa, wd = wt_ring.next()
                    wv = wa[:, :, 0:CW]
                    wsl = wt_slices[id(wa)]
                    if tt == 0:
                        cast_cg_into(win_units[a_it[0]], wa)
                    else:
                        pump_rest(a_quota)
                        P.dma("sp", wv, WB[name][cg], r=[WBD[(name, cg)]], w=[wd] + wsl)
                    a_it[0] += 1
                    wd = [wd] + wsl
                    if kind.startswith("tm"):
                        for b in range(TT // 128):
                            pa, pd = PS.next()
                            mm_group(pa[:, 0:CW], pd, [hT[:, kc, b * 128:(b + 1) * 128] for kc in range(KC)],
                                     [wv[:, kc, :] for kc in range(KC)], [hT_dep] + wd)
                            sa, sd = st_ring.next()
                            fn = AF.Silu if kind == "tm_silu" else AF.Copy
                            if kind == "tm_silu":
                                P.op("act", lambda e, o=sa[:, 0:CW], i_=pa[:, 0:CW]:
                                     e.activation(out=o, in_=i_, func=AF.Silu), r=[pd], w=[sd])
                            else:
                                P.op("dve", lambda e, o=sa[:, 0:CW], i_=pa[:, 0:CW]: e.tensor_copy(out=o, in_=i_),
                                     r=[pd], w=[sd])
                            dst = {"v": VV, "gv": GVM, "gr": GRM}[name]
                            P.dma("pool", dst[t0 + b * 128: t0 + (b + 1) * 128, cg * CW:(cg + 1) * CW], sa[:, 0:CW],
                                  r=[sd], w=[sdep])
                        continue
                    for j in range(max(1, CW // 128)):
                        M = min(128, CW)
                        ct = cg * max(1, CW // 128) + j
                        pa, pd = PS.next()
                        mm_group(pa[0:M, :], pd, [wv[:, kc, j * 128:j * 128 + M] for kc in range(KC)],
                                 [hT[:, kc, :] for kc in range(KC)], [hT_dep] + wd)
                        if kind == "fm_qk":
                            qa, qd = sq_ring.next()
                            P.op("act", lambda e, o=qa, i_=pa: e.activation(out=o, in_=i_, func=AF.Square),
                                 r=[pd], w=[qd])
                            p2, p2d = PS.next()
                            P.op("pe", lambda e, o=p2, i_=qa: e.matmul(o, lhsT=ones_b, rhs=i_, start=True, stop=True),
                                 r=[qd], w=[p2d])
                            ra, rd = rb_ring.next()
                            P.op("act", lambda e, o=ra, i_=p2: e.activation(out=o, in_=i_, func=AF.Ln, bias=eps_col,
                                                                            scale=1.0 / 128), r=[p2d], w=[rd])
                            P.op("act", lambda e, o=ra: e.activation(out=o, in_=o, func=AF.Exp, scale=-0.5),
                                 r=[rd], w=[rd])
                            sa, sd = st_ring.next()
                            gcol = gq_col[:, 0:1] if name == "q" else gq_col[:, 1:2]
                            P.op("dve", lambda e, o=sa, i_=pa, g=gcol, r_=ra:
                                 e.scalar_tensor_tensor(out=o, in0=i_, scalar=g, in1=r_, op0=ALU.mult, op1=ALU.mult),
                                 r=[pd, rd], w=[sd])
                            dst = QT if name == "q" else KT
                            P.dma("pool", dst[ct, :, t0:t0 + TT], sa, r=[sd], w=[sdep])
                        elif kind == "fm_plain":
                            sa, sd = st_ring.next()
                            sc = 256.0 ** -0.5 if name == "gq" else 1.0
                            P.op("act", lambda e, o=sa, i_=pa, s=sc: e.activation(out=o, in_=i_, func=AF.Copy, scale=s),
                                 r=[pd], w=[sd])
                            dst = GQT if name == "gq" else GKT
                            P.dma("pool", dst[ct, :, t0:t0 + TT], sa, r=[sd], w=[sdep])
                        elif kind == "fm_sig":
                            sa, sd = st_ring.next()
                            P.op("act", lambda e, o=sa, i_=pa: e.activation(out=o, in_=i_, func=AF.Sigmoid),
                                 r=[pd], w=[sd])
                            dst = SGA if name == "ga" else SGB
                            P.dma("pool", dst[ct, :, t0:t0 + TT], sa, r=[sd], w=[sdep])
                        elif kind == "fm_glr":
                            sa, sd = st32_ring.next()
                            P.op("dve", lambda e, o=sa, i_=pa[0:32, :]: e.tensor_copy(out=o, in_=i_), r=[pd], w=[sd])
                            P.dma("pool", GLT[:, t0:t0 + TT], sa, r=[sd], w=[sdep])
            name, c0, n, CW, kind = [g for g in groups if g[0] == "gk"][0]
            for cg in range(n // CW):
                wa, wd = wt_ring.next()
                wv = wa[:, :, 0:CW]
                P.dma("sp", wv, WB[name][cg], r=[WBD[(name, cg)]], w=[wd] + wt_slices[id(wa)])
                for b in range(TT // 128):
                    pa, pd = PS.next()
                    mm_group(pa[:, 0:CW], pd, [hT[:, kc, b * 128:(b + 1) * 128] for kc in range(KC)],
                             [wv[:, kc, :] for kc in range(KC)], [hT_dep, wd] + wt_slices[id(wa)])
                    sa, sd = st_ring.next()
                    P.op("dve", lambda e, o=sa[:, 0:CW], i_=pa[:, 0:CW]: e.tensor_copy(out=o, in_=i_), r=[pd], w=[sd])
                    P.dma("pool", GKM[t0 + b * 128: t0 + (b + 1) * 128, cg * CW:(cg + 1) * CW], sa[:, 0:CW],
                          r=[sd], w=[sdep])
        P.barrier()
    if "stopA" in dbg:
        P.finish()
        P.emit()
        es.close()
        return nc

    bump[0] = base_mark
    if True:
        PS_O = Ring(PS.aps[0:4])
        PS_S = Ring(PS.aps[4:7])
        PS_T = Ring(PS.aps[7:8])
        lamv = sb("b_lamv", (1, 512))
        lamp = sb("b_lamp", (1, 512))
        lams = sb("b_lams", (1, 4))
        neglam = sb("b_neglam", (128, 1))
        gsub = sb("b_gsub", (128, 256))
        relb = sb("b_relb", (32, 8))
        fbl = sb("b_fbl", (128, 2, 8))
        fb4 = sb("b_fb4", (128, 8, 4))
        BT = sb("b_bt", (128, 8, 512))
        th_ring = Ring([sb("b_th%d" % i, (128, 512)) for i in range(2)])
        _mk = bump[0]
        onehot = sb("b_onehot", (32, 6 * 640))
        ub_sb = sb("b_ubsb", (8, 6 * 640))
        bump[0] = _mk
        q_ring = Ring([sb("b_q%d" % i, (128, 2, NT), BF16) for i in range(2)])
        k_ring = Ring([sb("b_k%d" % i, (128, 2, NT), BF16) for i in range(2)])
        v_ring = Ring([sb("b_v%d" % i, (128, NKT, 257), BF16) for i in range(2)])
        PT = Ring([sb("b_pt%d" % i, (128, 512), BF16) for i in range(3)])
        TMP = Ring([sb("b_tmp%d" % i, (128, 512)) for i in range(2)])
        OA = sb("b_oa", (128, 4, 256))
        oa_dep = [Dep() for _ in range(4)]
        rz_ring = Ring([sb("b_rz%d" % i, (128, 1)) for i in range(4)])
        ss2_ring = Ring([sb("b_ss%d" % i, (128, 1)) for i in range(4)])
        on_ring = Ring([sb("b_on%d" % i, (128, 256), BF16) for i in range(2)])
        ost_ring = Ring([sb("b_ost%d" % i, (128, 2, 128), BF16) for i in range(2)])
        junkb = (sb("b_junk", (128, 256), BF16), Dep())
        sd0 = Dep()
        P.dma("sp", lamv, lam_in, w=[sd0])
        P.dma("sp", gsub, da_subln_g.partition_broadcast(128)[:, 0, :], w=[sd0])
        P.dma("sp", relb, rel_bias, w=[sd0])
        P.dma("sp", onehot, c_onehot, w=[sd0])
        P.dma("sp", fbl[:, 0, :], rel_bias[15:16, :].partition_broadcast(128)[:, 0, :], w=[sd0])
        P.dma("sp", fbl[:, 1, :], rel_bias[31:32, :].partition_broadcast(128)[:, 0, :], w=[sd0])
        P.op("dve", lambda e: e.tensor_tensor(out=lamp[:, 0:128], in0=lamv[:, 0:128], in1=lamv[:, 128:256], op=ALU.mult),
             r=[sd0], w=[sd0])
        P.op("dve", lambda e: e.tensor_tensor(out=lamp[:, 128:256], in0=lamv[:, 256:384], in1=lamv[:, 384:512],
                                              op=ALU.mult), r=[sd0], w=[sd0])
        P.op("dve", lambda e: e.tensor_reduce(out=lams[:, 0:2], in_=lamp[:, 0:256].rearrange("p (a b) -> p a b", b=128),
                                              axis=mybir.AxisListType.X, op=ALU.add), r=[sd0], w=[sd0])
        P.op("act", lambda e: e.activation(out=lams[:, 0:2], in_=lams[:, 0:2], func=AF.Exp), r=[sd0], w=[sd0])
        P.op("dve", lambda e: e.tensor_tensor(out=lams[:, 2:3], in0=lams[:, 1:2], in1=lams[:, 0:1], op=ALU.subtract),
             r=[sd0], w=[sd0])
        P.op("dve", lambda e: e.tensor_scalar(out=lams[:, 2:3], in0=lams[:, 2:3], scalar1=-0.2, scalar2=None,
                                              op0=ALU.add), r=[sd0], w=[sd0])
        pa, pd = PS_T.next()
        P.op("pe", lambda e: e.matmul(pa[:, 0:1], lhsT=ones_f[0:1, :], rhs=lams[:, 2:3], start=True, stop=True),
             r=[sd0], w=[pd])
        P.op("dve", lambda e: e.tensor_copy(out=neglam, in_=pa[:, 0:1]), r=[pd], w=[sd0])
        P.op("dve", lambda e: e.tensor_scalar(out=gsub, in0=gsub, scalar1=0.8, scalar2=None, op0=ALU.mult),
             r=[sd0], w=[sd0])
        for v in range(4):
            P.op("dve", lambda e, v=v: e.tensor_scalar(out=fb4[:, :, v], in0=fbl[:, v % 2, :], scalar1=SHIFT,
                                                       scalar2=(flags[:, 0:1] if v >= 2 else 0.0),
                                                       op0=ALU.add, op1=ALU.add), r=[sd0], w=[sd0])
        for ch in range(8):
            pa, pd = PS_S.next()
            P.op("pe", lambda e, pa=pa, ch=ch: e.matmul(pa[0:8, 0:480], lhsT=relb, rhs=onehot[:, ch * 480:(ch + 1) * 480],
                                                        start=True, stop=True), r=[sd0], w=[pd])
            P.op("dve", lambda e, pa=pa, ch=ch: e.tensor_copy(out=ub_sb[:, ch * 480:(ch + 1) * 480], in_=pa[0:8, 0:480]),
                 r=[pd], w=[sd0])
        P.dma("pool", UBI, ub_sb, r=[sd0], w=[sd0])
        P.barrier()
        for i in range(2):
            P.op("dve", lambda e, i=i: e.memset(v_ring.aps[i][:, :, 256:257], 1.0), w=[v_ring.deps[i]])
        bt_dep = Dep()
        odep = Dep()
        b_quota = -(-(rest_left() // 2) // (HA * NQC))
        for h in range(HA):
            qa, qd = q_ring.next()
            ka, kd = k_ring.next()
            va, vd = v_ring.next()
            P.dma("sp", qa, QT[2 * h:2 * h + 2].rearrange("m p t -> p m t"), w=[qd])
            P.dma("sp", ka, KT[2 * h:2 * h + 2].rearrange("m p t -> p m t"), w=[kd])
            P.dma("sp", va[:, :, 0:256], VV[:, h * 256:(h + 1) * 256].rearrange("(j p) c -> p j c", p=128), w=[vd])
            for di in range(6):
                ta, td = th_ring.next()
                src = bass.AP(tensor=UBI.tensor, offset=h * 6 * 640 + di * 640, ap=[[1, 128], [1, 512]])
                P.dma("sp", ta, src, w=[td])
                pa, pd = PS_T.next()
                P.op("pe", lambda e, pa=pa, ta=ta: e.matmul(pa, lhsT=flip_f, rhs=ta, start=True, stop=True),
                     r=[td], w=[pd])
                P.op("dve", lambda e, pa=pa, di=di: e.tensor_scalar(out=BT[:, di, :], in0=pa, scalar1=SHIFT, scalar2=None,
                                                                    op0=ALU.add), r=[pd], w=[bt_dep])
            P.op("dve", lambda e: e.tensor_scalar(out=BT[:, 6, :], in0=BT[:, 5, :], scalar1=flags[:, 0:1], scalar2=None,
                                                  op0=ALU.add), r=[bt_dep], w=[bt_dep])
            P.op("dve", lambda e: e.tensor_scalar(out=BT[:, 7, :], in0=BT[:, 0, :], scalar1=flags[:, 0:1], scalar2=None,
                                                  op0=ALU.add), r=[bt_dep], w=[bt_dep])
            for c in range(NQC):
                pump_rest(b_quota)
                for m in range(2):
                    qm = qa[:, m, c * 512:(c + 1) * 512]
                    O = [PS_O.next() for _ in range(4)]

                    def st_mm(j):
                        sa, sd = PS_S.next()
                        P.op("pe", lambda e, sa=sa, j=j, ka=ka, m=m, qm=qm: e.matmul(sa, lhsT=ka[:, m, j * 128:(j + 1) * 128], rhs=qm,
                                                                  start=True, stop=True), r=[kd, qd], w=[sd])
                        return sa, sd
                    pend = [st_mm(0)]
                    if NKT > 1:
                        pend.append(st_mm(1))
                    for j in range(NKT):
                        sa, sd = pend.pop(0)
                        if j + 2 < NKT:
                            pend.append(st_mm(j + 2))
                        pt, ptd = PT.next()
                        dl = j - 4 * c
                        if -1 <= dl <= 4:
                            var = dl + 1
                            if j == NKT // 2 and c == NQC // 2 - 1:
                                var = 6
                            if j == NKT // 2 - 1 and c == NQC // 2:
                                var = 7
                            ta, td = TMP.next()
                            P.op("dve", lambda e, ta=ta, sa=sa, var=var: e.tensor_tensor(out=ta, in0=sa, in1=BT[:, var, :],
                                                                                        op=ALU.add),
                                 r=[sd, bt_dep], w=[td])
                            P.op("act", lambda e, pt=pt, ta=ta: e.activation(out=pt, in_=ta, func=AF.Exp),
                                 r=[td], w=[ptd])
                        else:
                            idx = (0 if dl < 0 else 1) + (2 if ((j < NKT // 2) != (c < NQC // 2)) else 0)
                            P.op("act", lambda e, pt=pt, sa=sa, idx=idx, h=h: e.activation(out=pt, in_=sa, func=AF.Exp,
                                                                                     bias=fb4[:, h, idx:idx + 1]),
                                 r=[sd], w=[ptd])
                        for qb in range(4):
                            P.op("pe", lambda e, qb=qb, pt=pt, j=j, O=O, va=va: e.matmul(O[qb][0][:, 0:257],
                                                                             lhsT=pt[:, qb * 128:(qb + 1) * 128],
                                                                             rhs=va[:, j, :], start=(j == 0),
                                                                             stop=(j == NKT - 1)),
                                 r=[ptd, vd], w=[O[qb][1]])
                    for qb in range(4):
                        oa_, od_ = O[qb]
                        rz, rzd = rz_ring.next()
                        P.op("dve", lambda e, rz=rz, oa_=oa_: e.reciprocal(out=rz, in_=oa_[:, 256:257]), r=[od_], w=[rzd])
                        if m == 0:
                            P.op("dve", lambda e, rz=rz, oa_=oa_, qb=qb: e.tensor_scalar(out=OA[:, qb, :], in0=oa_[:, 0:256],
                                                                                        scalar1=rz, scalar2=None,
                                                                                        op0=ALU.mult),
                                 r=[od_, rzd], w=[oa_dep[qb]])
                            continue
                        P.op("dve", lambda e, rz=rz: e.tensor_scalar(out=rz, in0=rz, scalar1=neglam, scalar2=None,
                                                                     op0=ALU.mult), r=[rzd], w=[rzd])
                        P.op("dve", lambda e, rz=rz, oa_=oa_, qb=qb: e.scalar_tensor_tensor(
                            out=OA[:, qb, :], in0=oa_[:, 0:256], scalar=rz, in1=OA[:, qb, :], op0=ALU.mult, op1=ALU.add),
                            r=[od_, rzd, oa_dep[qb]], w=[oa_dep[qb]])
                        ss, ssd = ss2_ring.next()
                        P.op("act", lambda e, ss=ss, qb=qb: e.activation(out=junkb[0], in_=OA[:, qb, :], func=AF.Square,
                                                                         accum_out=ss), r=[oa_dep[qb]], w=[ssd, junkb[1]])
                        rstd_from_ss(ss, 256, ssd)
                        on, ond = on_ring.next()
                        P.op("dve", lambda e, on=on, ss=ss, qb=qb: e.scalar_tensor_tensor(
                            out=on, in0=OA[:, qb, :], scalar=ss, in1=gsub, op0=ALU.mult, op1=ALU.mult),
                            r=[oa_dep[qb], ssd], w=[ond])
                        pa, pd = PS_T.next()
                        pb = pa.bitcast(BF16)
                        for i in range(2):
                            P.op("pe", lambda e, i=i, pb=pb, on=on: e.transpose(out=pb[:, i * 128:(i + 1) * 128],
                                                                                in_=on[:, i * 128:(i + 1) * 128],
                                                                                identity=ident_b), r=[ond], w=[pd])
                        osb, osd = ost_ring.next()
                        P.op("act", lambda e, osb=osb, pb=pb: e.copy(out=osb, in_=pb[:, 0:256].rearrange(
                            "p (a b) -> p a b", b=128)), r=[pd], w=[osd])
                        t0 = c * 512 + qb * 128
                        P.dma("pool", OAT[2 * h:2 * h + 2, :, t0:t0 + 128].rearrange("k p t -> p k t"), osb,
                              r=[osd], w=[odep])
        P.barrier()
    if "stopB" in dbg:
        P.finish()
        P.emit()
        es.close()
        return nc

    bump[0] = base_mark
    if True:
        G2 = 2 * HG
        NHB = max(1, GKW // 512)
        HW_ = min(512, GKW)
        PS_G = Ring(PS.aps[0:2])
        PS_PO = Ring(PS.aps[2:6])
        PS_KV = Ring(PS.aps[6:8])
        wgp = [sb("c_wgf", (32, GKW)), sb("c_wgb", (32, GKW))]
        bgp = [sb("c_bgf", (1, GKW)), sb("c_bgb", (1, GKW))]
        ggr = sb("c_ggr", (128, 512))
        S = sb("c_S", (128, G2, 512))
        Sb = sb("c_Sb", (128, G2, 512), BF16)
        s_dep = [Dep() for _ in range(G2)]
        sb_dep = [Dep() for _ in range(G2)]

        def ring2(name, shape, dt=F32, n=2):
            return Ring([sb("%s%d" % (name, i), shape, dt) for i in range(n)])
        r_glt = ring2("c_glt", (32, 128))
        r_gqt = ring2("c_gqt", (128, G2, 128), BF16)
        r_gkt = ring2("c_gkt", (128, G2, 128), BF16)
        r_gk = ring2("c_gk", (128, GKW), BF16)
        r_gv = ring2("c_gv", (128, GVW), BF16)
        r_gr = ring2("c_gr", (128, GVW), BF16)
        r_of = ring2("c_of", (128, GVW))
        r_ls = ring2("c_ls", (128, GKW), n=1)
        r_lb = ring2("c_lb", (128, GKW), BF16)
        r_ep = ring2("c_ep", (128, G2, 128))
        r_em = ring2("c_em", (128, G2, 128))
        r_qd = ring2("c_qd", (128, G2, 128), BF16)
        r_ki = ring2("c_ki", (128, G2, 128), BF16)
        r_ed = ring2("c_ed", (128, GKW), n=1)
        r_kd = ring2("c_kd", (128, GKW), BF16)
        r_at = ring2("c_at", (128, HG, 128), BF16)
        r_ofs = ring2("c_ofs", (128, GVW), n=1)
        r_os = ring2("c_os", (128, 512))
        r_gg = ring2("c_gg", (128, 512))
        r_ob = ring2("c_ob", (128, 512), BF16)
        r_obst = ring2("c_obst", (128, 4, 128), BF16)
        r_ss = ring2("c_ss", (128, 1), n=4)
        junkc = (sb("c_junk", (128, 512), BF16), Dep())
        sd0 = Dep()
        for t in wgp:
            P.op("dve", lambda e, t=t: e.memset(t, 0.0), w=[sd0])
        P.dma("sp", wgp[0][0:16, :], w_gate_f, r=[sd0], w=[sd0])
        P.dma("sp", wgp[1][16:32, :], w_gate_b, r=[sd0], w=[sd0])
        P.dma("sp", bgp[0], b_gate_f, w=[sd0])
        P.dma("sp", bgp[1], b_gate_b, w=[sd0])
        P.dma("sp", ggr, gla_norm_g.partition_broadcast(128)[:, 0, :], w=[sd0])
        P.barrier()
        ofdep = Dep()
        obdep = Dep()
        NCH = NT // 64
        def make_block(di, blk):
            fwd = di == 0
            t0 = blk * 128
            X = {}

            def st_gates():
                glt, gltd = r_glt.next()
                gqt, gqtd = r_gqt.next()
                gkt, gktd = r_gkt.next()
                gk, gkd = r_gk.next()
                gv, gvd = r_gv.next()
                P.dma("sp", glt, GLT[:, t0:t0 + 128], w=[gltd])
                P.dma("sp", gqt, GQT[:, :, t0:t0 + 128].rearrange("k p t -> p k t"), w=[gqtd])
                P.dma("sp", gkt, GKT[:, :, t0:t0 + 128].rearrange("k p t -> p k t"), w=[gktd])
                P.dma("sp", gk, GKM[t0:t0 + 128, :], w=[gkd])
                P.dma("sp", gv, GVM[t0:t0 + 128, :], w=[gvd])
                X.update(gqt=gqt, gqtd=gqtd, gkt=gkt, gktd=gktd, gk=gk, gkd=gkd, gv=gv, gvd=gvd)
                if not fwd:
                    of, ofd = r_of.next()
                    gr, grd = r_gr.next()
                    P.dma("sp", of, OFW[t0:t0 + 128, :], r=[ofdep], w=[ofd])
                    P.dma("sp", gr, GRM[t0:t0 + 128, :], w=[grd])
                    X.update(of=of, ofd=ofd, gr=gr, grd=grd)
                ls, lsd = r_ls.next()
                lb, lbd = r_lb.next()
                for hb in range(NHB):
                    pa, pd = PS_G.next()
                    cs = slice(hb * HW_, (hb + 1) * HW_)
                    P.op("pe", lambda e, pa=pa, cs=cs, glt=glt: e.matmul(pa[:, 0:HW_], lhsT=glt, rhs=wgp[di][:, cs],
                                                                         start=True, stop=False), r=[gltd], w=[pd])
                    P.op("pe", lambda e, pa=pa, cs=cs: e.matmul(pa[:, 0:HW_], lhsT=ones_f[0:1, :], rhs=bgp[di][:, cs],
                                                                start=False, stop=True), w=[pd])
                    P.op("act", lambda e, pa=pa, cs=cs, ls=ls: e.activation(out=ls[:, cs], in_=pa[:, 0:HW_], func=AF.Exp,
                                                                            scale=-1.0), r=[pd], w=[lsd])
                P.op("act", lambda e, ls=ls, lb=lb: e.activation(out=lb, in_=ls, func=AF.Ln, bias=ones_f[:, 0:1]),
                     r=[lsd], w=[lbd])
                X.update(lb=lb, lbd=lbd)

            def st_cum():
                lb, lbd = X["lb"], X["lbd"]
                ep, epd = r_ep.next()
                em, emd = r_em.next()
                qd_, qdd = r_qd.next()
                ki, kid = r_ki.next()
                tri = 0 if fwd else 2
                for g0 in range(0, G2, 4):
                    ng = min(4, G2 - g0)
                    pa, pd = PS_G.next()
                    for i in range(ng):
                        P.op("pe", lambda e, pa=pa, i=i, g=g0 + i: e.matmul(
                            pa[:, i * 128:(i + 1) * 128], lhsT=lb[:, g * 128:(g + 1) * 128], rhs=gla_b[:, tri, :],
                            start=True, stop=True), r=[lbd], w=[pd])
                    pv = pa[:, 0:ng * 128].rearrange("p (a b) -> p a b", b=128)
                    P.op("act", lambda e, pv=pv, g0=g0, ng=ng: e.activation(out=ep[:, g0:g0 + ng, :], in_=pv,
                                                                            func=AF.Exp), r=[pd], w=[epd])
                    P.op("act", lambda e, pv=pv, g0=g0, ng=ng: e.activation(out=em[:, g0:g0 + ng, :], in_=pv,
                                                                            func=AF.Exp, scale=-1.0), r=[pd], w=[emd])
                P.op("dve", lambda e: e.tensor_tensor(out=qd_, in0=X["gqt"], in1=ep, op=ALU.mult),
                     r=[X["gqtd"], epd], w=[qdd])
                P.op("dve", lambda e: e.tensor_tensor(out=ki, in0=X["gkt"], in1=em, op=ALU.mult),
                     r=[X["gktd"], emd], w=[kid])
                X.update(ep=ep, epd=epd, qd_=qd_, qdd=qdd, ki=ki, kid=kid)

            def st_dec():
                lb, lbd = X["lb"], X["lbd"]
                ed, edd = r_ed.next()
                kd_, kdd = r_kd.next()
                ut = 1 if fwd else 3
                for hb in range(NHB):
                    pa, pd = PS_G.next()
                    cs = slice(hb * HW_, (hb + 1) * HW_)
                    P.op("pe", lambda e, pa=pa, cs=cs: e.matmul(pa[:, 0:HW_], lhsT=gla_b[:, ut, :], rhs=lb[:, cs],
                                                                start=True, stop=True), r=[lbd], w=[pd])
                    P.op("act", lambda e, pa=pa, cs=cs: e.activation(out=ed[:, cs], in_=pa[:, 0:HW_], func=AF.Exp),
                         r=[pd], w=[edd])
                P.op("dve", lambda e: e.tensor_tensor(out=kd_, in0=X["gk"], in1=ed, op=ALU.mult),
                     r=[X["gkd"], edd], w=[kdd])
                X.update(kd_=kd_, kdd=kdd)

            def st_att():
                ki, kid, qd_, qdd = X["ki"], X["kid"], X["qd_"], X["qdd"]
                at, atd = r_at.next()
                pa, pd = PS_G.next()
                for hd in range(HG):
                    for dh in range(2):
                        P.op("pe", lambda e, hd=hd, dh=dh: e.matmul(
                            pa[:, hd * 128:(hd + 1) * 128], lhsT=ki[:, hd * 2 + dh, :], rhs=qd_[:, hd * 2 + dh, :],
                            start=(dh == 0), stop=(dh == 1)), r=[kid, qdd], w=[pd])
                mk = 4 if fwd else 5
                for hd in range(HG):
                    P.op("dve", lambda e, hd=hd: e.tensor_tensor(
                        out=at[:, hd, :], in0=pa[:, hd * 128:(hd + 1) * 128], in1=gla_f[:, mk, :], op=ALU.mult),
                        r=[pd], w=[atd])
                X.update(at=at, atd=atd)

            def seq_steps():
                steps = []
                X["po"] = None

                def mk_step(ch, hd):
                    def step():
                        if X["po"] is None:
                            X["po"] = [PS_PO.next() for _ in range(HG)]
                        at, atd, gv, gvd, qd_, qdd, kd_, kdd, ep, epd = (X[k] for k in (
                            "at", "atd", "gv", "gvd", "qd_", "qdd", "kd_", "kdd", "ep", "epd"))
                        n = blk * 2 + ch
                        if hd == 0 and ((fwd and n == NCH // 2) or ((not fwd) and n == NCH // 2 - 1)):
                            for g in range(G2):
                                P.op("dve", lambda e, g=g: e.tensor_scalar(out=S[:, g, :], in0=S[:, g, :],
                                                                           scalar1=flags[:, 1:2], scalar2=None, op0=ALU.mult),
                                     r=[s_dep[g]], w=[s_dep[g]])
                                P.op("act", lambda e, g=g: e.copy(out=Sb[:, g, :], in_=S[:, g, :]), r=[s_dep[g]], w=[sb_dep[g]])
                        rows = slice(ch * 64, ch * 64 + 64)
                        dcol = (ch * 64 + 63) if fwd else (ch * 64)
                        pa, pd = X["po"][hd]
                        vs = slice(hd * 512, (hd + 1) * 512)
                        P.op("pe", lambda e: e.matmul(pa[rows, :], lhsT=at[rows, hd, rows], rhs=gv[rows, vs],
                                                      start=True, stop=False), r=[atd, gvd], w=[pd])
                        for dh in range(2):
                            g = hd * 2 + dh
                            P.op("pe", lambda e, g=g, dh=dh: e.matmul(pa[rows, :], lhsT=qd_[:, g, rows], rhs=Sb[:, g, :],
                                                                      start=False, stop=(dh == 1)),
                                 r=[qdd, sb_dep[g]], w=[pd])
                        for dh in range(2):
                            g = hd * 2 + dh
                            ka_, kvd = PS_KV.next()
                            P.op("pe", lambda e, ka_=ka_, g=g: e.matmul(ka_, lhsT=kd_[rows, g * 128:(g + 1) * 128],
                                                                        rhs=gv[rows, vs], start=True, stop=True),
                                 r=[kdd, gvd], w=[kvd])
                            P.op("dve", lambda e, ka_=ka_, g=g: e.scalar_tensor_tensor(
                                out=S[:, g, :], in0=S[:, g, :], scalar=ep[:, g, dcol:dcol + 1], in1=ka_,
                                op0=ALU.mult, op1=ALU.add), r=[kvd, epd, s_dep[g], sb_dep[g]], w=[s_dep[g]])
                            P.op("act", lambda e, g=g: e.copy(out=Sb[:, g, :], in_=S[:, g, :]), r=[s_dep[g]], w=[sb_dep[g]])
                    return step
                for ch in ((0, 1) if fwd else (1, 0)):
                    for hd in range(HG):
                        steps.append(mk_step(ch, hd))
                return steps

            def epilogue():
                po = X["po"]
                if fwd:
                    ofs, ofsd = r_ofs.next()
                    for hd in range(HG):
                        pa, pd = po[hd]
                        vs = slice(hd * 512, (hd + 1) * 512)
                        if hd % 2 == 0:
                            P.op("act", lambda e, pa=pa, vs=vs: e.copy(out=ofs[:, vs], in_=pa), r=[pd], w=[ofsd])
                        else:
                            P.op("dve", lambda e, pa=pa, vs=vs: e.tensor_copy(out=ofs[:, vs], in_=pa), r=[pd], w=[ofsd])
                    P.dma("pool", OFW[t0:t0 + 128, :], ofs, r=[ofsd], w=[ofdep])
                    return
                of, ofd, gr, grd = X["of"], X["ofd"], X["gr"], X["grd"]
                for hd in range(HG):
                    pa, pd = po[hd]
                    vs = slice(hd * 512, (hd + 1) * 512)
                    osm, osd = r_os.next()
                    P.op("dve", lambda e, pa=pa, osm=osm, vs=vs: e.tensor_tensor(out=osm, in0=pa, in1=of[:, vs], op=ALU.add),
                         r=[pd, ofd], w=[osd])
                    ss, ssd = r_ss.next()
                    P.op("act", lambda e, osm=osm, ss=ss: e.activation(out=junkc[0], in_=osm, func=AF.Square,
                                                                       accum_out=ss), r=[osd], w=[ssd, junkc[1]])
                    rstd_from_ss(ss, 512, ssd)
                    gg, ggd = r_gg.next()
                    P.op("dve", lambda e, gg=gg, vs=vs: e.tensor_tensor(out=gg, in0=gr[:, vs], in1=ggr, op=ALU.mult),
                         r=[grd], w=[ggd])
                    ob, obd = r_ob.next()
                    P.op("dve", lambda e, ob=ob, osm=osm, ss=ss, gg=gg: e.scalar_tensor_tensor(
                        out=ob, in0=osm, scalar=ss, in1=gg, op0=ALU.mult, op1=ALU.mult), r=[osd, ssd, ggd], w=[obd])
                    pt_, ptd_ = PS_G.next()
                    pb = pt_.bitcast(BF16)
                    for i in range(4):
                        P.op("pe", lambda e, pb=pb, ob=ob, i=i: e.transpose(out=pb[:, i * 128:(i + 1) * 128],
                                                                            in_=ob[:, i * 128:(i + 1) * 128],
                                                                            identity=ident_b), r=[obd], w=[ptd_])
                    obst, obsd = r_obst.next()
                    P.op("act", lambda e, pb=pb, obst=obst: e.copy(out=obst, in_=pb[:, 0:512].rearrange(
                        "p (a b) -> p a b", b=128)), r=[ptd_], w=[obsd])
                    P.dma("pool", OBT[hd * 4:(hd + 1) * 4, :, t0:t0 + 128].rearrange("k p t -> p k t"), obst,
                          r=[obsd], w=[obdep])
            return [st_gates, st_cum, st_dec, st_att], seq_steps, epilogue

        for di in range(2):
            fwd = di == 0
            for g in range(G2):
                P.op("dve", lambda e, g=g: e.memset(S[:, g, :], 0.0), w=[s_dep[g]])
                P.op("dve", lambda e, g=g: e.memset(Sb[:, g, :], 0.0), w=[sb_dep[g]])
            blocks = list(range(NB) if fwd else range(NB - 1, -1, -1))
            c_quota = -(-rest_left() // ((2 - di) * NB))
            cur = make_block(di, blocks[0])
            for st in cur[0]:
                st()
            for bi, blk in enumerate(blocks):
                pump_rest(c_quota, engs=("pool",))
                nxt = make_block(di, blocks[bi + 1]) if bi + 1 < len(blocks) else None
                stages = list(nxt[0]) if nxt else []
                steps = cur[1]()
                ns = len(steps)
                for si, step in enumerate(steps):
                    step()
                    want = (si + 1) * 4 // ns
                    while stages and (4 - len(stages)) < want:
                        stages.pop(0)()
                for st in stages:
                    st()
                cur[2]()
                cur = nxt
            P.barrier()
        pump_rest(rest_left())
        P.barrier()
    if "stopC" in dbg:
        P.finish()
        P.emit()
        es.close()
        return nc

    bump[0] = base_mark0
    PSD = Ring(PS.aps)
    if True:
        CWD = min(512, D)
        NJ = CWD // 128
        oat = sb("d_oat", (128, KCB, TT), BF16)
        obt = sb("d_obt", (128, KCG, TT), BF16)
        mT = sb("d_mT", (128, KC, TT), BF16)
        oat_d, obt_d, mT_d = Dep(), Dep(), Dep()
        r_wbr = Ring([sb("d_wbr%d" % i, (128, max(KCB, KCG), CWD), BF16) for i in range(2)])
        r_wo = Ring([sb("d_wo%d" % i, (128, KC, CWD), BF16) for i in range(2)])
        r_sg = Ring([sb("d_sg%d" % i, (128, TT), BF16) for i in range(4)])
        r_t = Ring([sb("d_t%d" % i, (128, TT)) for i in range(4)])
        r_xp = Ring([sb("d_xp%d" % i, (128, CWD)) for i in range(3)])
        x1dep = Dep()
        for tt in range(NTT):
            t0 = tt * TT
            P.dma("sp", oat, OAT[:, :, t0:t0 + TT].rearrange("k p t -> p k t"), w=[oat_d])
            P.dma("sp", obt, OBT[:, :, t0:t0 + TT].rearrange("k p t -> p k t"), w=[obt_d])
            for cg in range(D // CWD):
                wa_, wad = r_wbr.next()
                wb_, wbd = r_wbr.next()
                P.dma("sp", wa_[:, 0:KCB, :], WB["bra"][cg], r=[WBD[("bra", cg)]], w=[wad])
                P.dma("sp", wb_[:, 0:KCG, :], WB["brb"][cg], r=[WBD[("brb", cg)]], w=[wbd])
                for j in range(NJ):
                    ct = cg * NJ + j
                    sga_, sgad = r_sg.next()
                    sgb_, sgbd = r_sg.next()
                    P.dma("sp", sga_, SGA[ct, :, t0:t0 + TT], w=[sgad])
                    P.dma("sp", sgb_, SGB[ct, :, t0:t0 + TT], w=[sgbd])
                    pa, pd = PSD.next()
                    mm_group(pa, pd, [wa_[:, kc, j * 128:(j + 1) * 128] for kc in range(KCB)],
                             [oat[:, kc, :] for kc in range(KCB)], [wad, oat_d])
                    pb_, pbd = PSD.next()
                    mm_group(pb_, pbd, [wb_[:, kc, j * 128:(j + 1) * 128] for kc in range(KCG)],
                             [obt[:, kc, :] for kc in range(KCG)], [wbd, obt_d])
                    t1, t1d = r_t.next()
                    t2, t2d = r_t.next()
                    P.op("dve", lambda e, t1=t1, pa=pa, sga_=sga_: e.tensor_tensor(out=t1, in0=pa, in1=sga_, op=ALU.mult),
                         r=[pd, sgad], w=[t1d])
                    P.op("dve", lambda e, t2=t2, pb_=pb_, sgb_=sgb_: e.tensor_tensor(out=t2, in0=pb_, in1=sgb_, op=ALU.mult),
                         r=[pbd, sgbd], w=[t2d])
                    P.op("pool", lambda e, t1=t1, t2=t2, ct=ct: e.tensor_tensor(out=mT[:, ct, :], in0=t1, in1=t2, op=ALU.add),
                         r=[t1d, t2d], w=[mT_d])
            for cg in range(D // CWD):
                wo_, wod = r_wo.next()
                P.dma("sp", wo_, WB["out"][cg], r=[WBD[("out", cg)]], w=[wod])
                for b in range(TT // 128):
                    r0 = t0 + b * 128
                    xp, xpd = r_xp.next()
                    P.dma("sp", xp, x_in[r0:r0 + 128, cg * CWD:(cg + 1) * CWD], w=[xpd])
                    pa, pd = PSD.next()
                    mm_group(pa[:, 0:CWD], pd, [mT[:, kc, b * 128:(b + 1) * 128] for kc in range(KC)],
                             [wo_[:, kc, :] for kc in range(KC)], [wod, mT_d])
                    P.op("dve", lambda e, xp=xp, pa=pa: e.tensor_tensor(out=xp, in0=pa[:, 0:CWD], in1=xp, op=ALU.add),
                         r=[pd, xpd], w=[xpd])
                    P.dma("pool", X1[r0:r0 + 128, cg * CWD:(cg + 1) * CWD], xp, r=[xpd], w=[x1dep])
        P.barrier()
    if "stopD" in dbg:
        P.finish()
        P.emit()
        es.close()
        return nc

    bump[0] = base_mark0
    if True:
        NH = NTT - 1
        OG = min(4, D // 128)
        act = sb("e_act", (128, FT, TT), BF16)
        h2T = sb("e_h2T", (128, KC, TT), BF16)
        h2Th = sb("e_h2Th", (128, KC, 16), BF16)
        gT = sb("e_gT", (128, KC))
        cw = sb("e_cw", (128, 4, FT))
        AH = sb("e_ah", (128, FT, 16))
        act_d, h2T_d, h2Th_d, ah_d = Dep(), Dep(), Dep(), Dep()
        r_mark = bump[0]
        crow = sb("e_crow", (FT, 4, 128))
        grow = sb("e_grow", (KC, 128))
        sd0 = Dep()
        for k in range(3):
            P.dma("sp", crow[:, k, :], conv_w[k:k + 1, :].rearrange("o (c p) -> (o c) p", p=128), w=[sd0])
        P.dma("sp", crow[:, 3, :], conv_b.rearrange("o (c p) -> (o c) p", p=128), w=[sd0])
        P.dma("sp", grow, g_ffn.rearrange("o (c p) -> (o c) p", p=128), w=[sd0])
        for k in range(4):
            pa, pd = PSD.next()
            P.op("pe", lambda e, pa=pa, k=k: e.transpose(out=pa[:, 0:FT], in_=crow[:, k, :], identity=ident_f[0:FT, 0:FT]),
                 r=[sd0], w=[pd])
            P.op("dve", lambda e, pa=pa, k=k: e.tensor_copy(out=cw[:, k, :], in_=pa[:, 0:FT]), r=[pd], w=[sd0])
        pa, pd = PSD.next()
        P.op("pe", lambda e, pa=pa: e.transpose(out=pa[:, 0:KC], in_=grow, identity=ident_f[0:KC, 0:KC]), r=[sd0], w=[pd])
        P.op("dve", lambda e, pa=pa: e.tensor_copy(out=gT, in_=pa[:, 0:KC]), r=[pd], w=[sd0])
        P.op("dve", lambda e: e.memset(AH, 0.0), w=[ah_d])
        P.barrier()
        bump[0] = r_mark
        xt1 = Ring([sb("e_xt", (128, D))])
        xn1 = Ring([sb("e_xn", (128, D), BF16)])
        ss1 = Ring([sb("e_ss%d" % i, (128, 1)) for i in range(2)])
        if NH > 0:
            X1v = X1.rearrange("(j t) d -> j t d", t=TT)

            def halo_loader(xa, xd):
                P.op("dve", lambda e: e.memset(xa[0:16, :], 0.0), w=[xd])
                P.dma("sp", xa[0:NH, :], X1v[0:NH, TT - 1, :], r=[x1dep], w=[xd])
                P.dma("sp", xa[8:8 + NH, :], X1v[1:NH + 1, 0, :], r=[x1dep], w=[xd])
            norm_transpose_block(halo_loader, None, gT, xt1, xn1, ss1, h2Th, h2Th_d, 0, width=16)
            P.barrier()
        ydep = Dep()
        for tt in range(NTT):
            t0 = tt * TT
            bump[0] = r_mark
            r_wu = Ring([sb("e_wu%d" % i, (128, KC, 128), BF16) for i in range(4)])
            r_c = Ring([sb("e_c%d" % i, (128, TT)) for i in range(2)])
            r_u = Ring([sb("e_u%d" % i, (128, TT)) for i in range(2)])
            xt1 = Ring([sb("e_xt", (128, D))])
            xn1 = Ring([sb("e_xn", (128, D), BF16)])
            ss1 = Ring([sb("e_ss%d" % i, (128, 1)) for i in range(2)])
            for b in range(TT // 128):
                norm_transpose_block(lambda xa, xd, r0=t0 + b * 128: P.dma("sp", xa, X1[r0:r0 + 128, :], r=[x1dep], w=[xd]),
                                     None, gT, xt1, xn1, ss1, h2T, h2T_d, b * 128)
            for ct in range(FT):
                wa_, wad = r_wu.next()
                wg_, wgd = r_wu.next()
                P.dma("sp", wa_, WB["upa"][ct], r=[WBD[("upa", ct)]], w=[wad])
                P.dma("sp", wg_, WB["upg"][ct], r=[WBD[("upg", ct)]], w=[wgd])
                pa, pd = PSD.next()
                mm_group(pa, pd, [wa_[:, kc, :] for kc in range(KC)], [h2T[:, kc, :] for kc in range(KC)], [wad, h2T_d])
                pg_, pgd = PSD.next()
                mm_group(pg_, pgd, [wg_[:, kc, :] for kc in range(KC)], [h2T[:, kc, :] for kc in range(KC)], [wgd, h2T_d])
                if tt == 0 and NH > 0:
                    ph_, phd = PSD.next()
                    mm_group(ph_[:, 0:16], phd, [wa_[:, kc, :] for kc in range(KC)], [h2Th[:, kc, :] for kc in range(KC)],
                             [wad, h2Th_d])
                    P.op("dve", lambda e, ph_=ph_, ct=ct: e.tensor_tensor(out=AH[:, ct, :], in0=ph_[:, 0:16],
                                                                         in1=flags[:, 16:32], op=ALU.mult),
                         r=[phd], w=[ah_d])
                c_, cd = r_c.next()
                u_, ud = r_u.next()
                P.op("act", lambda e, c_=c_, pa=pa, ct=ct: e.activation(out=c_, in_=pa, func=AF.Identity,
                                                                        bias=cw[:, 3, ct:ct + 1], scale=cw[:, 1, ct:ct + 1]),
                     r=[pd], w=[cd])
                P.op("dve", lambda e, c_=c_, pa=pa, ct=ct: e.scalar_tensor_tensor(
                    out=c_[:, 1:TT], in0=pa[:, 0:TT - 1], scalar=cw[:, 0, ct:ct + 1], in1=c_[:, 1:TT],
                    op0=ALU.mult, op1=ALU.add), r=[pd, cd], w=[cd])
                P.op("dve", lambda e, c_=c_, pa=pa, ct=ct: e.scalar_tensor_tensor(
                    out=c_[:, 0:TT - 1], in0=pa[:, 1:TT], scalar=cw[:, 2, ct:ct + 1], in1=c_[:, 0:TT - 1],
                    op0=ALU.mult, op1=ALU.add), r=[pd, cd], w=[cd])
                if tt >= 1:
                    P.op("dve", lambda e, c_=c_, ct=ct, i=tt - 1: e.scalar_tensor_tensor(
                        out=c_[:, 0:1], in0=AH[:, ct, i:i + 1], scalar=cw[:, 0, ct:ct + 1], in1=c_[:, 0:1],
                        op0=ALU.mult, op1=ALU.add), r=[ah_d, cd], w=[cd])
                if tt <= NTT - 2:
                    P.op("dve", lambda e, c_=c_, ct=ct, i=8 + tt: e.scalar_tensor_tensor(
                        out=c_[:, TT - 1:TT], in0=AH[:, ct, i:i + 1], scalar=cw[:, 2, ct:ct + 1], in1=c_[:, TT - 1:TT],
                        op0=ALU.mult, op1=ALU.add), r=[ah_d, cd], w=[cd])
                P.op("dve", lambda e, c_=c_, u_=u_: e.tensor_tensor(out=u_, in0=c_, in1=c_, op=ALU.mult), r=[cd], w=[ud])
                P.op("dve", lambda e, u_=u_: e.tensor_scalar(out=u_, in0=u_, scalar1=0.044715, scalar2=1.0,
                                                             op0=ALU.mult, op1=ALU.add), r=[ud], w=[ud])
                P.op("dve", lambda e, c_=c_, u_=u_: e.tensor_tensor(out=u_, in0=u_, in1=c_, op=ALU.mult), r=[cd, ud], w=[ud])
                P.op("act", lambda e, u_=u_: e.activation(out=u_, in_=u_, func=AF.Sigmoid, scale=1.5957691216057308),
                     r=[ud], w=[ud])
                P.op("dve", lambda e, c_=c_, u_=u_: e.tensor_tensor(out=u_, in0=u_, in1=c_, op=ALU.mult), r=[cd, ud], w=[ud])
                P.op("dve", lambda e, u_=u_, pg_=pg_, ct=ct: e.tensor_tensor(out=act[:, ct, :], in0=u_, in1=pg_, op=ALU.mult),
                     r=[ud, pgd], w=[act_d])
            P.barrier()
            bump[0] = r_mark
            r_wd = Ring([sb("e_wd%d" % i, (128, FT, 128), BF16) for i in range(2)])
            r_yt = Ring([sb("e_yt%d" % i, (128, TT)) for i in range(2)])
            r_yio = Ring([sb("e_yio%d" % i, (128, TT // 128, OG * 128)) for i in range(2)])
            for og in range(D // (OG * 128)):
                yio, yiod = r_yio.next()
                cs = slice(og * OG * 128, (og + 1) * OG * 128)
                P.dma("sp", yio, X1[t0:t0 + TT, cs].rearrange("(b p) c -> p b c", p=128), r=[x1dep], w=[yiod])
                for oi in range(OG):
                    ot = og * OG + oi
                    wd_, wdd = r_wd.next()
                    P.dma("sp", wd_, WB["down"][ot], r=[WBD[("down", ot)]], w=[wdd])
                    py, pyd = PSD.next()
                    mm_group(py, pyd, [wd_[:, kc, :] for kc in range(FT)], [act[:, kc, :] for kc in range(FT)], [wdd, act_d])
                    yt, ytd = r_yt.next()
                    P.op("act", lambda e, yt=yt, py=py: e.copy(out=yt, in_=py), r=[pyd], w=[ytd])
                    pT, pTd = PSD.next()
                    for b in range(TT // 128):
                        P.op("pe", lambda e, pT=pT, yt=yt, b=b: e.transpose(out=pT[:, b * 128:(b + 1) * 128],
                                                                            in_=yt[:, b * 128:(b + 1) * 128],
                                                                            identity=ident_f), r=[ytd], w=[pTd])
                    P.op("dve", lambda e, pT=pT, yio=yio, oi=oi: e.tensor_tensor(
                        out=yio[:, :, oi * 128:(oi + 1) * 128], in0=pT.rearrange("p (b c) -> p b c", c=128),
                        in1=yio[:, :, oi * 128:(oi + 1) * 128], op=ALU.add), r=[pTd, yiod], w=[yiod])
                P.dma("pool", y_out[t0:t0 + TT, cs].rearrange("(b p) c -> p b c", p=128), yio, r=[yiod], w=[ydep])
            P.barrier()
    P.finish()
    P.emit()
    es.close()
    return nc


PHASES = {}


def core_flags_np(cfg, is_sample):
    fl = np.zeros((128, 32), np.float32)
    fl[:, 0] = NEG if is_sample else 0.0
    fl[:, 1] = 0.0 if is_sample else 1.0
    fl[:, 16:32] = 1.0
    ntt = cfg["NT"] // 512
    if is_sample:
        fl[:, 16 + ntt // 2 - 1] = 0.0
        fl[:, 24 + ntt // 2 - 1] = 0.0
    return fl


_NC_CACHE = {}


def kernel(x_prompt, x_sample, rel_bias, g_mix, w_in, q_norm_g, k_norm_g, lambda_q1, lambda_k1, lambda_q2, lambda_k2,
           da_subln_g, w_gate_fwd, b_gate_fwd, w_gate_bwd, b_gate_bwd, gla_norm_g, w_branch_a, w_branch_b, w_out,
           g_ffn, w_up, conv_w, conv_b, w_down):
    cfg = full_cfg()
    f = lambda a: np.ascontiguousarray(np.asarray(a, dtype=np.float32))
    shared = dict(
        rel_bias=f(rel_bias), g_mix=f(g_mix[0:1]), w_in=f(w_in[0]), q_norm_g=f(q_norm_g[0:1]), k_norm_g=f(k_norm_g[0:1]),
        lam4=f(np.concatenate([lambda_q1[0], lambda_k1[0], lambda_q2[0], lambda_k2[0]])[None, :]),
        da_subln_g=f(da_subln_g[0:1]), w_gate_fwd=f(w_gate_fwd[0]), b_gate_fwd=f(b_gate_fwd[0:1]),
        w_gate_bwd=f(w_gate_bwd[0]), b_gate_bwd=f(b_gate_bwd[0:1]), gla_norm_g=f(gla_norm_g[0:1]),
        w_branch_a=f(w_branch_a[0]), w_branch_b=f(w_branch_b[0]), w_out=f(w_out[0]), g_ffn=f(g_ffn[0:1]),
        w_up=f(w_up[0]), conv_w=f(conv_w[0]), conv_b=f(conv_b[0:1]), w_down=f(w_down[0]))
    shared.update(host_consts())
    xp = np.asarray(x_prompt, dtype=np.float32)
    xs = np.asarray(x_sample, dtype=np.float32)
    NT, D = cfg["NT"], cfg["D"]
    in_maps = []
    for c in range(8):
        m = dict(shared)
        if c < 4:
            m["x"] = np.ascontiguousarray(xp[c])
        else:
            m["x"] = np.ascontiguousarray(xs[2 * (c - 4):2 * (c - 4) + 2].reshape(NT, D))
        m["core_flags"] = core_flags_np(cfg, c >= 4)
        in_maps.append(m)
    if "nc" not in _NC_CACHE:
        _NC_CACHE["nc"] = build(cfg)
    res = run_bass_kernel_spmd(_NC_CACHE["nc"], in_maps, core_ids=list(range(8)))
    yp = np.stack([np.asarray(res.results[c]["y"], dtype=np.float32) for c in range(4)])
    ys = np.concatenate([np.asarray(res.results[c]["y"], dtype=np.float32).reshape(2, NT // 2, D) for c in range(4, 8)])
    return (yp, ys)
```

```python
import math
from contextlib import ExitStack
import numpy as np
import concourse.bass as bass
import concourse.mybir as mybir
from concourse.bass_utils import run_bass_kernel_spmd

F32 = mybir.dt.float32
BF16 = mybir.dt.bfloat16
AF = mybir.ActivationFunctionType
ALU = mybir.AluOpType
EPS = 1e-6
NEG = -30000.0
SHIFT = -8.0


class Dep:
    __slots__ = ("w", "r")

    def __init__(self):
        self.w = []
        self.r = []


class Op:
    __slots__ = ("eng", "fn", "deps", "signal", "isdma", "sigidx", "slot", "target")


class Prog:
    ENG = ("pe", "act", "dve", "pool", "sp")

    def __init__(self, nc, nslots=14):
        self.nc = nc
        self.ops = {e: [] for e in self.ENG}
        self.all = []
        self.pending = {e: set() for e in self.ENG}
        self.K = nslots
        self.last_compute = {}
        self.dma_since = []

    def _dep(self, op, x, raw):
        if x is op:
            return
        if x.eng == op.eng and not x.isdma and not op.isdma:
            if op.eng == "pe" or not raw:
                return
        op.deps.add(x)
        x.signal = True

    def _add(self, eng, fn, r, w, isdma):
        op = Op()
        op.eng, op.fn, op.isdma, op.signal, op.deps = eng, fn, isdma, isdma, set()
        op.sigidx = 0
        for d in r:
            for x in d.w:
                self._dep(op, x, True)
        for d in w:
            for x in d.w:
                self._dep(op, x, False)
            for x in d.r:
                self._dep(op, x, False)
        for x in self.pending[eng]:
            if x is not op:
                op.deps.add(x)
                x.signal = True
        self.pending[eng] = set()
        for d in r:
            if (not isdma) and d.r and d.r[-1].eng == eng and not d.r[-1].isdma:
                d.r[-1] = op
            else:
                d.r.append(op)
        for d in w:
            if d.r:
                d.w = [op]
                d.r = []
            elif (not isdma) and d.w and d.w[-1].eng == eng and not d.w[-1].isdma:
                d.w[-1] = op
            else:
                d.w.append(op)
        self.ops[eng].append(op)
        self.all.append(op)
        if isdma:
            self.dma_since.append(op)
        else:
            self.last_compute[eng] = op
        return op

    def op(self, eng, fn, r=(), w=()):
        return self._add(eng, fn, r, w, False)

    def dma(self, q, out, in_, r=(), w=(), **kw):
        return self._add(q, lambda e: e.dma_start(out=out, in_=in_, **kw), r, w, True)

    def barrier(self):
        last = set(self.last_compute.values()) | set(self.dma_since)
        self.dma_since = []
        for e in self.ENG:
            self.pending[e] |= last

    def finish(self):
        self.barrier()
        self._add("sp", None, (), (), False)

    def emit(self):
        nc = self.nc
        cnt = {e: 0 for e in self.ENG}
        dcount = {e: 0 for e in self.ENG}
        slot_last = {}
        for op in self.all:
            if op.isdma:
                s = dcount[op.eng] % self.K
                dcount[op.eng] += 1
                op.slot = (op.eng, s)
                prev = slot_last.get(op.slot)
                op.target = (prev.target if prev else 0) + 16
                if prev is not None:
                    op.deps.add(prev)
                slot_last[op.slot] = op
            elif op.signal:
                cnt[op.eng] += 1
                op.sigidx = cnt[op.eng]
        with ExitStack() as st:
            esem = {e: st.enter_context(nc.semaphore("s_" + e)) for e in self.ENG}
            dsem = {}
            for e in self.ENG:
                for s in range(min(self.K, dcount[e])):
                    dsem[(e, s)] = st.enter_context(nc.semaphore("d_%s%d" % (e, s)))
            block = st.enter_context(nc.Block())

            def replay(e, engine):
                waited = {}
                for op in self.ops[e]:
                    need = {}
                    for x in op.deps:
                        if x.isdma:
                            key, val, sem = ("d",) + x.slot, x.target, dsem[x.slot]
                        else:
                            key, val, sem = ("e", x.eng), x.sigidx, esem[x.eng]
                        if val > need.get(key, (0, None))[0]:
                            need[key] = (val, sem)
                    for key, (val, sem) in need.items():
                        if waited.get(key, 0) < val:
                            engine.wait_ge(sem, val)
                            waited[key] = val
                    if op.fn is None:
                        continue
                    ins = op.fn(engine)
                    if op.isdma:
                        ins.then_inc(dsem[op.slot], 16)
                    elif op.signal:
                        ins.then_inc(esem[e], 1)

            block.sync(lambda eng: replay("sp", eng))
            block.tensor(lambda eng: replay("pe", eng))
            block.scalar(lambda eng: replay("act", eng))
            block.vector(lambda eng: replay("dve", eng))
            block.gpsimd(lambda eng: replay("pool", eng))


class Ring:
    def __init__(self, aps):
        self.aps = aps
        self.deps = [Dep() for _ in aps]
        self.i = 0

    def next(self):
        k = self.i % len(self.aps)
        self.i += 1
        return self.aps[k], self.deps[k]


def full_cfg():
    return dict(D=4096, NT=4096, HA=8, HG=4, DFF=11008)


def t5_bucket_np(rel):
    half, max_exact = 16, 8
    ret = np.where(rel > 0, half, 0)
    n = np.abs(rel)
    nf = np.maximum(n, 1).astype(np.float32)
    large = max_exact + (np.log(nf / np.float32(max_exact)) / np.float32(math.log(128 / max_exact))
                         * np.float32(half - max_exact)).astype(np.int32)
    large = np.minimum(large, half - 1)
    return ret + np.where(n < max_exact, n, large)


def host_consts():
    c = {}
    c["c_ident"] = np.eye(128, dtype=np.float32)
    c["c_flip"] = np.eye(128, dtype=np.float32)[::-1].copy()
    s = np.arange(128)[:, None]
    t = np.arange(128)[None, :]
    same = (s // 64) == (t // 64)
    g = np.zeros((128, 6, 128), np.float32)
    g[:, 0] = (same & (s <= t)) * (-1.0 / 16)
    g[:, 1] = (same & (s > t)) * (-1.0 / 16)
    g[:, 2] = (same & (s >= t)) * (-1.0 / 16)
    g[:, 3] = (same & (s < t)) * (-1.0 / 16)
    g[:, 4] = (same & (s <= t)) * 1.0
    g[:, 5] = (same & (s >= t)) * 1.0
    c["c_gla"] = g
    oh = np.zeros((32, 6, 640), np.float32)
    for di, dl in enumerate(range(-1, 5)):
        i = np.arange(640)
        rel = 128 * dl + 127 - i
        b = t5_bucket_np(rel.astype(np.int32))
        oh[b, di, i] = 1.0
    c["c_onehot"] = oh.reshape(32, 6 * 640)
    return c


def in_groups(cfg):
    HA, HG, D = cfg["HA"], cfg["HG"], cfg["D"]
    QK, GK_, GV_ = HA * 256, HG * 256, HG * 512
    o = 0
    g = []
    for name, n, kind in (("q", QK, "fm_qk"), ("k", QK, "fm_qk"), ("v", QK, "tm"), ("gq", GK_, "fm_plain"),
                          ("gk", GK_, "fm_plain"), ("gv", GV_, "tm"), ("gr", GV_, "tm_silu"),
                          ("glr", 32, "fm_glr"), ("ga", D, "fm_sig"), ("gb", D, "fm_sig")):
        g.append((name, o, n, min(512, n), kind))
        o += n
    return g, o


def build(cfg, dbg=()):
    D, NT, HA, HG, DFF = cfg["D"], cfg["NT"], cfg["HA"], cfg["HG"], cfg["DFF"]
    KC = D // 128
    NB = NT // 128
    TT = 512
    NTT = NT // TT
    NQC = NT // 512
    NKT = NT // 128
    QK, GKW, GVW = HA * 256, HG * 256, HG * 512
    KCB = QK // 128
    KCG = GVW // 128
    FT = DFF // 128
    groups, DIN = in_groups(cfg)
    nc = bass.Bass("TRN2", target_bir_lowering=False)

    def din(name, shape):
        return nc.dram_tensor(name, list(shape), F32, kind="ExternalInput").ap()

    def dscr(name, shape, dt=BF16):
        kind = "ExternalOutput" if name in dbg else "Internal"
        return nc.dram_tensor(name, list(shape), dt, kind=kind).ap()

    x_in = din("x", (NT, D))
    y_out = nc.dram_tensor("y", [NT, D], F32, kind="ExternalOutput").ap()
    rel_bias = din("rel_bias", (32, 8))
    g_mix = din("g_mix", (1, D))
    w_in = din("w_in", (D, DIN))
    q_norm_g = din("q_norm_g", (1, 128))
    k_norm_g = din("k_norm_g", (1, 128))
    lam_in = din("lam4", (1, 512))
    da_subln_g = din("da_subln_g", (1, 256))
    w_gate_f = din("w_gate_fwd", (16, GKW))
    b_gate_f = din("b_gate_fwd", (1, GKW))
    w_gate_b = din("w_gate_bwd", (16, GKW))
    b_gate_b = din("b_gate_bwd", (1, GKW))
    gla_norm_g = din("gla_norm_g", (1, 512))
    w_br_a = din("w_branch_a", (QK, D))
    w_br_b = din("w_branch_b", (GVW, D))
    w_out = din("w_out", (D, D))
    g_ffn = din("g_ffn", (1, D))
    w_up = din("w_up", (D, 2 * DFF))
    conv_w = din("conv_w", (3, DFF))
    conv_b = din("conv_b", (1, DFF))
    w_down = din("w_down", (DFF, D))
    c_ident = din("c_ident", (128, 128))
    c_flip = din("c_flip", (128, 128))
    c_gla = din("c_gla", (128, 6, 128))
    c_onehot = din("c_onehot", (32, 6 * 640))
    core_flags = din("core_flags", (128, 32))

    WB = {}
    for (name, c0, n, CW, kind) in groups:
        WB[name] = dscr("wb_" + name, (n // CW, 128, KC, CW))
    WB["bra"] = dscr("wb_bra", (D // 512 if D >= 512 else 1, 128, KCB, min(512, D)))
    WB["brb"] = dscr("wb_brb", (D // 512 if D >= 512 else 1, 128, KCG, min(512, D)))
    WB["out"] = dscr("wb_out", (D // 512 if D >= 512 else 1, 128, KC, min(512, D)))
    WB["upa"] = dscr("wb_upa", (FT, 128, KC, 128))
    WB["upg"] = dscr("wb_upg", (FT, 128, KC, 128))
    WB["down"] = dscr("wb_down", (D // 128, 128, FT, 128))
    QT = dscr("s_qt", (2 * HA, 128, NT))
    KT = dscr("s_kt", (2 * HA, 128, NT))
    VV = dscr("s_v", (NT, QK))
    GQT = dscr("s_gqt", (2 * HG, 128, NT))
    GKT = dscr("s_gkt", (2 * HG, 128, NT))
    GKM = dscr("s_gk", (NT, GKW))
    GVM = dscr("s_gv", (NT, GVW))
    GRM = dscr("s_gr", (NT, GVW))
    GLT = dscr("s_glt", (32, NT), F32)
    SGA = dscr("s_sga", (KC, 128, NT))
    SGB = dscr("s_sgb", (KC, 128, NT))
    OAT = dscr("s_oat", (KCB, 128, NT))
    OBT = dscr("s_obt", (KCG, 128, NT))
    OFW = dscr("s_of", (NT, GVW), F32)
    X1 = dscr("s_x1", (NT, D), F32)
    UBI = dscr("s_ubias", (8, 6 * 640), F32)

    P = Prog(nc)
    es = ExitStack()

    SB_BYTES = 207 * 1024
    BIG = es.enter_context(nc.sbuf_tensor("big", [128, SB_BYTES // 2], BF16))
    bump = [0]

    def sb(name, shape, dt=F32):
        esz = 4 if dt == F32 else 2
        n = 1
        for v in shape[1:]:
            n *= v
        nb = (n * esz + 63) // 64 * 64
        off = bump[0]
        bump[0] += nb
        assert bump[0] <= SB_BYTES, ("SBUF overflow", name, bump[0])
        v = BIG[0:shape[0], off // 2: off // 2 + n * esz // 2]
        if dt == F32:
            v = v.bitcast(F32)
        if len(shape) == 3:
            v = v.rearrange("p (a b) -> p a b", b=shape[2])
        return v

    banks = [es.enter_context(nc.psum_tensor("ps%d" % i, [128, 512], F32)) for i in range(8)]
    PS = Ring([b[:] for b in banks])

    ident_f = sb("ident_f", (128, 128))
    ident_b = sb("ident_b", (128, 128), BF16)
    flip_f = sb("flip_f", (128, 128))
    ones_b = sb("ones_b", (128, 128), BF16)
    ones_f = sb("ones_f", (128, 128))
    gla_f = sb("gla_f", (128, 6, 128))
    gla_b = sb("gla_b", (128, 4, 128), BF16)
    flags = sb("flags", (128, 32))
    eps_col = sb("eps_col", (128, 1))
    gq_col = sb("gq_col", (128, 2))
    cdep = Dep()
    P.dma("sp", ident_f, c_ident, w=[cdep])
    P.dma("sp", flip_f, c_flip, w=[cdep])
    P.dma("sp", gla_f, c_gla, w=[cdep])
    P.dma("sp", flags, core_flags, w=[cdep])
    P.dma("sp", gq_col[:, 0:1], q_norm_g.rearrange("o d -> d o"), w=[cdep])
    P.dma("sp", gq_col[:, 1:2], k_norm_g.rearrange("o d -> d o"), w=[cdep])
    P.op("dve", lambda e: e.tensor_copy(out=ident_b, in_=ident_f), r=[cdep], w=[cdep])
    P.op("dve", lambda e: e.memset(ones_b, 1.0), w=[cdep])
    P.op("dve", lambda e: e.memset(ones_f, 1.0), w=[cdep])
    P.op("dve", lambda e: e.memset(eps_col, EPS), w=[cdep])
    P.op("dve", lambda e: e.tensor_copy(out=gla_b, in_=gla_f[:, 0:4, :]), r=[cdep], w=[cdep])
    P.op("dve", lambda e: e.tensor_scalar(out=gq_col[:, 0:1], in0=gq_col[:, 0:1], scalar1=128.0 ** -0.5,
                                          scalar2=None, op0=ALU.mult), r=[cdep], w=[cdep])
    P.barrier()

    UE = 2048
    base_mark0 = bump[0]
    p0_s32 = Ring([sb("p0a%d" % i, (128, UE)) for i in range(3)])
    p0_s16 = Ring([sb("p0b%d" % i, (128, UE), BF16) for i in range(3)])
    base_mark = bump[0]
    d4 = Dep()
    _o4 = (base_mark0 + 3 * UE * 4) // 2
    p0_t0 = Ring(list(p0_s32.aps) + [BIG[:, _o4:_o4 + UE * 2].bitcast(F32)])
    p0_t0.deps = list(p0_s32.deps) + [d4]
    WBD = {}
    win_units = []
    rest_units = []

    def plan_weight(src2d, K, c0, ncols, CW, dst, name, out_lists):
        kcn = K // 128
        nk = max(1, min(kcn, UE // CW))
        for cg in range(ncols // CW):
            dep = WBD.setdefault((name, cg), Dep())
            lst = []
            for k0 in range(0, kcn, nk):
                n = min(nk, kcn - k0)
                src = src2d[k0 * 128:(k0 + n) * 128, c0 + cg * CW: c0 + (cg + 1) * CW].rearrange(
                    "(k p) c -> p k c", p=128)
                lst.append((src, dst[cg, :, k0:k0 + n, :], n, CW, dep))
            out_lists.append(lst)

    for (name, c0, n, CW, kind) in groups:
        plan_weight(w_in, D, c0, n, CW, WB[name], name, win_units)
    _tmp = []
    cwd = min(512, D)
    plan_weight(w_br_a, QK, 0, D, cwd, WB["bra"], "bra", _tmp)
    plan_weight(w_br_b, GVW, 0, D, cwd, WB["brb"], "brb", _tmp)
    plan_weight(w_out, D, 0, D, cwd, WB["out"], "out", _tmp)
    _ua, _ug = [], []
    plan_weight(w_up, D, 0, DFF, 128, WB["upa"], "upa", _ua)
    plan_weight(w_up, D, DFF, DFF, 128, WB["upg"], "upg", _ug)
    for a_, g_ in zip(_ua, _ug):
        _tmp.append(a_)
        _tmp.append(g_)
    plan_weight(w_down, DFF, 0, D, 128, WB["down"], "down", _tmp)
    for lst in _tmp:
        rest_units.extend(lst)
    pump_state = {"i": 0, "rest": 0, "win": 0}

    def emit_unit(u, engs):
        src, dstap, n, CW, dep = u
        a32, d32 = p0_s32.next()
        a16, d16 = p0_s16.next()
        P.dma("sp", a32[:, 0:n * CW].rearrange("p (k c) -> p k c", c=CW), src, w=[d32])
        eng = engs[pump_state["i"] % len(engs)]
        pump_state["i"] += 1
        if eng == "act":
            P.op("act", lambda e, o=a16[:, 0:n * CW], i=a32[:, 0:n * CW]: e.copy(out=o, in_=i), r=[d32], w=[d16, d4])
        else:
            P.op(eng, lambda e, o=a16[:, 0:n * CW], i=a32[:, 0:n * CW]: e.tensor_copy(out=o, in_=i), r=[d32], w=[d16, d4])
        P.dma("pool", dstap, a16[:, 0:n * CW].rearrange("p (k c) -> p k c", c=CW), r=[d16], w=[dep])

    def pump_win(upto, engs=("dve", "act")):
        while pump_state["win"] < min(upto, len(win_units)):
            for u in win_units[pump_state["win"]]:
                emit_unit(u, engs)
            pump_state["win"] += 1

    def pump_rest(k, engs=("dve",)):
        for _ in range(k):
            if pump_state["rest"] >= len(rest_units):
                return
            emit_unit(rest_units[pump_state["rest"]], engs)
            pump_state["rest"] += 1

    def rest_left():
        return len(rest_units) - pump_state["rest"]

    def rstd_from_ss(ss_ap, n, dep):
        P.op("act", lambda e: e.activation(out=ss_ap, in_=ss_ap, func=AF.Ln, bias=eps_col[0:ss_ap.shape[0], :],
                                           scale=1.0 / n), r=[dep], w=[dep])
        P.op("act", lambda e: e.activation(out=ss_ap, in_=ss_ap, func=AF.Exp, scale=-0.5), r=[dep], w=[dep])

    def norm_transpose_block(loader, g_rep, gT, xt_ring, xn_ring, ss_ring, hT, hT_dep, col0, width=128):
        xa, xd = xt_ring.next()
        na, nd = xn_ring.next()
        sa, sd = ss_ring.next()
        loader(xa, xd)
        P.op("act", lambda e: e.activation(out=na[0:width, :], in_=xa[0:width, :], func=AF.Square,
                                           accum_out=sa[0:width, :]), r=[xd], w=[sd, nd])
        rstd_from_ss(sa[0:width, :], D, sd)
        if g_rep is not None:
            P.op("dve", lambda e: e.scalar_tensor_tensor(out=na[0:width, :], in0=xa[0:width, :], scalar=sa[0:width, 0:1],
                                                         in1=g_rep[0:width, :], op0=ALU.mult, op1=ALU.mult),
                 r=[xd, sd], w=[nd])
        else:
            P.op("dve", lambda e: e.tensor_scalar(out=na[0:width, :], in0=xa[0:width, :], scalar1=sa[0:width, 0:1],
                                                  scalar2=None, op0=ALU.mult), r=[xd, sd], w=[nd])
        G = min(4, KC)
        for kg in range(KC // G):
            pa, pd = PS.next()
            pb = pa.bitcast(BF16)
            for i in range(G):
                kc = kg * G + i
                P.op("pe", lambda e, o=pb[:, i * 128:i * 128 + width], i_=na[0:width, kc * 128:(kc + 1) * 128]:
                     e.transpose(out=o, in_=i_, identity=ident_b[0:width, 0:width]), r=[nd], w=[pd])
            if g_rep is not None:
                src = pb[:, 0:G * 128].rearrange("p (g t) -> p g t", t=128)[:, :, 0:width]
                dst = hT[:, kg * G:(kg + 1) * G, col0:col0 + width]
                if kg % 2 == 0:
                    P.op("act", lambda e, o=dst, i_=src: e.copy(out=o, in_=i_), r=[pd], w=[hT_dep])
                else:
                    P.op("dve", lambda e, o=dst, i_=src: e.tensor_copy(out=o, in_=i_), r=[pd], w=[hT_dep])
            else:
                for i in range(G):
                    kc = kg * G + i
                    src = pb[:, i * 128:i * 128 + width]
                    dst = hT[:, kc, col0:col0 + width]
                    if i % 2 == 0:
                        P.op("act", lambda e, o=dst, i_=src, kc=kc: e.activation(out=o, in_=i_, func=AF.Copy,
                                                                                 scale=gT[:, kc:kc + 1]),
                             r=[pd], w=[hT_dep])
                    else:
                        P.op("dve", lambda e, o=dst, i_=src, kc=kc: e.tensor_scalar(out=o, in0=i_, scalar1=gT[:, kc:kc + 1],
                                                                                    scalar2=None, op0=ALU.mult),
                             r=[pd], w=[hT_dep])

    def mm_group(out_ps, pd, lhs_list, rhs_list, r):
        n = len(lhs_list)
        for i in range(n):
            P.op("pe", lambda e, l=lhs_list[i], rr=rhs_list[i], s=(i == 0), t=(i == n - 1):
                 e.matmul(out_ps, lhsT=l, rhs=rr, start=s, stop=t), r=r, w=[pd])

    bump[0] = base_mark
    if True:
        psb = sb
        g_rep = psb("a_grep", (128, D))
        xt_ring = Ring([psb("a_xt%d" % i, (128, D)) for i in range(1)])
        xn_ring = Ring([psb("a_xn%d" % i, (128, D), BF16) for i in range(2)])
        ss_ring = Ring([psb("a_ss%d" % i, (128, 1)) for i in range(2)])
        hT = psb("a_hT", (128, KC, TT), BF16)
        hT_dep = Dep()
        wt_ring = Ring([psb("a_wt%d" % i, (128, KC, 512), BF16) for i in range(2)])
        wt_slices = {id(a): [Dep() for _ in range(KC)] for a in wt_ring.aps}

        def cast_cg_into(lst, wa):
            k0 = 0
            sl = wt_slices[id(wa)]
            for ui, (src, dstap, n, CW, dep) in enumerate(lst):
                a32, d32 = p0_t0.next()
                v32 = a32[:, 0:n * CW].rearrange("p (k c) -> p k c", c=CW)
                P.dma("sp", v32, src, w=[d32])
                eng = ("dve", "act")[pump_state["i"] % 2]
                pump_state["i"] += 1
                o = wa[:, k0:k0 + n, 0:CW]
                kd = sl[k0:k0 + n]
                if eng == "act":
                    P.op("act", lambda e, o=o, i=v32: e.copy(out=o, in_=i), r=[d32], w=kd)
                else:
                    P.op(eng, lambda e, o=o, i=v32: e.tensor_copy(out=o, in_=i), r=[d32], w=kd)
                P.dma("pool", dstap, o, r=kd, w=[dep])
                k0 += n
        sq_ring = Ring([psb("a_sq%d" % i, (128, 512), BF16) for i in range(2)])
        rb_ring = Ring([psb("a_rb%d" % i, (128, 512)) for i in range(2)])
        st_ring = Ring([psb("a_st%d" % i, (128, 512), BF16) for i in range(4)])
        st32_ring = Ring([psb("a_st32%d" % i, (32, 512)) for i in range(2)])
        gdep = Dep()
        P.dma("sp", g_rep, g_mix.partition_broadcast(128)[:, 0, :], w=[gdep])
        P.barrier()
        sdep = Dep()
        a_iters = sum(n // CW for (_, _, n, CW, _) in groups)
        a_quota = 0 if NTT <= 1 else -(-(len(rest_units) // 4) // (a_iters * (NTT - 1)))
        a_it = [0]
        for tt in range(NTT):
            t0 = tt * TT
            a_it[0] = 0
            for b in range(TT // 128):
                norm_transpose_block(lambda xa, xd, r0=t0 + b * 128: P.dma("sp", xa, x_in[r0:r0 + 128, :], w=[xd]),
                                     g_rep, None, xt_ring, xn_ring, ss_ring, hT, hT_dep, b * 128)
            for (name, c0, n, CW, kind) in groups:
                for cg in range(n // CW):
                    wa, wd = wt_ring.next()
                    wv = wa[:, :, 0:CW]
                    wsl = wt_slices[id(wa)]
                    if tt == 0:
                        cast_cg_into(win_units[a_it[0]], wa)
                    else:
                        pump_rest(a_quota)
                        P.dma("sp", wv, WB[name][cg], r=[WBD[(name, cg)]], w=[wd] + wsl)
                    a_it[0] += 1
                    wd = [wd] + wsl
                    if kind.startswith("tm"):
                        for b in range(TT // 128):
                            pa, pd = PS.next()
                            mm_group(pa[:, 0:CW], pd, [hT[:, kc, b * 128:(b + 1) * 128] for kc in range(KC)],
                                     [wv[:, kc, :] for kc in range(KC)], [hT_dep] + wd)
                            sa, sd = st_ring.next()
                            fn = AF.Silu if kind == "tm_silu" else AF.Copy
                            if kind == "tm_silu":
                                P.op("act", lambda e, o=sa[:, 0:CW], i_=pa[:, 0:CW]:
                                     e.activation(out=o, in_=i_, func=AF.Silu), r=[pd], w=[sd])
                            else:
                                P.op("dve", lambda e, o=sa[:, 0:CW], i_=pa[:, 0:CW]: e.tensor_copy(out=o, in_=i_),
                                     r=[pd], w=[sd])
                            dst = {"v": VV, "gv": GVM, "gr": GRM}[name]
                            P.dma("pool", dst[t0 + b * 128: t0 + (b + 1) * 128, cg * CW:(cg + 1) * CW], sa[:, 0:CW],
                                  r=[sd], w=[sdep])
                        continue
                    for j in range(max(1, CW // 128)):
                        M = min(128, CW)
                        ct = cg * max(1, CW // 128) + j
                        pa, pd = PS.next()
                        mm_group(pa[0:M, :], pd, [wv[:, kc, j * 128:j * 128 + M] for kc in range(KC)],
                                 [hT[:, kc, :] for kc in range(KC)], [hT_dep] + wd)
                        if kind == "fm_qk":
                            qa, qd = sq_ring.next()
                            P.op("act", lambda e, o=qa, i_=pa: e.activation(out=o, in_=i_, func=AF.Square),
                                 r=[pd], w=[qd])
                            p2, p2d = PS.next()
                            P.op("pe", lambda e, o=p2, i_=qa: e.matmul(o, lhsT=ones_b, rhs=i_, start=True, stop=True),
                                 r=[qd], w=[p2d])
                            ra, rd = rb_ring.next()
                            P.op("act", lambda e, o=ra, i_=p2: e.activation(out=o, in_=i_, func=AF.Ln, bias=eps_col,
                                                                            scale=1.0 / 128), r=[p2d], w=[rd])
                            P.op("act", lambda e, o=ra: e.activation(out=o, in_=o, func=AF.Exp, scale=-0.5),
                                 r=[rd], w=[rd])
                            sa, sd = st_ring.next()
                            gcol = gq_col[:, 0:1] if name == "q" else gq_col[:, 1:2]
                            P.op("dve", lambda e, o=sa, i_=pa, g=gcol, r_=ra:
                                 e.scalar_tensor_tensor(out=o, in0=i_, scalar=g, in1=r_, op0=ALU.mult, op1=ALU.mult),
                                 r=[pd, rd], w=[sd])
                            dst = QT if name == "q" else KT
                            P.dma("pool", dst[ct, :, t0:t0 + TT], sa, r=[sd], w=[sdep])
                        elif kind == "fm_plain":
                            sa, sd = st_ring.next()
                            sc = 256.0 ** -0.5 if name == "gq" else 1.0
                            P.op("act", lambda e, o=sa, i_=pa, s=sc: e.activation(out=o, in_=i_, func=AF.Copy, scale=s),
                                 r=[pd], w=[sd])
                            dst = GQT if name == "gq" else GKT
                            P.dma("pool", dst[ct, :, t0:t0 + TT], sa, r=[sd], w=[sdep])
                        elif kind == "fm_sig":
                            sa, sd = st_ring.next()
                            P.op("act", lambda e, o=sa, i_=pa: e.activation(out=o, in_=i_, func=AF.Sigmoid),
                                 r=[pd], w=[sd])
                            dst = SGA if name == "ga" else SGB
                            P.dma("pool", dst[ct, :, t0:t0 + TT], sa, r=[sd], w=[sdep])
                        elif kind == "fm_glr":
                            sa, sd = st32_ring.next()
                            P.op("dve", lambda e, o=sa, i_=pa[0:32, :]: e.tensor_copy(out=o, in_=i_), r=[pd], w=[sd])
                            P.dma("pool", GLT[:, t0:t0 + TT], sa, r=[sd], w=[sdep])
            name, c0, n, CW, kind = [g for g in groups if g[0] == "gk"][0]
            for cg in range(n // CW):
                wa, wd = wt_ring.next()
                wv = wa[:, :, 0:CW]
                P.dma("sp", wv, WB[name][cg], r=[WBD[(name, cg)]], w=[wd] + wt_slices[id(wa)])
                for b in range(TT // 128):
                    pa, pd = PS.next()
                    mm_group(pa[:, 0:CW], pd, [hT[:, kc, b * 128:(b + 1) * 128] for kc in range(KC)],
                             [wv[:, kc, :] for kc in range(KC)], [hT_dep, wd] + wt_slices[id(wa)])
                    sa, sd = st_ring.next()
                    P.op("dve", lambda e, o=sa[:, 0:CW], i_=pa[:, 0:CW]: e.tensor_copy(out=o, in_=i_), r=[pd], w=[sd])
                    P.dma("pool", GKM[t0 + b * 128: t0 + (b + 1) * 128, cg * CW:(cg + 1) * CW], sa[:, 0:CW],
                          r=[sd], w=[sdep])
        P.barrier()
    if "stopA" in dbg:
        P.finish()
        P.emit()
        es.close()
        return nc

    bump[0] = base_mark
    if True:
        PS_O = Ring(PS.aps[0:4])
        PS_S = Ring(PS.aps[4:7])
        PS_T = Ring(PS.aps[7:8])
        lamv = sb("b_lamv", (1, 512))
        lamp = sb("b_lamp", (1, 512))
        lams = sb("b_lams", (1, 4))
        neglam = sb("b_neglam", (128, 1))
        gsub = sb("b_gsub", (128, 256))
        relb = sb("b_relb", (32, 8))
        fbl = sb("b_fbl", (128, 2, 8))
        fb4 = sb("b_fb4", (128, 8, 4))
        BT = sb("b_bt", (128, 8, 512))
        th_ring = Ring([sb("b_th%d" % i, (128, 512)) for i in range(2)])
        _mk = bump[0]
        onehot = sb("b_onehot", (32, 6 * 640))
        ub_sb = sb("b_ubsb", (8, 6 * 640))
        bump[0] = _mk
        q_ring = Ring([sb("b_q%d" % i, (128, 2, NT), BF16) for i in range(2)])
        k_ring = Ring([sb("b_k%d" % i, (128, 2, NT), BF16) for i in range(2)])
        v_ring = Ring([sb("b_v%d" % i, (128, NKT, 257), BF16) for i in range(2)])
        PT = Ring([sb("b_pt%d" % i, (128, 512), BF16) for i in range(3)])
        TMP = Ring([sb("b_tmp%d" % i, (128, 512)) for i in range(2)])
        OA = sb("b_oa", (128, 4, 256))
        oa_dep = [Dep() for _ in range(4)]
        rz_ring = Ring([sb("b_rz%d" % i, (128, 1)) for i in range(4)])
        ss2_ring = Ring([sb("b_ss%d" % i, (128, 1)) for i in range(4)])
        on_ring = Ring([sb("b_on%d" % i, (128, 256), BF16) for i in range(2)])
        ost_ring = Ring([sb("b_ost%d" % i, (128, 2, 128), BF16) for i in range(2)])
        junkb = (sb("b_junk", (128, 256), BF16), Dep())
        sd0 = Dep()
        P.dma("sp", lamv, lam_in, w=[sd0])
        P.dma("sp", gsub, da_subln_g.partition_broadcast(128)[:, 0, :], w=[sd0])
        P.dma("sp", relb, rel_bias, w=[sd0])
        P.dma("sp", onehot, c_onehot, w=[sd0])
        P.dma("sp", fbl[:, 0, :], rel_bias[15:16, :].partition_broadcast(128)[:, 0, :], w=[sd0])
        P.dma("sp", fbl[:, 1, :], rel_bias[31:32, :].partition_broadcast(128)[:, 0, :], w=[sd0])
        P.op("dve", lambda e: e.tensor_tensor(out=lamp[:, 0:128], in0=lamv[:, 0:128], in1=lamv[:, 128:256], op=ALU.mult),
             r=[sd0], w=[sd0])
        P.op("dve", lambda e: e.tensor_tensor(out=lamp[:, 128:256], in0=lamv[:, 256:384], in1=lamv[:, 384:512],
                                              op=ALU.mult), r=[sd0], w=[sd0])
        P.op("dve", lambda e: e.tensor_reduce(out=lams[:, 0:2], in_=lamp[:, 0:256].rearrange("p (a b) -> p a b", b=128),
                                              axis=mybir.AxisListType.X, op=ALU.add), r=[sd0], w=[sd0])
        P.op("act", lambda e: e.activation(out=lams[:, 0:2], in_=lams[:, 0:2], func=AF.Exp), r=[sd0], w=[sd0])
        P.op("dve", lambda e: e.tensor_tensor(out=lams[:, 2:3], in0=lams[:, 1:2], in1=lams[:, 0:1], op=ALU.subtract),
             r=[sd0], w=[sd0])
        P.op("dve", lambda e: e.tensor_scalar(out=lams[:, 2:3], in0=lams[:, 2:3], scalar1=-0.2, scalar2=None,
                                              op0=ALU.add), r=[sd0], w=[sd0])
        pa, pd = PS_T.next()
        P.op("pe", lambda e: e.matmul(pa[:, 0:1], lhsT=ones_f[0:1, :], rhs=lams[:, 2:3], start=True, stop=True),
             r=[sd0], w=[pd])
        P.op("dve", lambda e: e.tensor_copy(out=neglam, in_=pa[:, 0:1]), r=[pd], w=[sd0])
        P.op("dve", lambda e: e.tensor_scalar(out=gsub, in0=gsub, scalar1=0.8, scalar2=None, op0=ALU.mult),
             r=[sd0], w=[sd0])
        for v in range(4):
            P.op("dve", lambda e, v=v: e.tensor_scalar(out=fb4[:, :, v], in0=fbl[:, v % 2, :], scalar1=SHIFT,
                                                       scalar2=(flags[:, 0:1] if v >= 2 else 0.0),
                                                       op0=ALU.add, op1=ALU.add), r=[sd0], w=[sd0])
        for ch in range(8):
            pa, pd = PS_S.next()
            P.op("pe", lambda e, pa=pa, ch=ch: e.matmul(pa[0:8, 0:480], lhsT=relb, rhs=onehot[:, ch * 480:(ch + 1) * 480],
                                                        start=True, stop=True), r=[sd0], w=[pd])
            P.op("dve", lambda e, pa=pa, ch=ch: e.tensor_copy(out=ub_sb[:, ch * 480:(ch + 1) * 480], in_=pa[0:8, 0:480]),
                 r=[pd], w=[sd0])
        P.dma("pool", UBI, ub_sb, r=[sd0], w=[sd0])
        P.barrier()
        for i in range(2):
            P.op("dve", lambda e, i=i: e.memset(v_ring.aps[i][:, :, 256:257], 1.0), w=[v_ring.deps[i]])
        bt_dep = Dep()
        odep = Dep()
        b_quota = -(-(rest_left() // 2) // (HA * NQC))
        for h in range(HA):
            qa, qd = q_ring.next()
            ka, kd = k_ring.next()
            va, vd = v_ring.next()
            P.dma("sp", qa, QT[2 * h:2 * h + 2].rearrange("m p t -> p m t"), w=[qd])
            P.dma("sp", ka, KT[2 * h:2 * h + 2].rearrange("m p t -> p m t"), w=[kd])
            P.dma("sp", va[:, :, 0:256], VV[:, h * 256:(h + 1) * 256].rearrange("(j p) c -> p j c", p=128), w=[vd])
            for di in range(6):
                ta, td = th_ring.next()
                src = bass.AP(tensor=UBI.tensor, offset=h * 6 * 640 + di * 640, ap=[[1, 128], [1, 512]])
                P.dma("sp", ta, src, w=[td])
                pa, pd = PS_T.next()
                P.op("pe", lambda e, pa=pa, ta=ta: e.matmul(pa, lhsT=flip_f, rhs=ta, start=True, stop=True),
                     r=[td], w=[pd])
                P.op("dve", lambda e, pa=pa, di=di: e.tensor_scalar(out=BT[:, di, :], in0=pa, scalar1=SHIFT, scalar2=None,
                                                                    op0=ALU.add), r=[pd], w=[bt_dep])
            P.op("dve", lambda e: e.tensor_scalar(out=BT[:, 6, :], in0=BT[:, 5, :], scalar1=flags[:, 0:1], scalar2=None,
                                                  op0=ALU.add), r=[bt_dep], w=[bt_dep])
            P.op("dve", lambda e: e.tensor_scalar(out=BT[:, 7, :], in0=BT[:, 0, :], scalar1=flags[:, 0:1], scalar2=None,
                                                  op0=ALU.add), r=[bt_dep], w=[bt_dep])
            for c in range(NQC):
                pump_rest(b_quota)
                for m in range(2):
                    qm = qa[:, m, c * 512:(c + 1) * 512]
                    O = [PS_O.next() for _ in range(4)]

                    def st_mm(j):
                        sa, sd = PS_S.next()
                        P.op("pe", lambda e, sa=sa, j=j, ka=ka, m=m, qm=qm: e.matmul(sa, lhsT=ka[:, m, j * 128:(j + 1) * 128], rhs=qm,
                                                                  start=True, stop=True), r=[kd, qd], w=[sd])
                        return sa, sd
                    pend = [st_mm(0)]
                    if NKT > 1:
                        pend.append(st_mm(1))
                    for j in range(NKT):
                        sa, sd = pend.pop(0)
                        if j + 2 < NKT:
                            pend.append(st_mm(j + 2))
                        pt, ptd = PT.next()
                        dl = j - 4 * c
                        if -1 <= dl <= 4:
                            var = dl + 1
                            if j == NKT // 2 and c == NQC // 2 - 1:
                                var = 6
                            if j == NKT // 2 - 1 and c == NQC // 2:
                                var = 7
                            ta, td = TMP.next()
                            P.op("dve", lambda e, ta=ta, sa=sa, var=var: e.tensor_tensor(out=ta, in0=sa, in1=BT[:, var, :],
                                                                                        op=ALU.add),
                                 r=[sd, bt_dep], w=[td])
                            P.op("act", lambda e, pt=pt, ta=ta: e.activation(out=pt, in_=ta, func=AF.Exp),
                                 r=[td], w=[ptd])
                        else:
                            idx = (0 if dl < 0 else 1) + (2 if ((j < NKT // 2) != (c < NQC // 2)) else 0)
                            P.op("act", lambda e, pt=pt, sa=sa, idx=idx, h=h: e.activation(out=pt, in_=sa, func=AF.Exp,
                                                                                     bias=fb4[:, h, idx:idx + 1]),
                                 r=[sd], w=[ptd])
                        for qb in range(4):
                            P.op("pe", lambda e, qb=qb, pt=pt, j=j, O=O, va=va: e.matmul(O[qb][0][:, 0:257],
                                                                             lhsT=pt[:, qb * 128:(qb + 1) * 128],
                                                                             rhs=va[:, j, :], start=(j == 0),
                                                                             stop=(j == NKT - 1)),
                                 r=[ptd, vd], w=[O[qb][1]])
                    for qb in range(4):
                        oa_, od_ = O[qb]
                        rz, rzd = rz_ring.next()
                        P.op("dve", lambda e, rz=rz, oa_=oa_: e.reciprocal(out=rz, in_=oa_[:, 256:257]), r=[od_], w=[rzd])
                        if m == 0:
                            P.op("dve", lambda e, rz=rz, oa_=oa_, qb=qb: e.tensor_scalar(out=OA[:, qb, :], in0=oa_[:, 0:256],
                                                                                        scalar1=rz, scalar2=None,
                                                                                        op0=ALU.mult),
                                 r=[od_, rzd], w=[oa_dep[qb]])
                            continue
                        P.op("dve", lambda e, rz=rz: e.tensor_scalar(out=rz, in0=rz, scalar1=neglam, scalar2=None,
                                                                     op0=ALU.mult), r=[rzd], w=[rzd])
                        P.op("dve", lambda e, rz=rz, oa_=oa_, qb=qb: e.scalar_tensor_tensor(
                            out=OA[:, qb, :], in0=oa_[:, 0:256], scalar=rz, in1=OA[:, qb, :], op0=ALU.mult, op1=ALU.add),
                            r=[od_, rzd, oa_dep[qb]], w=[oa_dep[qb]])
                        ss, ssd = ss2_ring.next()
                        P.op("act", lambda e, ss=ss, qb=qb: e.activation(out=junkb[0], in_=OA[:, qb, :], func=AF.Square,
                                                                         accum_out=ss), r=[oa_dep[qb]], w=[ssd, junkb[1]])
                        rstd_from_ss(ss, 256, ssd)
                        on, ond = on_ring.next()
                        P.op("dve", lambda e, on=on, ss=ss, qb=qb: e.scalar_tensor_tensor(
                            out=on, in0=OA[:, qb, :], scalar=ss, in1=gsub, op0=ALU.mult, op1=ALU.mult),
                            r=[oa_dep[qb], ssd], w=[ond])
                        pa, pd = PS_T.next()
                        pb = pa.bitcast(BF16)
                        for i in range(2):
                            P.op("pe", lambda e, i=i, pb=pb, on=on: e.transpose(out=pb[:, i * 128:(i + 1) * 128],
                                                                                in_=on[:, i * 128:(i + 1) * 128],
                                                                                identity=ident_b), r=[ond], w=[pd])
                        osb, osd = ost_ring.next()
                        P.op("act", lambda e, osb=osb, pb=pb: e.copy(out=osb, in_=pb[:, 0:256].rearrange(
                            "p (a b) -> p a b", b=128)), r=[pd], w=[osd])
                        t0 = c * 512 + qb * 128
                        P.dma("pool", OAT[2 * h:2 * h + 2, :, t0:t0 + 128].rearrange("k p t -> p k t"), osb,
                              r=[osd], w=[odep])
        P.barrier()
    if "stopB" in dbg:
        P.finish()
        P.emit()
        es.close()
        return nc

    bump[0] = base_mark
    if True:
        G2 = 2 * HG
        NHB = max(1, GKW // 512)
        HW_ = min(512, GKW)
        PS_G = Ring(PS.aps[0:2])
        PS_PO = Ring(PS.aps[2:6])
        PS_KV = Ring(PS.aps[6:8])
        wgp = [sb("c_wgf", (32, GKW)), sb("c_wgb", (32, GKW))]
        bgp = [sb("c_bgf", (1, GKW)), sb("c_bgb", (1, GKW))]
        ggr = sb("c_ggr", (128, 512))
        S = sb("c_S", (128, G2, 512))
        Sb = sb("c_Sb", (128, G2, 512), BF16)
        s_dep = [Dep() for _ in range(G2)]
        sb_dep = [Dep() for _ in range(G2)]

        def ring2(name, shape, dt=F32, n=2):
            return Ring([sb("%s%d" % (name, i), shape, dt) for i in range(n)])
        r_glt = ring2("c_glt", (32, 128))
        r_gqt = ring2("c_gqt", (128, G2, 128), BF16)
        r_gkt = ring2("c_gkt", (128, G2, 128), BF16)
        r_gk = ring2("c_gk", (128, GKW), BF16)
        r_gv = ring2("c_gv", (128, GVW), BF16)
        r_gr = ring2("c_gr", (128, GVW), BF16)
        r_of = ring2("c_of", (128, GVW))
        r_ls = ring2("c_ls", (128, GKW), n=1)
        r_lb = ring2("c_lb", (128, GKW), BF16)
        r_ep = ring2("c_ep", (128, G2, 128))
        r_em = ring2("c_em", (128, G2, 128))
        r_qd = ring2("c_qd", (128, G2, 128), BF16)
        r_ki = ring2("c_ki", (128, G2, 128), BF16)
        r_ed = ring2("c_ed", (128, GKW), n=1)
        r_kd = ring2("c_kd", (128, GKW), BF16)
        r_at = ring2("c_at", (128, HG, 128), BF16)
        r_ofs = ring2("c_ofs", (128, GVW), n=1)
        r_os = ring2("c_os", (128, 512))
        r_gg = ring2("c_gg", (128, 512))
        r_ob = ring2("c_ob", (128, 512), BF16)
        r_obst = ring2("c_obst", (128, 4, 128), BF16)
        r_ss = ring2("c_ss", (128, 1), n=4)
        junkc = (sb("c_junk", (128, 512), BF16), Dep())
        sd0 = Dep()
        for t in wgp:
            P.op("dve", lambda e, t=t: e.memset(t, 0.0), w=[sd0])
        P.dma("sp", wgp[0][0:16, :], w_gate_f, r=[sd0], w=[sd0])
        P.dma("sp", wgp[1][16:32, :], w_gate_b, r=[sd0], w=[sd0])
        P.dma("sp", bgp[0], b_gate_f, w=[sd0])
        P.dma("sp", bgp[1], b_gate_b, w=[sd0])
        P.dma("sp", ggr, gla_norm_g.partition_broadcast(128)[:, 0, :], w=[sd0])
        P.barrier()
        ofdep = Dep()
        obdep = Dep()
        NCH = NT // 64
        for di in range(2):
            fwd = di == 0
            for g in range(G2):
                P.op("dve", lambda e, g=g: e.memset(S[:, g, :], 0.0), w=[s_dep[g]])
                P.op("dve", lambda e, g=g: e.memset(Sb[:, g, :], 0.0), w=[sb_dep[g]])
            blocks = range(NB) if fwd else range(NB - 1, -1, -1)
            c_quota = -(-rest_left() // ((2 - di) * NB))
            for blk in blocks:
                pump_rest(c_quota, engs=("pool",))
                t0 = blk * 128
                glt, gltd = r_glt.next()
                gqt, gqtd = r_gqt.next()
                gkt, gktd = r_gkt.next()
                gk, gkd = r_gk.next()
                gv, gvd = r_gv.next()
                P.dma("sp", glt, GLT[:, t0:t0 + 128], w=[gltd])
                P.dma("sp", gqt, GQT[:, :, t0:t0 + 128].rearrange("k p t -> p k t"), w=[gqtd])
                P.dma("sp", gkt, GKT[:, :, t0:t0 + 128].rearrange("k p t -> p k t"), w=[gktd])
                P.dma("sp", gk, GKM[t0:t0 + 128, :], w=[gkd])
                P.dma("sp", gv, GVM[t0:t0 + 128, :], w=[gvd])
                if not fwd:
                    of, ofd = r_of.next()
                    gr, grd = r_gr.next()
                    P.dma("sp", of, OFW[t0:t0 + 128, :], r=[ofdep], w=[ofd])
                    P.dma("sp", gr, GRM[t0:t0 + 128, :], w=[grd])
                ls, lsd = r_ls.next()
                lb, lbd = r_lb.next()
                for hb in range(NHB):
                    pa, pd = PS_G.next()
                    cs = slice(hb * HW_, (hb + 1) * HW_)
                    P.op("pe", lambda e, pa=pa, cs=cs, glt=glt, di=di: e.matmul(pa[:, 0:HW_], lhsT=glt, rhs=wgp[di][:, cs],
                                                                         start=True, stop=False), r=[gltd], w=[pd])
                    P.op("pe", lambda e, pa=pa, cs=cs, di=di: e.matmul(pa[:, 0:HW_], lhsT=ones_f[0:1, :], rhs=bgp[di][:, cs],
                                                                start=False, stop=True), w=[pd])
                    P.op("act", lambda e, pa=pa, cs=cs, ls=ls: e.activation(out=ls[:, cs], in_=pa[:, 0:HW_], func=AF.Exp,
                                                                            scale=-1.0), r=[pd], w=[lsd])
                P.op("act", lambda e, ls=ls, lb=lb: e.activation(out=lb, in_=ls, func=AF.Ln, bias=ones_f[:, 0:1]),
                     r=[lsd], w=[lbd])
                ep, epd = r_ep.next()
                em, emd = r_em.next()
                qd_, qdd = r_qd.next()
                ki, kid = r_ki.next()
                tri = 0 if fwd else 2
                for g0 in range(0, G2, 4):
                    ng = min(4, G2 - g0)
                    pa, pd = PS_G.next()
                    for i in range(ng):
                        P.op("pe", lambda e, pa=pa, i=i, g=g0 + i, lb=lb, tri=tri: e.matmul(
                            pa[:, i * 128:(i + 1) * 128], lhsT=lb[:, g * 128:(g + 1) * 128], rhs=gla_b[:, tri, :],
                            start=True, stop=True), r=[lbd], w=[pd])
                    pv = pa[:, 0:ng * 128].rearrange("p (a b) -> p a b", b=128)
                    P.op("act", lambda e, pv=pv, ep=ep, g0=g0, ng=ng: e.activation(out=ep[:, g0:g0 + ng, :], in_=pv,
                                                                                   func=AF.Exp), r=[pd], w=[epd])
                    P.op("act", lambda e, pv=pv, em=em, g0=g0, ng=ng: e.activation(out=em[:, g0:g0 + ng, :], in_=pv,
                                                                                   func=AF.Exp, scale=-1.0),
                         r=[pd], w=[emd])
                P.op("dve", lambda e, qd_=qd_, gqt=gqt, ep=ep: e.tensor_tensor(out=qd_, in0=gqt, in1=ep, op=ALU.mult),
                     r=[gqtd, epd], w=[qdd])
                P.op("dve", lambda e, ki=ki, gkt=gkt, em=em: e.tensor_tensor(out=ki, in0=gkt, in1=em, op=ALU.mult),
                     r=[gktd, emd], w=[kid])
                ed, edd = r_ed.next()
                kd_, kdd = r_kd.next()
                ut = 1 if fwd else 3
                for hb in range(NHB):
                    pa, pd = PS_G.next()
                    cs = slice(hb * HW_, (hb + 1) * HW_)
                    P.op("pe", lambda e, pa=pa, cs=cs, lb=lb, ut=ut: e.matmul(pa[:, 0:HW_], lhsT=gla_b[:, ut, :], rhs=lb[:, cs],
                                                                       start=True, stop=True), r=[lbd], w=[pd])
                    P.op("act", lambda e, pa=pa, cs=cs, ed=ed: e.activation(out=ed[:, cs], in_=pa[:, 0:HW_], func=AF.Exp),
                         r=[pd], w=[edd])
                P.op("dve", lambda e, kd_=kd_, gk=gk, ed=ed: e.tensor_tensor(out=kd_, in0=gk, in1=ed, op=ALU.mult),
                     r=[gkd, edd], w=[kdd])
                at, atd = r_at.next()
                pa, pd = PS_G.next()
                for hd in range(HG):
                    for dh in range(2):
                        P.op("pe", lambda e, pa=pa, hd=hd, dh=dh, ki=ki, qd_=qd_: e.matmul(
                            pa[:, hd * 128:(hd + 1) * 128], lhsT=ki[:, hd * 2 + dh, :], rhs=qd_[:, hd * 2 + dh, :],
                            start=(dh == 0), stop=(dh == 1)), r=[kid, qdd], w=[pd])
                mk = 4 if fwd else 5
                for hd in range(HG):
                    P.op("dve", lambda e, pa=pa, hd=hd, at=at, mk=mk: e.tensor_tensor(
                        out=at[:, hd, :], in0=pa[:, hd * 128:(hd + 1) * 128], in1=gla_f[:, mk, :], op=ALU.mult),
                        r=[pd], w=[atd])
                if fwd and blk == 0 and "d_lb" in dbg:
                    P.dma("pool", dscr("d_lb", (128, GKW)), lb, r=[lbd], w=[Dep()])
                    P.dma("pool", dscr("d_ep", (128, G2, 128), F32), ep, r=[epd], w=[Dep()])
                    P.dma("pool", dscr("d_at", (128, HG, 128)), at, r=[atd], w=[Dep()])
                    P.dma("pool", dscr("d_qd", (128, G2, 128)), qd_, r=[qdd], w=[Dep()])
                    P.dma("pool", dscr("d_kd", (128, GKW)), kd_, r=[kdd], w=[Dep()])
                po = [PS_PO.next() for _ in range(HG)]
                for ch in ((0, 1) if fwd else (1, 0)):
                    n = blk * 2 + ch
                    if (fwd and n == NCH // 2) or ((not fwd) and n == NCH // 2 - 1):
                        for g in range(G2):
                            P.op("dve", lambda e, g=g: e.tensor_scalar(out=S[:, g, :], in0=S[:, g, :], scalar1=flags[:, 1:2],
                                                                       scalar2=None, op0=ALU.mult),
                                 r=[s_dep[g]], w=[s_dep[g]])
                            P.op("act", lambda e, g=g: e.copy(out=Sb[:, g, :], in_=S[:, g, :]), r=[s_dep[g]], w=[sb_dep[g]])
                    rows = slice(ch * 64, ch * 64 + 64)
                    dcol = (ch * 64 + 63) if fwd else (ch * 64)
                    for hd in range(HG):
                        pa, pd = po[hd]
                        vs = slice(hd * 512, (hd + 1) * 512)
                        P.op("pe", lambda e, pa=pa, at=at, gv=gv, hd=hd, rows=rows, vs=vs: e.matmul(
                            pa[rows, :], lhsT=at[rows, hd, rows], rhs=gv[rows, vs], start=True, stop=False),
                            r=[atd, gvd], w=[pd])
                        for dh in range(2):
                            g = hd * 2 + dh
                            P.op("pe", lambda e, pa=pa, qd_=qd_, g=g, rows=rows, dh=dh: e.matmul(
                                pa[rows, :], lhsT=qd_[:, g, rows], rhs=Sb[:, g, :], start=False, stop=(dh == 1)),
                                r=[qdd, sb_dep[g]], w=[pd])
                        for dh in range(2):
                            g = hd * 2 + dh
                            ka_, kvd = PS_KV.next()
                            P.op("pe", lambda e, ka_=ka_, kd_=kd_, gv=gv, g=g, rows=rows, vs=vs: e.matmul(
                                ka_, lhsT=kd_[rows, g * 128:(g + 1) * 128], rhs=gv[rows, vs], start=True, stop=True),
                                r=[kdd, gvd], w=[kvd])
                            P.op("dve", lambda e, ka_=ka_, g=g, ep=ep, dcol=dcol: e.scalar_tensor_tensor(
                                out=S[:, g, :], in0=S[:, g, :], scalar=ep[:, g, dcol:dcol + 1], in1=ka_,
                                op0=ALU.mult, op1=ALU.add), r=[kvd, epd, s_dep[g], sb_dep[g]], w=[s_dep[g]])
                            P.op("act", lambda e, g=g: e.copy(out=Sb[:, g, :], in_=S[:, g, :]), r=[s_dep[g]], w=[sb_dep[g]])
                if fwd:
                    ofs, ofsd = r_ofs.next()
                    for hd in range(HG):
                        pa, pd = po[hd]
                        vs = slice(hd * 512, (hd + 1) * 512)
                        if hd % 2 == 0:
                            P.op("act", lambda e, pa=pa, ofs=ofs, vs=vs: e.copy(out=ofs[:, vs], in_=pa), r=[pd], w=[ofsd])
                        else:
                            P.op("dve", lambda e, pa=pa, ofs=ofs, vs=vs: e.tensor_copy(out=ofs[:, vs], in_=pa),
                                 r=[pd], w=[ofsd])
                    P.dma("pool", OFW[t0:t0 + 128, :], ofs, r=[ofsd], w=[ofdep])
                else:
                    for hd in range(HG):
                        pa, pd = po[hd]
                        vs = slice(hd * 512, (hd + 1) * 512)
                        osm, osd = r_os.next()
                        P.op("dve", lambda e, pa=pa, osm=osm, of=of, vs=vs: e.tensor_tensor(out=osm, in0=pa, in1=of[:, vs],
                                                                                          op=ALU.add),
                             r=[pd, ofd], w=[osd])
                        ss, ssd = r_ss.next()
                        P.op("act", lambda e, osm=osm, ss=ss: e.activation(out=junkc[0], in_=osm, func=AF.Square,
                                                                           accum_out=ss), r=[osd], w=[ssd, junkc[1]])
                        rstd_from_ss(ss, 512, ssd)
                        gg, ggd = r_gg.next()
                        P.op("dve", lambda e, gg=gg, gr=gr, vs=vs: e.tensor_tensor(out=gg, in0=gr[:, vs], in1=ggr, op=ALU.mult),
                             r=[grd], w=[ggd])
                        ob, obd = r_ob.next()
                        P.op("dve", lambda e, ob=ob, osm=osm, ss=ss, gg=gg: e.scalar_tensor_tensor(
                            out=ob, in0=osm, scalar=ss, in1=gg, op0=ALU.mult, op1=ALU.mult), r=[osd, ssd, ggd], w=[obd])
                        pt_, ptd_ = PS_G.next()
                        pb = pt_.bitcast(BF16)
                        for i in range(4):
                            P.op("pe", lambda e, pb=pb, ob=ob, i=i: e.transpose(out=pb[:, i * 128:(i + 1) * 128],
                                                                                in_=ob[:, i * 128:(i + 1) * 128],
                                                                                identity=ident_b), r=[obd], w=[ptd_])
                        obst, obsd = r_obst.next()
                        P.op("act", lambda e, pb=pb, obst=obst: e.copy(out=obst, in_=pb[:, 0:512].rearrange(
                            "p (a b) -> p a b", b=128)), r=[ptd_], w=[obsd])
                        P.dma("pool", OBT[hd * 4:(hd + 1) * 4, :, t0:t0 + 128].rearrange("k p t -> p k t"), obst,
                              r=[obsd], w=[obdep])
            P.barrier()
        pump_rest(rest_left())
        P.barrier()
    if "stopC" in dbg:
        P.finish()
        P.emit()
        es.close()
        return nc

    bump[0] = base_mark0
    PSD = Ring(PS.aps)
    if True:
        CWD = min(512, D)
        NJ = CWD // 128
        oat = sb("d_oat", (128, KCB, TT), BF16)
        obt = sb("d_obt", (128, KCG, TT), BF16)
        mT = sb("d_mT", (128, KC, TT), BF16)
        oat_d, obt_d, mT_d = Dep(), Dep(), Dep()
        r_wbr = Ring([sb("d_wbr%d" % i, (128, max(KCB, KCG), CWD), BF16) for i in range(3)])
        r_wo = Ring([sb("d_wo%d" % i, (128, KC, CWD), BF16) for i in range(2)])
        r_sg = Ring([sb("d_sg%d" % i, (128, TT), BF16) for i in range(4)])
        r_t = Ring([sb("d_t%d" % i, (128, TT)) for i in range(4)])
        r_xp = Ring([sb("d_xp%d" % i, (128, CWD)) for i in range(3)])
        x1dep = Dep()
        for tt in range(NTT):
            t0 = tt * TT
            P.dma("sp", oat, OAT[:, :, t0:t0 + TT].rearrange("k p t -> p k t"), w=[oat_d])
            P.dma("sp", obt, OBT[:, :, t0:t0 + TT].rearrange("k p t -> p k t"), w=[obt_d])
            for cg in range(D // CWD):
                wa_, wad = r_wbr.next()
                wb_, wbd = r_wbr.next()
                P.dma("sp", wa_[:, 0:KCB, :], WB["bra"][cg], r=[WBD[("bra", cg)]], w=[wad])
                P.dma("sp", wb_[:, 0:KCG, :], WB["brb"][cg], r=[WBD[("brb", cg)]], w=[wbd])
                for j in range(NJ):
                    ct = cg * NJ + j
                    sga_, sgad = r_sg.next()
                    sgb_, sgbd = r_sg.next()
                    P.dma("sp", sga_, SGA[ct, :, t0:t0 + TT], w=[sgad])
                    P.dma("sp", sgb_, SGB[ct, :, t0:t0 + TT], w=[sgbd])
                    pa, pd = PSD.next()
                    mm_group(pa, pd, [wa_[:, kc, j * 128:(j + 1) * 128] for kc in range(KCB)],
                             [oat[:, kc, :] for kc in range(KCB)], [wad, oat_d])
                    pb_, pbd = PSD.next()
                    mm_group(pb_, pbd, [wb_[:, kc, j * 128:(j + 1) * 128] for kc in range(KCG)],
                             [obt[:, kc, :] for kc in range(KCG)], [wbd, obt_d])
                    t1, t1d = r_t.next()
                    t2, t2d = r_t.next()
                    P.op("dve", lambda e, t1=t1, pa=pa, sga_=sga_: e.tensor_tensor(out=t1, in0=pa, in1=sga_, op=ALU.mult),
                         r=[pd, sgad], w=[t1d])
                    P.op("dve", lambda e, t2=t2, pb_=pb_, sgb_=sgb_: e.tensor_tensor(out=t2, in0=pb_, in1=sgb_, op=ALU.mult),
                         r=[pbd, sgbd], w=[t2d])
                    P.op("pool", lambda e, t1=t1, t2=t2, ct=ct: e.tensor_tensor(out=mT[:, ct, :], in0=t1, in1=t2, op=ALU.add),
                         r=[t1d, t2d], w=[mT_d])
            for cg in range(D // CWD):
                wo_, wod = r_wo.next()
                P.dma("sp", wo_, WB["out"][cg], r=[WBD[("out", cg)]], w=[wod])
                for b in range(TT // 128):
                    r0 = t0 + b * 128
                    xp, xpd = r_xp.next()
                    P.dma("sp", xp, x_in[r0:r0 + 128, cg * CWD:(cg + 1) * CWD], w=[xpd])
                    pa, pd = PSD.next()
                    mm_group(pa[:, 0:CWD], pd, [mT[:, kc, b * 128:(b + 1) * 128] for kc in range(KC)],
                             [wo_[:, kc, :] for kc in range(KC)], [wod, mT_d])
                    P.op("dve", lambda e, xp=xp, pa=pa: e.tensor_tensor(out=xp, in0=pa[:, 0:CWD], in1=xp, op=ALU.add),
                         r=[pd, xpd], w=[xpd])
                    P.dma("pool", X1[r0:r0 + 128, cg * CWD:(cg + 1) * CWD], xp, r=[xpd], w=[x1dep])
        P.barrier()
    if "stopD" in dbg:
        P.finish()
        P.emit()
        es.close()
        return nc

    bump[0] = base_mark0
    if True:
        NH = NTT - 1
        OG = min(4, D // 128)
        act = sb("e_act", (128, FT, TT), BF16)
        h2T = sb("e_h2T", (128, KC, TT), BF16)
        h2Th = sb("e_h2Th", (128, KC, 16), BF16)
        gT = sb("e_gT", (128, KC))
        cw = sb("e_cw", (128, 4, FT))
        AH = sb("e_ah", (128, FT, 16))
        act_d, h2T_d, h2Th_d, ah_d = Dep(), Dep(), Dep(), Dep()
        r_mark = bump[0]
        crow = sb("e_crow", (FT, 4, 128))
        grow = sb("e_grow", (KC, 128))
        sd0 = Dep()
        for k in range(3):
            P.dma("sp", crow[:, k, :], conv_w[k:k + 1, :].rearrange("o (c p) -> (o c) p", p=128), w=[sd0])
        P.dma("sp", crow[:, 3, :], conv_b.rearrange("o (c p) -> (o c) p", p=128), w=[sd0])
        P.dma("sp", grow, g_ffn.rearrange("o (c p) -> (o c) p", p=128), w=[sd0])
        for k in range(4):
            pa, pd = PSD.next()
            P.op("pe", lambda e, pa=pa, k=k: e.transpose(out=pa[:, 0:FT], in_=crow[:, k, :], identity=ident_f[0:FT, 0:FT]),
                 r=[sd0], w=[pd])
            P.op("dve", lambda e, pa=pa, k=k: e.tensor_copy(out=cw[:, k, :], in_=pa[:, 0:FT]), r=[pd], w=[sd0])
        pa, pd = PSD.next()
        P.op("pe", lambda e, pa=pa: e.transpose(out=pa[:, 0:KC], in_=grow, identity=ident_f[0:KC, 0:KC]), r=[sd0], w=[pd])
        P.op("dve", lambda e, pa=pa: e.tensor_copy(out=gT, in_=pa[:, 0:KC]), r=[pd], w=[sd0])
        P.op("dve", lambda e: e.memset(AH, 0.0), w=[ah_d])
        P.barrier()
        bump[0] = r_mark
        xt1 = Ring([sb("e_xt", (128, D))])
        xn1 = Ring([sb("e_xn", (128, D), BF16)])
        ss1 = Ring([sb("e_ss%d" % i, (128, 1)) for i in range(2)])
        if NH > 0:
            X1v = X1.rearrange("(j t) d -> j t d", t=TT)

            def halo_loader(xa, xd):
                P.op("dve", lambda e: e.memset(xa[0:16, :], 0.0), w=[xd])
                P.dma("sp", xa[0:NH, :], X1v[0:NH, TT - 1, :], r=[x1dep], w=[xd])
                P.dma("sp", xa[8:8 + NH, :], X1v[1:NH + 1, 0, :], r=[x1dep], w=[xd])
            norm_transpose_block(halo_loader, None, gT, xt1, xn1, ss1, h2Th, h2Th_d, 0, width=16)
            P.barrier()
        ydep = Dep()
        for tt in range(NTT):
            t0 = tt * TT
            bump[0] = r_mark
            r_wu = Ring([sb("e_wu%d" % i, (128, KC, 128), BF16) for i in range(4)])
            r_c = Ring([sb("e_c%d" % i, (128, TT)) for i in range(2)])
            r_u = Ring([sb("e_u%d" % i, (128, TT)) for i in range(2)])
            xt1 = Ring([sb("e_xt", (128, D))])
            xn1 = Ring([sb("e_xn", (128, D), BF16)])
            ss1 = Ring([sb("e_ss%d" % i, (128, 1)) for i in range(2)])
            for b in range(TT // 128):
                norm_transpose_block(lambda xa, xd, r0=t0 + b * 128: P.dma("sp", xa, X1[r0:r0 + 128, :], r=[x1dep], w=[xd]),
                                     None, gT, xt1, xn1, ss1, h2T, h2T_d, b * 128)
            for ct in range(FT):
                wa_, wad = r_wu.next()
                wg_, wgd = r_wu.next()
                P.dma("sp", wa_, WB["upa"][ct], r=[WBD[("upa", ct)]], w=[wad])
                P.dma("sp", wg_, WB["upg"][ct], r=[WBD[("upg", ct)]], w=[wgd])
                pa, pd = PSD.next()
                mm_group(pa, pd, [wa_[:, kc, :] for kc in range(KC)], [h2T[:, kc, :] for kc in range(KC)], [wad, h2T_d])
                pg_, pgd = PSD.next()
                mm_group(pg_, pgd, [wg_[:, kc, :] for kc in range(KC)], [h2T[:, kc, :] for kc in range(KC)], [wgd, h2T_d])
                if tt == 0 and NH > 0:
                    ph_, phd = PSD.next()
                    mm_group(ph_[:, 0:16], phd, [wa_[:, kc, :] for kc in range(KC)], [h2Th[:, kc, :] for kc in range(KC)],
                             [wad, h2Th_d])
                    P.op("dve", lambda e, ph_=ph_, ct=ct: e.tensor_tensor(out=AH[:, ct, :], in0=ph_[:, 0:16],
                                                                         in1=flags[:, 16:32], op=ALU.mult),
                         r=[phd], w=[ah_d])
                c_, cd = r_c.next()
                u_, ud = r_u.next()
                P.op("act", lambda e, c_=c_, pa=pa, ct=ct: e.activation(out=c_, in_=pa, func=AF.Identity,
                                                                        bias=cw[:, 3, ct:ct + 1], scale=cw[:, 1, ct:ct + 1]),
                     r=[pd], w=[cd])
                P.op("dve", lambda e, c_=c_, pa=pa, ct=ct: e.scalar_tensor_tensor(
                    out=c_[:, 1:TT], in0=pa[:, 0:TT - 1], scalar=cw[:, 0, ct:ct + 1], in1=c_[:, 1:TT],
                    op0=ALU.mult, op1=ALU.add), r=[pd, cd], w=[cd])
                P.op("dve", lambda e, c_=c_, pa=pa, ct=ct: e.scalar_tensor_tensor(
                    out=c_[:, 0:TT - 1], in0=pa[:, 1:TT], scalar=cw[:, 2, ct:ct + 1], in1=c_[:, 0:TT - 1],
                    op0=ALU.mult, op1=ALU.add), r=[pd, cd], w=[cd])
                if tt >= 1:
                    P.op("dve", lambda e, c_=c_, ct=ct, i=tt - 1: e.scalar_tensor_tensor(
                        out=c_[:, 0:1], in0=AH[:, ct, i:i + 1], scalar=cw[:, 0, ct:ct + 1], in1=c_[:, 0:1],
                        op0=ALU.mult, op1=ALU.add), r=[ah_d, cd], w=[cd])
                if tt <= NTT - 2:
                    P.op("dve", lambda e, c_=c_, ct=ct, i=8 + tt: e.scalar_tensor_tensor(
                        out=c_[:, TT - 1:TT], in0=AH[:, ct, i:i + 1], scalar=cw[:, 2, ct:ct + 1], in1=c_[:, TT - 1:TT],
                        op0=ALU.mult, op1=ALU.add), r=[ah_d, cd], w=[cd])
                P.op("dve", lambda e, c_=c_, u_=u_: e.tensor_tensor(out=u_, in0=c_, in1=c_, op=ALU.mult), r=[cd], w=[ud])
                P.op("dve", lambda e, u_=u_: e.tensor_scalar(out=u_, in0=u_, scalar1=0.044715, scalar2=1.0,
                                                             op0=ALU.mult, op1=ALU.add), r=[ud], w=[ud])
                P.op("dve", lambda e, c_=c_, u_=u_: e.tensor_tensor(out=u_, in0=u_, in1=c_, op=ALU.mult), r=[cd, ud], w=[ud])
                P.op("act", lambda e, u_=u_: e.activation(out=u_, in_=u_, func=AF.Sigmoid, scale=1.5957691216057308),
                     r=[ud], w=[ud])
                P.op("dve", lambda e, c_=c_, u_=u_: e.tensor_tensor(out=u_, in0=u_, in1=c_, op=ALU.mult), r=[cd, ud], w=[ud])
                P.op("dve", lambda e, u_=u_, pg_=pg_, ct=ct: e.tensor_tensor(out=act[:, ct, :], in0=u_, in1=pg_, op=ALU.mult),
                     r=[ud, pgd], w=[act_d])
            P.barrier()
            bump[0] = r_mark
            r_wd = Ring([sb("e_wd%d" % i, (128, FT, 128), BF16) for i in range(2)])
            r_yt = Ring([sb("e_yt%d" % i, (128, TT)) for i in range(2)])
            r_yio = Ring([sb("e_yio%d" % i, (128, TT // 128, OG * 128)) for i in range(2)])
            for og in range(D // (OG * 128)):
                yio, yiod = r_yio.next()
                cs = slice(og * OG * 128, (og + 1) * OG * 128)
                P.dma("sp", yio, X1[t0:t0 + TT, cs].rearrange("(b p) c -> p b c", p=128), r=[x1dep], w=[yiod])
                for oi in range(OG):
                    ot = og * OG + oi
                    wd_, wdd = r_wd.next()
                    P.dma("sp", wd_, WB["down"][ot], r=[WBD[("down", ot)]], w=[wdd])
                    py, pyd = PSD.next()
                    mm_group(py, pyd, [wd_[:, kc, :] for kc in range(FT)], [act[:, kc, :] for kc in range(FT)], [wdd, act_d])
                    yt, ytd = r_yt.next()
                    P.op("act", lambda e, yt=yt, py=py: e.copy(out=yt, in_=py), r=[pyd], w=[ytd])
                    pT, pTd = PSD.next()
                    for b in range(TT // 128):
                        P.op("pe", lambda e, pT=pT, yt=yt, b=b: e.transpose(out=pT[:, b * 128:(b + 1) * 128],
                                                                            in_=yt[:, b * 128:(b + 1) * 128],
                                                                            identity=ident_f), r=[ytd], w=[pTd])
                    P.op("dve", lambda e, pT=pT, yio=yio, oi=oi: e.tensor_tensor(
                        out=yio[:, :, oi * 128:(oi + 1) * 128], in0=pT.rearrange("p (b c) -> p b c", c=128),
                        in1=yio[:, :, oi * 128:(oi + 1) * 128], op=ALU.add), r=[pTd, yiod], w=[yiod])
                P.dma("pool", y_out[t0:t0 + TT, cs].rearrange("(b p) c -> p b c", p=128), yio, r=[yiod], w=[ydep])
            P.barrier()
    P.finish()
    P.emit()
    es.close()
    return nc


PHASES = {}


def core_flags_np(cfg, is_sample):
    fl = np.zeros((128, 32), np.float32)
    fl[:, 0] = NEG if is_sample else 0.0
    fl[:, 1] = 0.0 if is_sample else 1.0
    fl[:, 16:32] = 1.0
    ntt = cfg["NT"] // 512
    if is_sample:
        fl[:, 16 + ntt // 2 - 1] = 0.0
        fl[:, 24 + ntt // 2 - 1] = 0.0
    return fl


_NC_CACHE = {}


def kernel(x_prompt, x_sample, rel_bias, g_mix, w_in, q_norm_g, k_norm_g, lambda_q1, lambda_k1, lambda_q2, lambda_k2,
           da_subln_g, w_gate_fwd, b_gate_fwd, w_gate_bwd, b_gate_bwd, gla_norm_g, w_branch_a, w_branch_b, w_out,
           g_ffn, w_up, conv_w, conv_b, w_down):
    cfg = full_cfg()
    f = lambda a: np.ascontiguousarray(np.asarray(a, dtype=np.float32))
    shared = dict(
        rel_bias=f(rel_bias), g_mix=f(g_mix[0:1]), w_in=f(w_in[0]), q_norm_g=f(q_norm_g[0:1]), k_norm_g=f(k_norm_g[0:1]),
        lam4=f(np.concatenate([lambda_q1[0], lambda_k1[0], lambda_q2[0], lambda_k2[0]])[None, :]),
        da_subln_g=f(da_subln_g[0:1]), w_gate_fwd=f(w_gate_fwd[0]), b_gate_fwd=f(b_gate_fwd[0:1]),
        w_gate_bwd=f(w_gate_bwd[0]), b_gate_bwd=f(b_gate_bwd[0:1]), gla_norm_g=f(gla_norm_g[0:1]),
        w_branch_a=f(w_branch_a[0]), w_branch_b=f(w_branch_b[0]), w_out=f(w_out[0]), g_ffn=f(g_ffn[0:1]),
        w_up=f(w_up[0]), conv_w=f(conv_w[0]), conv_b=f(conv_b[0:1]), w_down=f(w_down[0]))
    shared.update(host_consts())
    xp = np.asarray(x_prompt, dtype=np.float32)
    xs = np.asarray(x_sample, dtype=np.float32)
    NT, D = cfg["NT"], cfg["D"]
    in_maps = []
    for c in range(8):
        m = dict(shared)
        if c < 4:
            m["x"] = np.ascontiguousarray(xp[c])
        else:
            m["x"] = np.ascontiguousarray(xs[2 * (c - 4):2 * (c - 4) + 2].reshape(NT, D))
        m["core_flags"] = core_flags_np(cfg, c >= 4)
        in_maps.append(m)
    if "nc" not in _NC_CACHE:
        _NC_CACHE["nc"] = build(cfg)
    res = run_bass_kernel_spmd(_NC_CACHE["nc"], in_maps, core_ids=list(range(8)))
    yp = np.stack([np.asarray(res.results[c]["y"], dtype=np.float32) for c in range(4)])
    ys = np.concatenate([np.asarray(res.results[c]["y"], dtype=np.float32).reshape(2, NT // 2, D) for c in range(4, 8)])
    return (yp, ys)
```
